# Optimizing a Trainium2 kernel written in Bass

```python
import math
import jax, jax.numpy as jnp
from jax import lax
import numpy as np

D_MODEL = 1024
BATCH = 2
SEQ = 8192
DEPTH = 1

ATTN_HEADS = 8
ATTN_HEAD_DIM = 64
ATTN_W = ATTN_HEADS * ATTN_HEAD_DIM
IDX_HEADS = 4
IDX_HEAD_DIM = 64
IDX_Q_W = IDX_HEADS * IDX_HEAD_DIM
TOPK_MAX = 256
Q_BLOCK = 128
RET_HEADS = 8
RET_QK_DIM = 64
RET_V_DIM = 128
RET_QK_W = RET_HEADS * RET_QK_DIM
RET_V_W = RET_HEADS * RET_V_DIM
RET_CHUNK = 128
ROPE_BASE = 10000.0
D_FF = 4 * D_MODEL
NUM_BUCKETS = 32
MAX_DISTANCE = 128
LN_EPS = 1e-5
DEEPNORM_ALPHA = (2.0 * DEPTH) ** 0.25
DEEPNORM_BETA = (8.0 * DEPTH) ** -0.25
IN_SIZES = (ATTN_W, ATTN_W, ATTN_W,
            IDX_Q_W, IDX_HEAD_DIM, IDX_HEADS,
            RET_QK_W, RET_QK_W, RET_V_W, RET_V_W,
            D_MODEL, D_MODEL)
IN_WIDTH = 3 * ATTN_W + IDX_Q_W + IDX_HEAD_DIM + IDX_HEADS + 2 * RET_QK_W + 2 * RET_V_W + 2 * D_MODEL

kernel_name = 'hybrid_dsa_retention_block'


def layer_norm(x, g, b):
    xf = x.astype(jnp.float32)
    mu = jnp.mean(xf, axis=-1, keepdims=True)
    var = jnp.mean(jnp.square(xf - mu), axis=-1, keepdims=True)
    return ((xf - mu) * lax.rsqrt(var + LN_EPS) * g.astype(jnp.float32) + b.astype(jnp.float32)).astype(x.dtype)


def head_norm(x):
    xf = x.astype(jnp.float32)
    mu = jnp.mean(xf, axis=-1, keepdims=True)
    var = jnp.mean(jnp.square(xf - mu), axis=-1, keepdims=True)
    return (xf - mu) * lax.rsqrt(var + LN_EPS)


def t5_bucket(rel):
    n = jnp.maximum(rel, 0)
    max_exact = NUM_BUCKETS // 2
    nf = jnp.maximum(n, 1).astype(jnp.float32)
    large = max_exact + (jnp.log(nf / max_exact) / math.log(MAX_DISTANCE / max_exact)
                         * (NUM_BUCKETS - max_exact)).astype(jnp.int32)
    large = jnp.minimum(large, NUM_BUCKETS - 1)
    return jnp.where(n < max_exact, n, large)


def rope(x, pos):
    half = x.shape[-1] // 2
    inv = ROPE_BASE ** (-jnp.arange(half, dtype=jnp.float32) / half)
    ang = pos.astype(jnp.float32)[:, :, None, None] * inv
    cos, sin = jnp.cos(ang), jnp.sin(ang)
    x1 = x[..., :half].astype(jnp.float32)
    x2 = x[..., half:].astype(jnp.float32)
    return jnp.concatenate([x1 * cos - x2 * sin, x1 * sin + x2 * cos], axis=-1).astype(x.dtype)


def sparse_attention(q, k, v, iq, ik, iw, pos, rel_bias):
    B, S, H, dh = q.shape
    k_sel = min(TOPK_MAX, S // 4)
    n_blk = S // Q_BLOCK
    key_idx = jnp.arange(S)

    def block(start):
        qb = lax.dynamic_slice_in_dim(q, start, Q_BLOCK, axis=1)
        iqb = lax.dynamic_slice_in_dim(iq, start, Q_BLOCK, axis=1)
        iwb = lax.dynamic_slice_in_dim(iw, start, Q_BLOCK, axis=1)
        posb = lax.dynamic_slice_in_dim(pos, start, Q_BLOCK, axis=1)
        t = start + jnp.arange(Q_BLOCK)
        rel_scores = jax.nn.relu(jnp.einsum('bqhd,bsd->bqhs', iqb, ik).astype(jnp.float32)) * (IDX_HEAD_DIM ** -0.5)
        score = jnp.einsum('bqhs,bqh->bqs', rel_scores, iwb.astype(jnp.float32) * (IDX_HEADS ** -0.5))
        causal = key_idx[None, :] <= t[:, None]
        score = jnp.where(causal[None], score, -jnp.inf)
        _, sel = lax.top_k(score, k_sel)
        valid = sel <= t[None, :, None]
        k_g = jax.vmap(lambda kk, ii: kk[ii])(k, sel)
        v_g = jax.vmap(lambda vv, ii: vv[ii])(v, sel)
        pos_g = jax.vmap(lambda pp, ii: pp[ii])(pos, sel)
        bias = rel_bias[t5_bucket(posb[:, :, None] - pos_g)]
        logits = (jnp.einsum('bqhd,bqkhd->bqhk', qb, k_g).astype(jnp.float32) * (dh ** -0.5)
                  + bias.astype(jnp.float32).transpose(0, 1, 3, 2))
        logits = jnp.where(valid[:, :, None, :], logits, -jnp.inf)
        p = jax.nn.softmax(logits, axis=-1).astype(v.dtype)
        return jnp.einsum('bqhk,bqkhd->bqhd', p, v_g)

    out = lax.map(block, jnp.arange(n_blk) * Q_BLOCK)
    return out.transpose(1, 0, 2, 3, 4).reshape(B, S, H * dh)


def retention(q, k, v):
    B, S, H, dk = q.shape
    dv = v.shape[-1]
    C = RET_CHUNK
    N = S // C
    gamma = 1.0 - 2.0 ** (-5.0 - jnp.arange(H, dtype=jnp.float32))
    log_g = jnp.log(gamma)
    n = jnp.arange(C, dtype=jnp.float32)
    diff = n[:, None] - n[None, :]
    decay_in = jnp.where(diff[None] >= 0, jnp.exp(log_g[:, None, None] * jnp.maximum(diff, 0.0)[None]), 0.0)
    xi = jnp.exp(log_g[None, :] * (n[:, None] + 1.0))
    zeta = jnp.exp(log_g[None, :] * (C - 1.0 - n[:, None]))
    g_chunk = jnp.exp(log_g * C)

    def to_chunks(a):
        return a.astype(jnp.float32).reshape(B, N, C, H, a.shape[-1]).transpose(1, 0, 2, 3, 4)

    def step(R, inp):
        qi, ki, vi = inp
        inner = jnp.einsum('bnhd,bmhd->bhnm', qi, ki) * decay_in[None]
        o = (jnp.einsum('bhnm,bmhv->bnhv', inner, vi)
             + jnp.einsum('bnhd,bhdv->bnhv', qi, R) * xi[None, :, :, None])
        R = R * g_chunk[None, :, None, None] + jnp.einsum('bmhd,bmhv->bhdv', ki * zeta[None, :, :, None], vi)
        return R, o

    R0 = jnp.zeros((B, H, dk, dv), jnp.float32)
    _, o = lax.scan(step, R0, (to_chunks(q), to_chunks(k), to_chunks(v)))
    return o.transpose(1, 0, 2, 3, 4).reshape(B, S, H, dv)


def setup_inputs(seed: int = 0) -> dict:
    key = jax.random.key(seed)
    ks = jax.random.split(key, 16)

    def nrm(k, shape, scale):
        return jax.random.normal(k, shape, jnp.float32) * scale

    return {
        'x': nrm(ks[0], (BATCH, SEQ, D_MODEL), 1.0),
        'positions': jnp.broadcast_to(jnp.arange(SEQ, dtype=jnp.int32), (BATCH, SEQ)),
        'w_in': nrm(ks[1], (DEPTH, D_MODEL, IN_WIDTH), D_MODEL ** -0.5),
        'rel_bias': nrm(ks[2], (NUM_BUCKETS, ATTN_HEADS), 0.5),
        'idx_k_ln_g': 1.0 + nrm(ks[3], (DEPTH, IDX_HEAD_DIM), 0.05),
        'idx_k_ln_b': nrm(ks[4], (DEPTH, IDX_HEAD_DIM), 0.02),
        'w_attn_branch': nrm(ks[5], (DEPTH, ATTN_W, D_MODEL), ATTN_W ** -0.5),
        'w_ret_branch': nrm(ks[6], (DEPTH, RET_V_W, D_MODEL), RET_V_W ** -0.5),
        'w_out': nrm(ks[7], (DEPTH, D_MODEL, D_MODEL), DEEPNORM_BETA * D_MODEL ** -0.5),
        'ln_mix_g': 1.0 + nrm(ks[8], (DEPTH, D_MODEL), 0.05),
        'ln_mix_b': nrm(ks[9], (DEPTH, D_MODEL), 0.02),
        'w_up': nrm(ks[10], (DEPTH, D_MODEL, D_FF), D_MODEL ** -0.5),
        'w_down': nrm(ks[11], (DEPTH, D_FF, D_MODEL), DEEPNORM_BETA * D_FF ** -0.5),
        'ln_ffn_g': 1.0 + nrm(ks[12], (DEPTH, D_MODEL), 0.05),
        'ln_ffn_b': nrm(ks[13], (DEPTH, D_MODEL), 0.02),
    }


def reference(x, positions, w_in, rel_bias, idx_k_ln_g, idx_k_ln_b, w_attn_branch, w_ret_branch,
              w_out, ln_mix_g, ln_mix_b, w_up, w_down, ln_ffn_g, ln_ffn_b):
    B, S, _ = x.shape
    offsets = [int(o) for o in np.cumsum(IN_SIZES)[:-1]]
    for l in range(DEPTH):
        proj = jnp.einsum('bsd,de->bse', x, w_in[l])
        (q_a, k_a, v_a, iq, ik, iw, q_r, k_r, v_r, g_r, gate_a, gate_r) = jnp.split(proj, offsets, axis=-1)
        q_a = q_a.reshape(B, S, ATTN_HEADS, ATTN_HEAD_DIM)
        k_a = k_a.reshape(B, S, ATTN_HEADS, ATTN_HEAD_DIM)
        v_a = v_a.reshape(B, S, ATTN_HEADS, ATTN_HEAD_DIM)
        iq = iq.reshape(B, S, IDX_HEADS, IDX_HEAD_DIM)
        ik = layer_norm(ik, idx_k_ln_g[l], idx_k_ln_b[l])
        y_a = sparse_attention(q_a, k_a, v_a, iq, ik, iw, positions, rel_bias)

        q_r = rope(q_r.reshape(B, S, RET_HEADS, RET_QK_DIM), positions)
        k_r = rope(k_r.reshape(B, S, RET_HEADS, RET_QK_DIM), positions) * (RET_QK_DIM ** -0.5)
        v_r = v_r.reshape(B, S, RET_HEADS, RET_V_DIM)
        ret = head_norm(retention(q_r, k_r, v_r)).reshape(B, S, RET_V_W)
        y_r = (jax.nn.silu(g_r.astype(jnp.float32)) * ret).astype(x.dtype)

        h = (jax.nn.sigmoid(gate_a) * jnp.einsum('bse,ed->bsd', y_a, w_attn_branch[l])
             + jax.nn.sigmoid(gate_r) * jnp.einsum('bse,ed->bsd', y_r, w_ret_branch[l]))
        mix = jnp.einsum('bsd,de->bse', h, w_out[l])
        x = layer_norm(DEEPNORM_ALPHA * x + mix, ln_mix_g[l], ln_mix_b[l])

        hid = jax.nn.relu(jnp.einsum('bsd,df->bsf', x, w_up[l]))
        ffn = jnp.einsum('bsf,fd->bsd', hid * hid, w_down[l])
        x = layer_norm(DEEPNORM_ALPHA * x + ffn, ln_ffn_g[l], ln_ffn_b[l])
    return x
```

```python
import contextlib
import math
import numpy as np
import concourse.bass as bass
import concourse.mybir as mybir
from concourse.bass_utils import run_bass_kernel_spmd

F32 = mybir.dt.float32
BF16 = mybir.dt.bfloat16
I32 = mybir.dt.int32
ALU = mybir.AluOpType
AF = mybir.ActivationFunctionType
AX = mybir.AxisListType

NBIS = 22
LO_INIT = -60.0
MASKV = -30000.0
ALPHA = 2.0 ** 0.25
EPS = 1e-5
PI = math.pi


class Sched:
    CE = ("pe", "act", "dve", "pool")
    ALLE = ("pe", "act", "dve", "pool", "sp")

    def __init__(self, nc, stack, n_dma=(("sp", 24), ("act", 8))):
        self.nc = nc
        self.ops = {e: [] for e in self.ALLE}
        self.cnt = {e: 0 for e in self.CE}
        self.sem = {}
        for e in self.CE:
            self.sem["c_" + e] = stack.enter_context(nc.semaphore("c_" + e))
        self.dma_slots = {}
        self.dma_next = {}
        self.dma_val = {}
        for e, n in n_dma:
            self.dma_slots[e] = []
            for i in range(n):
                k = "d_%s_%d" % (e, i)
                self.sem[k] = stack.enter_context(nc.semaphore(k))
                self.dma_slots[e].append(k)
                self.dma_val[k] = 0
            self.dma_next[e] = 0
        self.buf = {}
        self.waited = {e: {} for e in self.ALLE}
        self.pending = {e: {} for e in self.ALLE}
        self.nops = 0

    def _st(self, k):
        s = self.buf.get(k)
        if s is None:
            s = {"w": None, "r": {}}
            self.buf[k] = s
        return s

    def op(self, eng, fn, r=(), w=(), dma=False):
        waits = dict(self.pending[eng])
        self.pending[eng] = {}

        def add(ev):
            if ev is None:
                return
            k, v = ev
            if waits.get(k, 0) < v:
                waits[k] = v

        for k in r:
            add(self._st(k)["w"])
        for k in w:
            s = self._st(k)
            add(s["w"])
            for ev in s["r"].items():
                add(ev)
        if dma:
            slots = self.dma_slots[eng]
            k = slots[self.dma_next[eng] % len(slots)]
            self.dma_next[eng] += 1
            if self.dma_val[k] > 0:
                add((k, self.dma_val[k]))
            self.dma_val[k] += 16
            ev = (k, self.dma_val[k])
            inc = 16
        else:
            self.cnt[eng] += 1
            ev = ("c_" + eng, self.cnt[eng])
            inc = 1
        wl = []
        own = "c_" + eng
        for k, v in waits.items():
            if k == own and eng == "pe":
                continue
            if self.waited[eng].get(k, 0) >= v:
                continue
            self.waited[eng][k] = v
            wl.append((k, v))
        self.ops[eng].append((wl, fn, ev[0], inc))
        self.nops += 1
        for k in w:
            s = self._st(k)
            s["w"] = ev
            s["r"] = {}
        for k in r:
            if k in w:
                continue
            s = self._st(k)
            if s["r"].get(ev[0], 0) < ev[1]:
                s["r"][ev[0]] = ev[1]
        return ev

    def all_events(self):
        evs = {}
        for e in self.CE:
            if self.cnt[e] > 0:
                evs["c_" + e] = self.cnt[e]
        for k, v in self.dma_val.items():
            if v > 0:
                evs[k] = v
        return evs

    def barrier(self):
        evs = self.all_events()
        for e in self.ALLE:
            for k, v in evs.items():
                if self.pending[e].get(k, 0) < v:
                    self.pending[e][k] = v
        self.buf = {}

    def emit(self):
        nc = self.nc
        final = self.all_events()
        with nc.Block() as block:
            def run(eng_name, e, is_last=False):
                for wl, fn, sk, inc in self.ops[eng_name]:
                    for k, v in wl:
                        e.wait_ge(self.sem[k], v)
                    ins = fn(e)
                    ins.then_inc(self.sem[sk], inc)
                if is_last:
                    for k, v in final.items():
                        e.wait_ge(self.sem[k], v)

            @block.tensor
            def _(e):
                run("pe", e)

            @block.scalar
            def _(e):
                run("act", e)

            @block.vector
            def _(e):
                run("dve", e)

            @block.gpsimd
            def _(e):
                run("pool", e)

            @block.sync
            def _(e):
                run("sp", e, True)


def build(phases="ABCD", debug=False, ng=16, own=True, alltok=True, ostage=99, nb=16, bstage=99, mstart=0, nkcap=999, nobias=False, dummy=0):
    nc = bass.Bass("TRN2", target_bir_lowering=False)

    def din(name, shape, dt=F32):
        return nc.dram_tensor(name, shape, dt, kind="ExternalInput").ap()

    def dscr(name, shape, dt):
        kind = "ExternalOutput" if debug else "Internal"
        return nc.dram_tensor(name, shape, dt, kind=kind).ap()

    xT = din("xT", [1024, 8192])
    xoT = din("xoT", [1024, 2048])
    xo = din("xo", [2048, 1024])
    posT = din("posT", [128, 64], F32)
    posoT = din("posoT", [128, 16], F32)
    w_kv = din("w_kv", [1024, 2624])
    w_q = din("w_q", [1024, 772])
    w_ro = din("w_ro", [1024, 1536])
    w_g = din("w_g", [1024, 2048])
    w_a = din("w_a", [512, 1024])
    w_r = din("w_r", [1024, 1024])
    w_o = din("w_o", [1024, 1024])
    w_up = din("w_up", [1024, 4096])
    w_dn = din("w_dn", [4096, 1024])
    lnk_g = din("lnk_g", [64, 1])
    lnk_b = din("lnk_b", [64, 1])
    ln1_g = din("ln1_g", [128, 1024])
    ln1_b = din("ln1_b", [128, 1024])
    ln2_g = din("ln2_g", [128, 1024])
    ln2_b = din("ln2_b", [128, 1024])
    c_ident = din("c_ident", [128, 128])
    c_invf = din("c_invf", [128, 32])
    c_decayT = din("c_decayT", [128, 1024])
    c_zeta8 = din("c_zeta8", [128, 8])
    c_xiT = din("c_xiT", [64, 1024])
    c_gmat = din("c_gmat", [64, 1024])
    c_tb = din("c_tb", [128, 512])
    c_cm = din("c_cm", [128, 512])
    c_bt = din("c_bt", [128, 5 * 1024])
    c_c31 = din("c_c31", [128, 8])
    c_oh = din("c_oh", [128, 4])
    out = nc.dram_tensor("out", [2048, 1024], F32, kind="ExternalOutput").ap()

    KT = dscr("s_KT", [512, 8192], BF16)
    VV = dscr("s_V", [64, 128, 520], BF16)
    IKT = dscr("s_IKT", [64, 8192], BF16)
    YAT = dscr("s_YAT", [512, 2048], BF16)
    YRT = dscr("s_YRT", [1024, 2048], BF16)
    X1 = dscr("s_X1", [2048, 1024], F32)
    X1T = dscr("s_X1T", [1024, 2048], BF16)

    with contextlib.ExitStack() as st0:
        S = Sched(nc, st0)
        rr = {"cast": 0}

        def SB(stk, name, shape, dt):
            return stk.enter_context(nc.sbuf_tensor(name, shape, dt))

        def PS(stk, name, shape, dt):
            return stk.enter_context(nc.psum_tensor(name, shape, dt))

        def dma(out_ap, in_ap, r, w, eng="sp"):
            S.op(eng, lambda e, o=out_ap, i=in_ap: e.dma_start(out=o, in_=i), r=r, w=w, dma=True)

        def mm(out_ap, lhsT, rhs, start, stop, r, w):
            S.op("pe", lambda e, o=out_ap, l=lhsT, rh=rhs, s0=start, s1=stop: e.matmul(o, lhsT=l, rhs=rh, start=s0, stop=s1),
                 r=r, w=w)

        def tr(out_ap, in_ap, ident_ap, r, w):
            S.op("pe", lambda e, o=out_ap, i=in_ap, d=ident_ap: e.transpose(out=o, in_=i, identity=d), r=r, w=w)

        def act(out_ap, in_ap, func, r, w, bias=None, scale=None, accum=None):
            kw = {}
            if bias is not None:
                kw["bias"] = bias
            if scale is not None:
                kw["scale"] = scale
            if accum is not None:
                kw["accum_out"] = accum
            S.op("act", lambda e, o=out_ap, i=in_ap, f=func, kw=kw: e.activation(out=o, in_=i, func=f, **kw), r=r, w=w)

        def tt(eng, out_ap, in0, in1, op, r, w):
            S.op(eng, lambda e, o=out_ap, a=in0, b=in1, p=op: e.tensor_tensor(out=o, in0=a, in1=b, op=p), r=r, w=w)

        def ts(eng, out_ap, in0, s1, s2, op0, op1, r, w, accum=None):
            kw = {}
            if op1 is not None:
                kw["op1"] = op1
            if accum is not None:
                kw["accum_out"] = accum
            S.op(eng, lambda e, o=out_ap, a=in0, x1=s1, x2=s2, p0=op0, kw=kw:
                 e.tensor_scalar(out=o, in0=a, scalar1=x1, scalar2=x2, op0=p0, **kw), r=r, w=w)

        def stt(eng, out_ap, in0, scalar, in1, op0, op1, r, w):
            S.op(eng, lambda e, o=out_ap, a=in0, s=scalar, b=in1, p0=op0, p1=op1:
                 e.scalar_tensor_tensor(out=o, in0=a, scalar=s, in1=b, op0=p0, op1=p1), r=r, w=w)

        def cp(eng, out_ap, in_ap, r, w):
            if eng == "act":
                act(out_ap, in_ap, AF.Copy, r, w)
            else:
                S.op(eng, lambda e, o=out_ap, i=in_ap: e.tensor_copy(out=o, in_=i), r=r, w=w)

        def red(eng, out_ap, in_ap, op, r, w):
            S.op(eng, lambda e, o=out_ap, i=in_ap, p=op: e.tensor_reduce(out=o, in_=i, axis=AX.X, op=p), r=r, w=w)

        def memset(eng, ap, val, w):
            S.op(eng, lambda e, a=ap, v=val: e.memset(a, v), w=w)

        ident_f = SB(st0, "ident_f", [128, 128], F32)
        ident = SB(st0, "ident", [128, 128], BF16)
        dma(ident_f[:], c_ident[:, :], [], ["ident_f"])
        cp("dve", ident[:], ident_f[:], ["ident_f"], ["ident"])
        epsT = SB(st0, "epsT", [128, 1], F32)
        memset("dve", epsT[:], EPS, ["epsT"])

        def rstd(out_ap, var_ap, r, w):
            act(out_ap, var_ap, AF.Sqrt, list(r) + ["epsT"], list(w), bias=epsT[:, 0:1], scale=1.0)
            S.op("dve", lambda e, o=out_ap: e.reciprocal(out=o, in_=o), r=list(w), w=list(w))

        pb = [PS(st0, "pb%d" % i, [128, 512], F32) for i in range(7)]
        pT = PS(st0, "pT", [128, 1024], BF16)
        pk = ["pb%d" % i for i in range(7)]

        def cast_eng():
            rr["cast"] += 1
            return ("pool", "act")[rr["cast"] % 2]

        def load_w(stk_stage, dst, dst_key, src, nrow_chunks, cols, stg, col0=0):
            for rc in range(nrow_chunks):
                for c0 in range(0, cols, 2048):
                    cw = min(2048, cols - c0)
                    i = rr.setdefault("stg", 0)
                    rr["stg"] = i + 1
                    sk = "stg%d" % (i % len(stg))
                    stile = stg[i % len(stg)]
                    dma(stile[:, 0:cw], src[rc * 128:(rc + 1) * 128, c0:c0 + cw], [], [sk])
                    cp(cast_eng(), dst[:, rc, col0 + c0:col0 + c0 + cw], stile[:, 0:cw], [sk], [dst_key])

        def layer_norm_rows(stk, pre, pre_key, outf, out_key, g_t, b_t, tmp, nm):
            s1 = nm + "_s1"
            st_t = stk["stat"]
            red("dve", st_t[:, 0:1], pre[:], ALU.add, [pre_key], [s1])
            act(tmp[:], pre[:], AF.Square, [pre_key], [nm + "_tmp", s1 + "q"], accum=st_t[:, 1:2])
            ts("dve", st_t[:, 2:3], st_t[:, 0:1], 1.0 / 1024, None, ALU.mult, None, [s1], [s1 + "m"])
            ts("dve", st_t[:, 3:4], st_t[:, 1:2], 1.0 / 1024, None, ALU.mult, None, [s1 + "q"], [s1 + "e"])
            tt("dve", st_t[:, 4:5], st_t[:, 2:3], st_t[:, 2:3], ALU.mult, [s1 + "m"], [s1 + "mm"])
            tt("dve", st_t[:, 5:6], st_t[:, 3:4], st_t[:, 4:5], ALU.subtract, [s1 + "e", s1 + "mm"], [s1 + "v"])
            rstd(st_t[:, 6:7], st_t[:, 5:6], [s1 + "v"], [s1 + "r"])
            ts("dve", pre[:], pre[:], st_t[:, 2:3], st_t[:, 6:7], ALU.subtract, ALU.mult, [pre_key, s1 + "m", s1 + "r"], [pre_key])
            tt("pool", pre[:], pre[:], g_t[:], ALU.mult, [pre_key, "lng"], [pre_key])
            tt("pool", outf[:], pre[:], b_t[:], ALU.add, [pre_key, "lnb"], [out_key])

        if "A" in phases:
            with contextlib.ExitStack() as sa:
                Wkv = SB(sa, "Wkv", [128, 8, 2624], BF16)
                Wro = SB(sa, "Wro", [128, 8, 1536], BF16)
                stg = [SB(sa, "stgA%d" % i, [128, 2048], F32) for i in range(2)]
                xgf = SB(sa, "xgf", [128, 8, 512], F32)
                xgb = SB(sa, "xgb", [128, 8, 512], BF16)
                posf = SB(sa, "posf", [128, 64], F32)
                posof = SB(sa, "posof", [128, 16], F32)
                invf = SB(sa, "invf", [128, 32], F32)
                ang = SB(sa, "ang", [128, 2, 32], F32)
                CC = SB(sa, "CC", [128, 64], F32)
                SS = SB(sa, "SS", [128, 64], F32)
                kT_sb = SB(sa, "kT_sb", [128, 4, 512], BF16)
                v_sb = SB(sa, "v_sb", [128, 4, 8, 65], BF16)
                ikT_sb = SB(sa, "ikT_sb", [64, 512], BF16)
                ik_f = SB(sa, "ik_f", [128, 64], F32)
                ik_n = SB(sa, "ik_n", [128, 64], BF16)
                ik_junk = SB(sa, "ik_junk", [128, 64], F32)
                stat = SB(sa, "statA", [128, 8], F32)
                lng = SB(sa, "lnkg", [64, 1], F32)
                lnb = SB(sa, "lnkb", [64, 1], F32)
                kz = SB(sa, "kz", [128, 512], BF16)
                vr = SB(sa, "vr", [128, 1024], BF16)
                rA = SB(sa, "rA", [128, 512], F32)
                rB = SB(sa, "rB", [128, 512], F32)
                rO = SB(sa, "rO", [128, 512], F32)
                zeta8 = SB(sa, "zeta8", [128, 8], F32)
                R = SB(sa, "R", [64, 1024], F32)
                Rsel = SB(sa, "Rsel", [64, 1024], F32)
                Rselb = SB(sa, "Rselb", [64, 1024], BF16)
                gmat = SB(sa, "gmat", [64, 1024], F32)
                oh = SB(sa, "oh", [128, 4], F32)
                xiT = SB(sa, "xiT", [64, 1024], F32)
                decayT = SB(sa, "decayT", [128, 1024], F32)
                xof = SB(sa, "xof", [128, 8, 128], F32)
                xob = SB(sa, "xob", [128, 8, 128], BF16)
                qrb = SB(sa, "qrb", [128, 512], BF16)
                krb = SB(sa, "krb", [128, 512], BF16)
                qT = SB(sa, "qT", [64, 8, 128], BF16)
                qxiT = SB(sa, "qxiT", [64, 8, 128], BF16)
                kTo = SB(sa, "kTo", [64, 8, 128], BF16)
                vro = SB(sa, "vro", [128, 1024], BF16)
                sgr = SB(sa, "sgr", [128, 1024], F32)
                Dm = SB(sa, "Dm", [128, 1024], BF16)
                osb = SB(sa, "osb", [128, 1024], F32)
                osq = SB(sa, "osq", [128, 1024], F32)
                hst = SB(sa, "hst", [128, 8, 8], F32)
                yrb = SB(sa, "yrb", [128, 1024], BF16)
                yrT = SB(sa, "yrT", [128, 8, 128], BF16)

                load_w(sa, Wkv, "Wkv", w_kv, 8, 2624, stg)
                load_w(sa, Wro, "Wro", w_ro, 8, 1536, stg)
                dma(posf[:], posT[:, :], [], ["posf"])
                dma(posof[:], posoT[:, :], [], ["posof"])
                dma(invf[:], c_invf[:, :], [], ["invf"])
                dma(lng[:], lnk_g[:, :], [], ["lnkg"])
                dma(lnb[:], lnk_b[:, :], [], ["lnkb"])
                dma(zeta8[:], c_zeta8[:, :], [], ["zeta8"])
                dma(gmat[:], c_gmat[:, :], [], ["gmat"])
                dma(oh[:], c_oh[:, :], [], ["oh"])
                dma(xiT[:], c_xiT[:, :], [], ["xiT"])
                dma(decayT[:], c_decayT[:, :], [], ["decayT"])
                memset("pool", R[:], 0.0, ["R"])
                memset("pool", v_sb[:], 1.0, ["v_sb"])

                def cos_sin(pcol, pkey):
                    MG = 12582912.0
                    ts("dve", ang[:, 0, :], invf[:], pcol, None, ALU.mult, None, ["invf", pkey], ["ang"])
                    ts("dve", ang[:, 1, :], ang[:, 0, :], 0.5 * PI, None, ALU.add, None, ["ang"], ["ang"])
                    ts("dve", angk[:], ang[:], 1.0 / (2 * PI), MG, ALU.mult, ALU.add, ["ang"], ["angk"])
                    ts("dve", angk[:], angk[:], -MG, None, ALU.add, None, ["angk"], ["angk"])
                    stt("dve", ang[:], angk[:], -2 * PI, ang[:], ALU.mult, ALU.add, ["angk", "ang"], ["ang"])
                    ts("dve", ang[:], ang[:], 3.1415925, -3.1415925, ALU.min, ALU.max, ["ang"], ["ang"])
                    act(SS[:, 0:32], ang[:, 0, :], AF.Sin, ["ang"], ["SS"])
                    act(SS[:, 32:64], ang[:, 0, :], AF.Sin, ["ang"], ["SS"], scale=-1.0)
                    act(CC[:, 0:32], ang[:, 1, :], AF.Sin, ["ang"], ["CC"])
                    act(CC[:, 32:64], ang[:, 1, :], AF.Sin, ["ang"], ["CC"])

                angk = SB(sa, "angk", [128, 2, 32], F32)
                qf = SB(sa, "qf", [64, 1024], F32)

                def rope(src_ps, src_key, dst):
                    s3 = src_ps.rearrange("p (h d) -> p h d", h=8)
                    a3 = rA[:].rearrange("p (h d) -> p h d", h=8)
                    b3 = rB[:].rearrange("p (h d) -> p h d", h=8)
                    o3 = dst[:].rearrange("p (h d) -> p h d", h=8)
                    ccb = CC[:].unsqueeze(1).to_broadcast([128, 8, 64])
                    ssb = SS[:].unsqueeze(1).to_broadcast([128, 8, 64])
                    tt("dve", a3, s3, ccb, ALU.mult, [src_key, "CC"], ["rA"])
                    tt("dve", b3, s3, ssb, ALU.mult, [src_key, "SS"], ["rB"])
                    tt("pool", o3[:, :, 0:32], a3[:, :, 0:32], b3[:, :, 32:64], ALU.add, ["rA", "rB"], ["rO"])
                    tt("pool", o3[:, :, 32:64], a3[:, :, 32:64], b3[:, :, 0:32], ALU.add, ["rA", "rB"], ["rO"])

                xT3 = xT.rearrange("(c p) t -> p c t", p=128)
                xoT3 = xoT.rearrange("(c p) t -> p c t", p=128)
                KT3 = KT.rearrange("(c p) t -> p c t", p=128)
                YRT3 = YRT.rearrange("(c p) t -> p c t", p=128)
                VV3 = VV.rearrange("t p f -> p t f")

                for m in range(ng):
                    if not alltok:
                        break
                    dma(xgf[:], xT3[:, :, m * 512:(m + 1) * 512], [], ["xgf"])
                    cp("pool", xgb[:, 0:4, :], xgf[:, 0:4, :], ["xgf"], ["xgb"])
                    cp("act", xgb[:, 4:8, :], xgf[:, 4:8, :], ["xgf"], ["xgb"])
                    for fc in range(4):
                        for dc in range(8):
                            mm(pb[0][:], Wkv[:, dc, fc * 128:(fc + 1) * 128], xgb[:, dc, :], dc == 0, dc == 7,
                               ["Wkv", "xgb"], [pk[0]])
                        cp("act", kT_sb[:, fc, :], pb[0][:], [pk[0]], ["kT_sb"])
                    dma(KT3[:, :, m * 512:(m + 1) * 512], kT_sb[:], ["kT_sb"], ["KT"])
                    for i in range(4):
                        t = 4 * m + i
                        xs = slice(i * 128, (i + 1) * 128)
                        for dc in range(8):
                            mm(pb[0][:, 0:64], xgb[:, dc, xs], Wkv[:, dc, 1024:1088], dc == 0, dc == 7,
                               ["Wkv", "xgb"], [pk[0]])
                        cp("dve", ik_f[:], pb[0][:, 0:64], [pk[0]], ["ik_f"])
                        red("dve", stat[:, 0:1], ik_f[:], ALU.add, ["ik_f"], ["st0"])
                        act(ik_junk[:], ik_f[:], AF.Square, ["ik_f"], ["ik_junk", "st1"], accum=stat[:, 1:2])
                        ts("dve", stat[:, 2:3], stat[:, 0:1], 1.0 / 64, None, ALU.mult, None, ["st0"], ["st2"])
                        ts("dve", stat[:, 3:4], stat[:, 1:2], 1.0 / 64, None, ALU.mult, None, ["st1"], ["st3"])
                        tt("dve", stat[:, 4:5], stat[:, 2:3], stat[:, 2:3], ALU.mult, ["st2"], ["st4"])
                        tt("dve", stat[:, 5:6], stat[:, 3:4], stat[:, 4:5], ALU.subtract, ["st3", "st4"], ["st5"])
                        rstd(stat[:, 6:7], stat[:, 5:6], ["st5"], ["st6"])
                        ts("dve", ik_n[:], ik_f[:], stat[:, 2:3], stat[:, 6:7], ALU.subtract, ALU.mult,
                           ["ik_f", "st2", "st6"], ["ik_n"])
                        tr(pT[0:64, 0:128], ik_n[:], ident[:], ["ik_n", "ident"], ["pT"])
                        act(ikT_sb[:, xs], pT[0:64, 0:128], AF.Identity, ["pT", "lnkg", "lnkb"], ["ikT_sb"],
                            bias=lnb[:, 0:1], scale=lng[:, 0:1])
                        for dc in range(8):
                            mm(pb[1][:], xgb[:, dc, xs], Wkv[:, dc, 512:1024], dc == 0, dc == 7, ["Wkv", "xgb"], [pk[1]])
                        cp("act", v_sb[:, i, :, 0:64], pb[1][:].rearrange("p (h d) -> p h d", h=8), [pk[1]], ["v_sb"])
                        for dc in range(8):
                            mm(pb[2][:], xgb[:, dc, xs], Wkv[:, dc, 1088:1600], dc == 0, dc == 7, ["Wkv", "xgb"], [pk[2]])
                        for hf in range(2):
                            for dc in range(8):
                                mm(pb[3 + hf][:], xgb[:, dc, xs], Wkv[:, dc, 1600 + hf * 512:2112 + hf * 512],
                                   dc == 0, dc == 7, ["Wkv", "xgb"], [pk[3 + hf]])
                            cp("act", vr[:, hf * 512:(hf + 1) * 512], pb[3 + hf][:], [pk[3 + hf]], ["vr"])
                        cos_sin(posf[:, t:t + 1], "posf")
                        rope(pb[2][:], pk[2], rO)
                        tt("dve", kz[:].rearrange("p (h d) -> p h d", h=8), rO[:].rearrange("p (h d) -> p h d", h=8),
                           zeta8[:].unsqueeze(2).to_broadcast([128, 8, 64]), ALU.mult, ["rO", "zeta8"], ["kz"])
                        if i == 0:
                            ts("dve", Rsel[:], R[:], oh[0:64, 0:1], None, ALU.mult, None, ["R", "oh"], ["Rsel"])
                        else:
                            stt("dve", Rsel[:], R[:], oh[0:64, i:i + 1], Rsel[:], ALU.mult, ALU.add, ["R", "oh", "Rsel"], ["Rsel"])
                        for h in range(8):
                            mm(pb[5 + h // 4][0:64, (h % 4) * 128:(h % 4 + 1) * 128], kz[:, h * 64:(h + 1) * 64],
                               vr[:, h * 128:(h + 1) * 128], True, True, ["kz", "vr"], [pk[5 + h // 4]])
                        tt("pool", R[:], R[:], gmat[:], ALU.mult, ["R", "gmat"], ["R"])
                        tt("dve", R[:, 0:512], R[:, 0:512], pb[5][0:64, :], ALU.add, ["R", pk[5]], ["R"])
                        tt("dve", R[:, 512:1024], R[:, 512:1024], pb[6][0:64, :], ALU.add, ["R", pk[6]], ["R"])
                    dma(VV3[:, 4 * m:4 * m + 4, :], v_sb[:].rearrange("p t h d -> p t (h d)"), ["v_sb"], ["VV"])
                    dma(IKT[:, m * 512:(m + 1) * 512], ikT_sb[:], ["ikT_sb"], ["IKT"])
                    if not own:
                        continue

                    os_ = slice(m * 128, (m + 1) * 128)
                    dma(xof[:], xoT3[:, :, os_], [], ["xof"])
                    cp("pool", xob[:], xof[:], ["xof"], ["xob"])
                    cp("act", Rselb[:], Rsel[:], ["Rsel"], ["Rselb"])
                    cos_sin(posof[:, m:m + 1], "posof")
                    for dc in range(8):
                        mm(pb[1][:], xob[:, dc, :], Wro[:, dc, 0:512], dc == 0, dc == 7, ["Wro", "xob"], [pk[1]])
                    rope(pb[1][:], pk[1], rO)
                    cp("act", qrb[:], rO[:], ["rO"], ["qrb"])
                    for dc in range(8):
                        mm(pb[2][:], xob[:, dc, :], Wkv[:, dc, 1088:1600], dc == 0, dc == 7, ["Wkv", "xob"], [pk[2]])
                    rope(pb[2][:], pk[2], rO)
                    S.op("act", lambda e, o=krb[:], i=rO[:]: e.mul(out=o, in_=i, mul=0.125), r=["rO"], w=["krb"])
                    for hf in range(2):
                        for dc in range(8):
                            mm(pb[3 + hf][:], xob[:, dc, :], Wkv[:, dc, 1600 + hf * 512:2112 + hf * 512],
                               dc == 0, dc == 7, ["Wkv", "xob"], [pk[3 + hf]])
                        cp("act", vro[:, hf * 512:(hf + 1) * 512], pb[3 + hf][:], [pk[3 + hf]], ["vro"])
                    for hf in range(2):
                        for dc in range(8):
                            mm(pb[5 + hf][:], xob[:, dc, :], Wro[:, dc, 512 + hf * 512:1024 + hf * 512],
                               dc == 0, dc == 7, ["Wro", "xob"], [pk[5 + hf]])
                        act(sgr[:, hf * 512:(hf + 1) * 512], pb[5 + hf][:], AF.Sigmoid, [pk[5 + hf]], ["sgr"])
                        tt("dve", sgr[:, hf * 512:(hf + 1) * 512], sgr[:, hf * 512:(hf + 1) * 512], pb[5 + hf][:], ALU.mult,
                           ["sgr", pk[5 + hf]], ["sgr"])
                    if ostage < 1:
                        continue
                    for h in range(8):
                        tr(pT[0:64, h * 128:(h + 1) * 128], qrb[:, h * 64:(h + 1) * 64], ident[:], ["qrb", "ident"], ["pT"])
                    cp("act", qf[:], pT[0:64, :], ["pT"], ["qf"])
                    cp("pool", qT[:].rearrange("p h n -> p (h n)"), qf[:], ["qf"], ["qT"])
                    tt("dve", qxiT[:].rearrange("p h n -> p (h n)"), qf[:], xiT[:], ALU.mult, ["qf", "xiT"], ["qxiT"])
                    for h in range(8):
                        tr(pT[0:64, h * 128:(h + 1) * 128], krb[:, h * 64:(h + 1) * 64], ident[:], ["krb", "ident"], ["pT"])
                    cp("act", kTo[:].rearrange("p h n -> p (h n)"), pT[0:64, :], ["pT"], ["kTo"])
                    if ostage < 2:
                        continue
                    for h in range(8):
                        mm(pb[3 + h // 4][:, (h % 4) * 128:(h % 4 + 1) * 128], kTo[:, h, :], qT[:, h, :], True, True,
                           ["kTo", "qT"], [pk[3 + h // 4]])
                    tt("dve", Dm[:, 0:512], pb[3][:], decayT[:, 0:512], ALU.mult, [pk[3], "decayT"], ["Dm"])
                    tt("dve", Dm[:, 512:1024], pb[4][:], decayT[:, 512:1024], ALU.mult, [pk[4], "decayT"], ["Dm"])
                    for h in range(8):
                        o_ap = pb[5 + h // 4][:, (h % 4) * 128:(h % 4 + 1) * 128]
                        mm(o_ap, Dm[:, h * 128:(h + 1) * 128], vro[:, h * 128:(h + 1) * 128], True, False,
                           ["Dm", "vro"], [pk[5 + h // 4]])
                        mm(o_ap, qxiT[:, h, :], Rselb[:, h * 128:(h + 1) * 128], False, True,
                           ["qxiT", "Rselb"], [pk[5 + h // 4]])
                    if ostage < 3:
                        continue
                    cp("act", osb[:, 0:512], pb[5][:], [pk[5]], ["osb"])
                    cp("act", osb[:, 512:1024], pb[6][:], [pk[6]], ["osb"])
                    o3 = osb[:].rearrange("p (h v) -> p h v", h=8)
                    q3 = osq[:].rearrange("p (h v) -> p h v", h=8)
                    red("dve", hst[:, 0, :], o3, ALU.add, ["osb"], ["h0"])
                    tt("pool", osq[:], osb[:], osb[:], ALU.mult, ["osb"], ["osq"])
                    red("dve", hst[:, 1, :], q3, ALU.add, ["osq"], ["h1"])
                    ts("dve", hst[:, 2, :], hst[:, 0, :], 1.0 / 128, None, ALU.mult, None, ["h0"], ["h2"])
                    ts("dve", hst[:, 3, :], hst[:, 1, :], 1.0 / 128, None, ALU.mult, None, ["h1"], ["h3"])
                    tt("dve", hst[:, 4, :], hst[:, 2, :], hst[:, 2, :], ALU.mult, ["h2"], ["h4"])
                    tt("dve", hst[:, 5, :], hst[:, 3, :], hst[:, 4, :], ALU.subtract, ["h3", "h4"], ["h5"])
                    rstd(hst[:, 6, :], hst[:, 5, :], ["h5"], ["h6"])
                    tt("dve", q3, o3, hst[:, 2, :].unsqueeze(2).to_broadcast([128, 8, 128]), ALU.subtract, ["osb", "h2", "osq"], ["osq"])
                    tt("dve", q3, q3, hst[:, 6, :].unsqueeze(2).to_broadcast([128, 8, 128]), ALU.mult, ["osq", "h6"], ["osq"])
                    if ostage < 4:
                        continue
                    tt("pool", yrb[:], osq[:], sgr[:], ALU.mult, ["osq", "sgr"], ["yrb"])
                    for fc in range(8):
                        tr(pT[:, fc * 128:(fc + 1) * 128], yrb[:, fc * 128:(fc + 1) * 128], ident[:], ["yrb", "ident"], ["pT"])
                    cp("act", yrT[:].rearrange("p c n -> p (c n)"), pT[:, :], ["pT"], ["yrT"])
                    dma(YRT3[:, :, os_], yrT[:], ["yrT"], ["YRT"])
            S.barrier()

        if "B" in phases:
            with contextlib.ExitStack() as sbk:
                KTs = SB(sbk, "KTs", [128, 4, 8192], BF16)
                Vc = [SB(sbk, "Vc%d" % i, [128, 4, 520], BF16) for i in range(2)]
                IKs = SB(sbk, "IKs", [64, 8192], BF16)
                score = SB(sbk, "score", [128, 8192], F32)
                Wq = SB(sbk, "Wq", [128, 8, 772], BF16)
                stg = [SB(sbk, "stgB%d" % i, [128, 1024], F32) for i in range(2)]
                xof = SB(sbk, "xofB", [128, 8, 128], F32)
                xob = SB(sbk, "xobB", [128, 8, 128], BF16)
                qaT = SB(sbk, "qaT", [128, 4, 2, 128], BF16)
                iqT = SB(sbk, "iqT", [64, 4, 128], BF16)
                iwf = SB(sbk, "iwf", [128, 4], F32)
                aw = SB(sbk, "aw", [128, 4], F32)
                sgn = SB(sbk, "sgn", [128, 4], F32)
                rl = [SB(sbk, "rl%d" % i, [128, 512], F32) for i in range(2)]
                tb = SB(sbk, "tb", [128, 512], F32)
                cm = SB(sbk, "cm", [128, 512], F32)
                btf = SB(sbk, "btf", [128, 1024], F32)
                bts = SB(sbk, "bts", [128, 5, 1024], BF16)
                c31 = SB(sbk, "c31", [128, 8], F32)
                bs = SB(sbk, "bs", [128, 16], F32)
                junk = SB(sbk, "junkB", [128, 2048], BF16)
                mk = [SB(sbk, "mk%d" % i, [128, 128], BF16) for i in range(2)]
                mkT = [SB(sbk, "mkT%d" % i, [128, 128], BF16) for i in range(2)]
                Eb = [SB(sbk, "Eb%d" % i, [128, 1024], BF16) for i in range(2)]
                Pb = [SB(sbk, "Pb%d" % i, [128, 1024], BF16) for i in range(2)]
                osb = SB(sbk, "osbB", [128, 520], F32)
                rs = SB(sbk, "rsB", [128, 8], F32)
                yab = SB(sbk, "yab", [128, 512], BF16)
                yaT = SB(sbk, "yaT", [128, 4, 128], BF16)

                KT3 = KT.rearrange("(c p) t -> p c t", p=128)
                for c4 in range(4):
                    dma(KTs[:, c4, :], KT3[:, c4, :], ["KT"], ["KTs"])
                VV3 = VV.rearrange("t p f -> p t f")
                dma(IKs[:], IKT[:, :], ["IKT"], ["IKs"])
                load_w(sbk, Wq, "Wq", w_q, 8, 772, stg)
                memset("pool", qaT[:], 0.0, ["qaT"])
                zr = SB(sbk, "zr", [128, 512], BF16)
                memset("pool", zr[:], 0.0, ["zr"])
                dma(tb[:], c_tb[:, :], [], ["tb"])
                dma(cm[:], c_cm[:, :], [], ["cm"])
                dma(c31[:], c_c31[:, :], [], ["c31"])
                for kr in range(5):
                    dma(btf[:], c_bt[:, kr * 1024:(kr + 1) * 1024], [], ["btf"])
                    tt("dve", bts[:, kr, :].rearrange("p (h q) -> p h q", h=8), btf[:].rearrange("p (h q) -> p h q", h=8),
                       c31[:].unsqueeze(2).to_broadcast([128, 8, 128]), ALU.subtract, ["btf", "c31"], ["bts"])
                xoT3 = xoT.rearrange("(c p) t -> p c t", p=128)
                YAT3 = YAT.rearrange("(c p) t -> p c t", p=128)
                vcount = [0]
                LO, HI, MID, CNT, PRED, DLT, THR = 0, 1, 2, 3, 4, 5, 6

                for m in range(mstart, nb):
                    os_ = slice(m * 128, (m + 1) * 128)
                    n5 = m + 1
                    nk = min(4 * (m + 1), nkcap)
                    n = 512 * (m + 1)
                    dma(xof[:], xoT3[:, :, os_], [], ["xof"])
                    cp("pool", xob[:], xof[:], ["xof"], ["xob"])
                    for fc in range(4):
                        for dc in range(8):
                            mm(pb[0][:, fc * 128:(fc + 1) * 128], Wq[:, dc, fc * 128:(fc + 1) * 128], xob[:, dc, :],
                               dc == 0, dc == 7, ["Wq", "xob"], [pk[0]])
                    p03 = pb[0][:].rearrange("p (c n) -> p c n", c=4)
                    S.op("act", lambda e, o=qaT[0:64, :, 0, :], i=p03[0:64, :, :]: e.mul(out=o, in_=i, mul=0.125),
                         r=[pk[0]], w=["qaT"])
                    S.op("act", lambda e, o=qaT[64:128, :, 1, :], i=p03[64:128, :, :]: e.mul(out=o, in_=i, mul=0.125),
                         r=[pk[0]], w=["qaT"])
                    for h in range(4):
                        for dc in range(8):
                            mm(pb[2][0:64, h * 128:(h + 1) * 128], Wq[:, dc, 512 + h * 64:512 + (h + 1) * 64], xob[:, dc, :],
                               dc == 0, dc == 7, ["Wq", "xob"], [pk[2]])
                    cp("act", iqT[:].rearrange("p h n -> p (h n)"), pb[2][0:64, :], [pk[2]], ["iqT"])
                    for dc in range(8):
                        mm(pb[3][:, 0:4], xob[:, dc, :], Wq[:, dc, 768:772], dc == 0, dc == 7, ["Wq", "xob"], [pk[3]])
                    cp("dve", iwf[:], pb[3][:, 0:4], [pk[3]], ["iwf"])
                    act(aw[:], iwf[:], AF.Abs, ["iwf"], ["aw"], scale=1.0 / 16)
                    act(sgn[:], iwf[:], AF.Sign, ["iwf"], ["sgn"])
                    if bstage < 1:
                        continue
                    for c5 in range(n5):
                        ks = slice(c5 * 512, (c5 + 1) * 512)
                        for h in range(4):
                            mm(pb[h][:], iqT[:, h, :], IKs[:, ks], True, True, ["iqT", "IKs"], [pk[h]])
                        ts("pool", score[:, ks], tb[:], -1e-30 * 512 * c5, None, ALU.add, None, ["tb"], ["score"])
                        if c5 == n5 - 1:
                            tt("pool", score[:, ks], score[:, ks], cm[:], ALU.add, ["score", "cm"], ["score"])
                        for h in range(4):
                            rk = "rl%d" % (h % 2)
                            act(rl[h % 2][:], pb[h][:], AF.Relu, [pk[h], "aw"], [rk], scale=aw[:, h:h + 1])
                            stt("dve", score[:, ks], rl[h % 2][:], sgn[:, h:h + 1], score[:, ks], ALU.mult, ALU.add,
                                [rk, "sgn", "score"], ["score"])
                    if bstage < 2:
                        continue
                    red("dve", bs[:, HI:HI + 1], score[:, 0:n], ALU.max, ["score"], ["bs"])
                    ts("dve", bs[:, HI:HI + 1], bs[:, HI:HI + 1], 1e-3, None, ALU.add, None, ["bs"], ["bs"])
                    memset("dve", bs[:, LO:LO + 1], LO_INIT, ["bs"])
                    for it in range(NBIS):
                        tt("dve", bs[:, MID:MID + 1], bs[:, LO:LO + 1], bs[:, HI:HI + 1], ALU.add, ["bs"], ["bs"])
                        ts("dve", bs[:, MID:MID + 1], bs[:, MID:MID + 1], 0.5, None, ALU.mult, None, ["bs"], ["bs"])
                        nch = (n + 2047) // 2048
                        memset("dve", bs[:, 8:8 + nch], 0.0, ["bs"])
                        for ci in range(nch):
                            c0 = ci * 2048
                            c1 = min(n, c0 + 2048)
                            ts("dve", junk[:, 0:c1 - c0], score[:, c0:c1], bs[:, MID:MID + 1], 0.0, ALU.is_ge, ALU.add,
                               ["score", "bs"], ["junk", "bs"], accum=bs[:, 8 + ci:9 + ci])
                        if nch > 1:
                            red("dve", bs[:, CNT:CNT + 1], bs[:, 8:8 + nch], ALU.add, ["bs"], ["bs"])
                        else:
                            cp("dve", bs[:, CNT:CNT + 1], bs[:, 8:9], ["bs"], ["bs"])
                        ts("dve", bs[:, PRED:PRED + 1], bs[:, CNT:CNT + 1], 255.5, None, ALU.is_ge, None, ["bs"], ["bs"])
                        tt("dve", bs[:, DLT:DLT + 1], bs[:, MID:MID + 1], bs[:, LO:LO + 1], ALU.subtract, ["bs"], ["bs"])
                        stt("dve", bs[:, LO:LO + 1], bs[:, DLT:DLT + 1], bs[:, PRED:PRED + 1], bs[:, LO:LO + 1],
                            ALU.mult, ALU.add, ["bs"], ["bs"])
                        tt("dve", bs[:, DLT:DLT + 1], bs[:, HI:HI + 1], bs[:, MID:MID + 1], ALU.subtract, ["bs"], ["bs"])
                        stt("dve", bs[:, HI:HI + 1], bs[:, DLT:DLT + 1], bs[:, PRED:PRED + 1], bs[:, MID:MID + 1],
                            ALU.mult, ALU.add, ["bs"], ["bs"])
                    if bstage < 3:
                        continue
                    for hh in range(2):
                        mm(pb[4 + hh][:], zr[:, 0:128], zr[:], True, True, ["zr"], [pk[4 + hh]])
                    for kc in range(nk):
                        sl = kc % 2
                        kcs = slice(kc * 128, (kc + 1) * 128)
                        pS = (pb[2 * sl], pb[2 * sl + 1])
                        pSk = (pk[2 * sl], pk[2 * sl + 1])
                        kr = kc - (4 * m - 1)
                        near = 0 <= kr <= 4 and kc >= 0 and not nobias
                        if kc % 4 == 0:
                            vs_ = vcount[0] % 2
                            vcount[0] += 1
                            dma(Vc[vs_][:], VV3[:, kc:kc + 4, :], ["VV"], ["Vc%d" % vs_])
                        Vv = Vc[vs_][:].rearrange("p t (h d) -> p t h d", h=8)
                        for h in range(8):
                            mm(pS[h // 4][:, (h % 4) * 128:(h % 4 + 1) * 128], KTs[:, h // 2, kcs], qaT[:, h // 2, h % 2, :], True, not near,
                               ["KTs", "qaT"], [pSk[h // 4]])
                            if near:
                                mm(pS[h // 4][:, (h % 4) * 128:(h % 4 + 1) * 128], ident[:], bts[:, kr, h * 128:(h + 1) * 128], False, True,
                                   ["ident", "bts"], [pSk[h // 4]])
                            elif dummy == 1:
                                mm(pb[6][:, 0:128], ident[:], bts[:, 0, h * 128:(h + 1) * 128], True, True,
                                   ["ident", "bts"], [pk[6]])
                        ts("dve", mk[sl][:], score[:, kcs], bs[:, LO:LO + 1], None, ALU.is_ge, None, ["score", "bs"], ["mk%d" % sl])
                        tr(pT[:, sl * 128:(sl + 1) * 128], mk[sl][:], ident[:], ["mk%d" % sl, "ident"], ["pT%d" % sl])
                        for hh in range(2):
                            act(Eb[sl][:, hh * 512:(hh + 1) * 512], pS[hh][:], AF.Exp, [pSk[hh]], ["Eb%d" % sl])
                        cp("act", mkT[sl][:], pT[:, sl * 128:(sl + 1) * 128], ["pT%d" % sl], ["mkT%d" % sl])
                        tt("dve", Pb[sl][:].rearrange("p (h q) -> p h q", h=8), Eb[sl][:].rearrange("p (h q) -> p h q", h=8),
                           mkT[sl][:].unsqueeze(1).to_broadcast([128, 8, 128]), ALU.mult,
                           ["Eb%d" % sl, "mkT%d" % sl], ["Pb%d" % sl])
                        for h in range(8):
                            mm(pb[4 + h // 4][:, (h % 4) * 128:(h % 4) * 128 + 65], Pb[sl][:, h * 128:(h + 1) * 128],
                               Vv[:, kc % 4, h, :], False, kc == nk - 1, ["Pb%d" % sl, "Vc%d" % vs_], [pk[4 + h // 4]])
                    if bstage < 4:
                        continue
                    cp("act", osb[:, 0:260].rearrange("p (h d) -> p h d", h=4), pb[4][:].rearrange("p (h d) -> p h d", h=4)[:, :, 0:65],
                       [pk[4]], ["osbB"])
                    cp("act", osb[:, 260:520].rearrange("p (h d) -> p h d", h=4), pb[5][:].rearrange("p (h d) -> p h d", h=4)[:, :, 0:65],
                       [pk[5]], ["osbB"])
                    o4 = osb[:].rearrange("p (h d) -> p h d", h=8)
                    S.op("dve", lambda e, o=rs[:], i=o4[:, :, 64]: e.reciprocal(out=o, in_=i), r=["osbB"], w=["rsB"])
                    tt("dve", yab[:].rearrange("p (h d) -> p h d", h=8), o4[:, :, 0:64],
                       rs[:].unsqueeze(2).to_broadcast([128, 8, 64]), ALU.mult, ["osbB", "rsB"], ["yab"])
                    for fc in range(4):
                        tr(pT[:, 256 + fc * 128:256 + (fc + 1) * 128], yab[:, fc * 128:(fc + 1) * 128], ident[:],
                           ["yab", "ident"], ["pTy"])
                    cp("act", yaT[:].rearrange("p c n -> p (c n)"), pT[:, 256:768], ["pTy"], ["yaT"])
                    dma(YAT3[:, :, os_], yaT[:], ["yaT"], ["YAT"])
            S.barrier()

        if "C" in phases:
            with contextlib.ExitStack() as sc:
                Wg = SB(sc, "Wg", [128, 8, 2048], BF16)
                Wa = SB(sc, "Wa", [128, 4, 1024], BF16)
                Wr = SB(sc, "Wr", [128, 8, 1024], BF16)
                Wo = SB(sc, "Wo", [128, 8, 1024], BF16)
                stg = [SB(sc, "stgC%d" % i, [128, 2048], F32) for i in range(2)]
                g1 = SB(sc, "g1", [128, 1024], F32)
                b1 = SB(sc, "b1", [128, 1024], F32)
                xof = SB(sc, "xofC", [128, 8, 128], F32)
                xob = SB(sc, "xobC", [128, 8, 128], BF16)
                yaT = SB(sc, "yaTC", [128, 4, 128], BF16)
                yrT = SB(sc, "yrTC", [128, 8, 128], BF16)
                sg = SB(sc, "sg", [128, 2048], F32)
                hf_ = SB(sc, "hf", [128, 1024], F32)
                h2 = SB(sc, "h2", [128, 1024], F32)
                hb = SB(sc, "hb", [128, 1024], BF16)
                hT = SB(sc, "hT", [128, 8, 128], BF16)
                xres = SB(sc, "xres", [128, 1024], F32)
                pre = SB(sc, "pre", [128, 1024], F32)
                tmp = SB(sc, "tmpC", [128, 1024], F32)
                x1f = SB(sc, "x1f", [128, 1024], F32)
                x1b = SB(sc, "x1b", [128, 1024], BF16)
                x1T = SB(sc, "x1T", [128, 8, 128], BF16)
                stat = SB(sc, "statC", [128, 8], F32)
                load_w(sc, Wg, "Wg", w_g, 8, 2048, stg)
                load_w(sc, Wa, "Wa", w_a, 4, 1024, stg)
                load_w(sc, Wr, "Wr", w_r, 8, 1024, stg)
                load_w(sc, Wo, "Wo", w_o, 8, 1024, stg)
                dma(g1[:], ln1_g[:, :], [], ["lng"])
                dma(b1[:], ln1_b[:, :], [], ["lnb"])
                xoT3 = xoT.rearrange("(c p) t -> p c t", p=128)
                YAT3 = YAT.rearrange("(c p) t -> p c t", p=128)
                YRT3 = YRT.rearrange("(c p) t -> p c t", p=128)
                X1T3 = X1T.rearrange("(c p) t -> p c t", p=128)
                for m in range(16):
                    os_ = slice(m * 128, (m + 1) * 128)
                    dma(xof[:], xoT3[:, :, os_], [], ["xof"])
                    cp("pool", xob[:], xof[:], ["xof"], ["xob"])
                    dma(yaT[:], YAT3[:, :, os_], ["YAT"], ["yaT"])
                    dma(yrT[:], YRT3[:, :, os_], ["YRT"], ["yrT"])
                    dma(xres[:], xo[os_, :], [], ["xres"])
                    for q4 in range(4):
                        for dc in range(8):
                            mm(pb[q4][:], xob[:, dc, :], Wg[:, dc, q4 * 512:(q4 + 1) * 512], dc == 0, dc == 7,
                               ["Wg", "xob"], [pk[q4]])
                        act(sg[:, q4 * 512:(q4 + 1) * 512], pb[q4][:], AF.Sigmoid, [pk[q4]], ["sg"])
                    for hh in range(2):
                        for fc in range(4):
                            mm(pb[4 + hh][:], yaT[:, fc, :], Wa[:, fc, hh * 512:(hh + 1) * 512], fc == 0, fc == 3,
                               ["Wa", "yaT"], [pk[4 + hh]])
                        tt("dve", hf_[:, hh * 512:(hh + 1) * 512], pb[4 + hh][:], sg[:, hh * 512:(hh + 1) * 512], ALU.mult,
                           [pk[4 + hh], "sg"], ["hf"])
                    for hh in range(2):
                        for fc in range(8):
                            mm(pb[hh][:], yrT[:, fc, :], Wr[:, fc, hh * 512:(hh + 1) * 512], fc == 0, fc == 7,
                               ["Wr", "yrT"], [pk[hh]])
                        tt("dve", h2[:, hh * 512:(hh + 1) * 512], pb[hh][:], sg[:, 1024 + hh * 512:1024 + (hh + 1) * 512],
                           ALU.mult, [pk[hh], "sg"], ["h2"])
                    tt("pool", hb[:], hf_[:], h2[:], ALU.add, ["hf", "h2"], ["hb"])
                    for fc in range(8):
                        tr(pT[:, fc * 128:(fc + 1) * 128], hb[:, fc * 128:(fc + 1) * 128], ident[:], ["hb", "ident"], ["pT"])
                    cp("act", hT[:].rearrange("p c n -> p (c n)"), pT[:, :], ["pT"], ["hT"])
                    for hh in range(2):
                        for fc in range(8):
                            mm(pb[2 + hh][:], hT[:, fc, :], Wo[:, fc, hh * 512:(hh + 1) * 512], fc == 0, fc == 7,
                               ["Wo", "hT"], [pk[2 + hh]])
                        stt("dve", pre[:, hh * 512:(hh + 1) * 512], xres[:, hh * 512:(hh + 1) * 512], ALPHA, pb[2 + hh][:],
                            ALU.mult, ALU.add, ["xres", pk[2 + hh]], ["pre"])
                    layer_norm_rows({"stat": stat}, pre, "pre", x1f, "x1f", g1, b1, tmp, "lnC")
                    dma(X1[os_, :], x1f[:], ["x1f"], ["X1"])
                    cp("act", x1b[:], x1f[:], ["x1f"], ["x1b"])
                    for fc in range(8):
                        tr(pT[:, fc * 128:(fc + 1) * 128], x1b[:, fc * 128:(fc + 1) * 128], ident[:], ["x1b", "ident"], ["pT"])
                    cp("act", x1T[:].rearrange("p c n -> p (c n)"), pT[:, :], ["pT"], ["x1T"])
                    dma(X1T3[:, :, os_], x1T[:], ["x1T"], ["X1T"])
            S.barrier()

        if "D" in phases:
            with contextlib.ExitStack() as sd:
                Wdn = SB(sd, "Wdn", [128, 32, 1024], BF16)
                HT = SB(sd, "HT", [128, 32, 1024], BF16)
                x1T = SB(sd, "x1TD", [128, 8, 1024], BF16)
                stg = [SB(sd, "stgD%d" % i, [128, 1024], F32) for i in range(3)]
                wub = [SB(sd, "wub%d" % i, [128, 8, 128], BF16) for i in range(2)]
                rl = [SB(sd, "rlD%d" % i, [128, 512], F32) for i in range(2)]
                g2 = SB(sd, "g2", [128, 1024], F32)
                b2 = SB(sd, "b2", [128, 1024], F32)
                x1r = SB(sd, "x1r", [128, 1024], F32)
                pre = SB(sd, "preD", [128, 1024], F32)
                tmp = SB(sd, "tmpD", [128, 1024], F32)
                of_ = SB(sd, "of", [128, 1024], F32)
                stat = SB(sd, "statD", [128, 8], F32)
                load_w(sd, Wdn, "Wdn", w_dn, 32, 1024, stg)
                dma(g2[:], ln2_g[:, :], [], ["lng"])
                dma(b2[:], ln2_b[:, :], [], ["lnb"])
                X1T3 = X1T.rearrange("(c p) t -> p c t", p=128)
                w_up3 = w_up.rearrange("(c p) f -> p c f", p=128)
                for th in range(2):
                    dma(x1T[:], X1T3[:, :, th * 1024:(th + 1) * 1024], ["X1T"], ["x1TD"])
                    for f in range(32):
                        ws = f % 2
                        wk = "wub%d" % ws
                        i = rr["stg"]
                        rr["stg"] = i + 1
                        sk = "stg%d" % (i % 3)
                        stile = stg[i % 3]
                        dma(stile[:].rearrange("p (c f) -> p c f", c=8), w_up3[:, :, f * 128:(f + 1) * 128], [], [sk])
                        cp("pool", wub[ws][:].rearrange("p c f -> p (c f)"), stile[:], [sk], [wk])
                        for s2 in range(2):
                            bank = 2 * (f % 2) + s2
                            for dc in range(8):
                                mm(pb[bank][:], wub[ws][:, dc, :], x1T[:, dc, s2 * 512:(s2 + 1) * 512], dc == 0, dc == 7,
                                   [wk, "x1TD"], [pk[bank]])
                            rk = "rlD%d" % s2
                            act(rl[s2][:], pb[bank][:], AF.Relu, [pk[bank]], [rk])
                            tt("dve" if s2 == 0 else "pool", HT[:, f, s2 * 512:(s2 + 1) * 512], rl[s2][:], rl[s2][:], ALU.mult,
                               [rk], ["HT"])
                    for tl in range(8):
                        tg = th * 8 + tl
                        os_ = slice(tg * 128, (tg + 1) * 128)
                        dma(x1r[:], X1[os_, :], ["X1"], ["x1r"])
                        for hh in range(2):
                            for f in range(32):
                                mm(pb[4 + hh][:], HT[:, f, tl * 128:(tl + 1) * 128], Wdn[:, f, hh * 512:(hh + 1) * 512],
                                   f == 0, f == 31, ["HT", "Wdn"], [pk[4 + hh]])
                            stt("dve", pre[:, hh * 512:(hh + 1) * 512], x1r[:, hh * 512:(hh + 1) * 512], ALPHA, pb[4 + hh][:],
                                ALU.mult, ALU.add, ["x1r", pk[4 + hh]], ["preD"])
                        layer_norm_rows({"stat": stat}, pre, "preD", of_, "of", g2, b2, tmp, "lnD")
                        dma(out[os_, :], of_[:], ["of"], ["out"])
        S.emit()
    return nc


def _t5_bucket(n):
    n = np.maximum(n, 0)
    nf = np.maximum(n, 1).astype(np.float32)
    large = 16 + (np.log(nf / np.float32(16)) / np.float32(math.log(128 / 16)) * np.float32(16)).astype(np.int32)
    large = np.minimum(large, 31)
    return np.where(n < 16, n, large)


def _consts(j):
    c = {}
    c["c_ident"] = np.eye(128, dtype=np.float32)
    half = 32
    inv = (np.float32(10000.0) ** (-np.arange(half, dtype=np.float32) / np.float32(half))).astype(np.float32)
    c["c_invf"] = np.broadcast_to(inv[None, :], (128, 32)).copy()
    H = 8
    gamma = (1.0 - 2.0 ** (-5.0 - np.arange(H, dtype=np.float64)))
    lg = np.log(gamma)
    nn = np.arange(128, dtype=np.float64)
    diff = nn[None, :] - nn[:, None]
    dT = np.where(diff[:, None, :] >= 0, np.exp(lg[None, :, None] * np.maximum(diff, 0)[:, None, :]), 0.0)
    c["c_decayT"] = dT.reshape(128, 1024).astype(np.float32)
    zeta = np.exp(lg[None, :] * (127.0 - nn[:, None]))
    c["c_zeta8"] = (zeta / 8.0).astype(np.float32)
    xi = np.exp(lg[:, None] * (nn[None, :] + 1.0))
    c["c_xiT"] = np.broadcast_to(xi.reshape(1, 1024), (64, 1024)).astype(np.float32).copy()
    g = np.exp(lg * 128.0)
    c["c_gmat"] = np.broadcast_to(np.repeat(g, 128)[None, :], (64, 1024)).astype(np.float32).copy()
    c["c_tb"] = np.broadcast_to((-1e-30 * np.arange(512, dtype=np.float64))[None, :], (128, 512)).astype(np.float32).copy()
    q = np.arange(128)[:, None]
    kk = np.arange(512)[None, :]
    c["c_cm"] = np.where(kk <= 128 * j + q, 0.0, MASKV).astype(np.float32)
    oh = np.zeros((128, 4), np.float32)
    oh[:, j] = 1.0
    c["c_oh"] = oh
    return c


def _bias_tiles(rel_bias, j):
    key = np.arange(128)[:, None, None]
    kr = np.arange(5)[None, :, None]
    q = np.arange(128)[None, None, :]
    dist = (j + 1 - kr) * 128 + q - key
    bucket = np.where(dist >= 0, _t5_bucket(dist), 31)
    bt = rel_bias[bucket]
    bt = np.transpose(bt, (0, 1, 3, 2))
    return np.ascontiguousarray(bt.reshape(128, 5 * 1024)).astype(np.float32)


_NC_CACHE = {}


def make_in_maps(x, positions, w_in, rel_bias, idx_k_ln_g, idx_k_ln_b, w_attn_branch, w_ret_branch,
                 w_out, ln_mix_g, ln_mix_b, w_up, w_down, ln_ffn_g, ln_ffn_b):
    x = np.asarray(x, np.float32)
    positions = np.asarray(positions, np.int32)
    w = np.asarray(w_in, np.float32)[0]
    rel_bias = np.asarray(rel_bias, np.float32)
    cs = lambda a, b: w[:, a:b]
    w_kv = np.ascontiguousarray(np.concatenate([cs(512, 1024), cs(1024, 1536), cs(1792, 1856), cs(2372, 2884), cs(2884, 3908)], axis=1))
    w_q = np.ascontiguousarray(np.concatenate([cs(0, 512), cs(1536, 1792), cs(1856, 1860)], axis=1))
    w_ro = np.ascontiguousarray(np.concatenate([cs(1860, 2372), cs(3908, 4932)], axis=1))
    w_g = np.ascontiguousarray(cs(4932, 6980))
    rep = lambda v: np.ascontiguousarray(np.broadcast_to(np.asarray(v, np.float32).reshape(1, -1), (128, 1024)))
    shared = {
        "w_kv": w_kv, "w_q": w_q, "w_ro": w_ro, "w_g": w_g,
        "w_a": np.ascontiguousarray(np.asarray(w_attn_branch, np.float32)[0]),
        "w_r": np.ascontiguousarray(np.asarray(w_ret_branch, np.float32)[0]),
        "w_o": np.ascontiguousarray(np.asarray(w_out, np.float32)[0]),
        "w_up": np.ascontiguousarray(np.asarray(w_up, np.float32)[0]),
        "w_dn": np.ascontiguousarray(np.asarray(w_down, np.float32)[0]),
        "lnk_g": np.ascontiguousarray(np.asarray(idx_k_ln_g, np.float32).reshape(64, 1)),
        "lnk_b": np.ascontiguousarray(np.asarray(idx_k_ln_b, np.float32).reshape(64, 1)),
        "ln1_g": rep(ln_mix_g), "ln1_b": rep(ln_mix_b), "ln2_g": rep(ln_ffn_g), "ln2_b": rep(ln_ffn_b),
        "c_c31": np.ascontiguousarray(np.broadcast_to(rel_bias[31][None, :], (128, 8))),
    }
    xTs = [np.ascontiguousarray(x[b].T) for b in range(2)]
    in_maps = []
    own_idx = []
    for c in range(8):
        b, j = c // 4, c % 4
        tok = (np.arange(16)[:, None] * 4 + j) * 128 + np.arange(128)[None, :]
        tok = tok.reshape(-1)
        own_idx.append((b, tok))
        d = dict(shared)
        d["xT"] = xTs[b]
        d["xoT"] = np.ascontiguousarray(xTs[b][:, tok])
        d["xo"] = np.ascontiguousarray(x[b][tok])
        d["posT"] = np.ascontiguousarray(positions[b].reshape(64, 128).T).astype(np.float32)
        d["posoT"] = np.ascontiguousarray(positions[b][tok].reshape(16, 128).T).astype(np.float32)
        d.update(_consts(j))
        d["c_bt"] = _bias_tiles(rel_bias, j)
        in_maps.append(d)
    return in_maps, own_idx


def kernel(**inputs):
    in_maps, own_idx = make_in_maps(**inputs)
    if "nc" not in _NC_CACHE:
        _NC_CACHE["nc"] = build()
    nc = _NC_CACHE["nc"]
    res = run_bass_kernel_spmd(nc, in_maps, core_ids=list(range(8)))
    outp = np.zeros((2, 8192, 1024), np.float32)
    for c in range(8):
        b, tok = own_idx[c]
        outp[b, tok] = res.results[c]["out"]
    return outp
```

```python
import contextlib
import math
import numpy as np
import concourse.bass as bass
import concourse.mybir as mybir
from concourse.bass_utils import run_bass_kernel_spmd

F32 = mybir.dt.float32
BF16 = mybir.dt.bfloat16
I32 = mybir.dt.int32
ALU = mybir.AluOpType
AF = mybir.ActivationFunctionType
AX = mybir.AxisListType

NBIS = 19
LO_INIT = -60.0
MASKV = -30000.0
ALPHA = 2.0 ** 0.25
EPS = 1e-5
PI = math.pi


class Sched:
    CE = ("pe", "act", "dve", "pool")
    ALLE = ("pe", "act", "dve", "pool", "sp")

    def __init__(self, nc, stack, n_dma=(("sp", 24), ("act", 8))):
        self.nc = nc
        self.ops = {e: [] for e in self.ALLE}
        self.cnt = {e: 0 for e in self.CE}
        self.sem = {}
        for e in self.CE:
            self.sem["c_" + e] = stack.enter_context(nc.semaphore("c_" + e))
        self.dma_slots = {}
        self.dma_next = {}
        self.dma_val = {}
        for e, n in n_dma:
            self.dma_slots[e] = []
            for i in range(n):
                k = "d_%s_%d" % (e, i)
                self.sem[k] = stack.enter_context(nc.semaphore(k))
                self.dma_slots[e].append(k)
                self.dma_val[k] = 0
            self.dma_next[e] = 0
        self.buf = {}
        self.waited = {e: {} for e in self.ALLE}
        self.pending = {e: {} for e in self.ALLE}
        self.nops = 0

    def _st(self, k):
        s = self.buf.get(k)
        if s is None:
            s = {"w": None, "r": {}}
            self.buf[k] = s
        return s

    def op(self, eng, fn, r=(), w=(), dma=False):
        waits = dict(self.pending[eng])
        self.pending[eng] = {}

        def add(ev):
            if ev is None:
                return
            k, v = ev
            if waits.get(k, 0) < v:
                waits[k] = v

        for k in r:
            add(self._st(k)["w"])
        for k in w:
            s = self._st(k)
            add(s["w"])
            for ev in s["r"].items():
                add(ev)
        if dma:
            slots = self.dma_slots[eng]
            k = slots[self.dma_next[eng] % len(slots)]
            self.dma_next[eng] += 1
            if self.dma_val[k] > 0:
                add((k, self.dma_val[k]))
            self.dma_val[k] += 16
            ev = (k, self.dma_val[k])
            inc = 16
        else:
            self.cnt[eng] += 1
            ev = ("c_" + eng, self.cnt[eng])
            inc = 1
        wl = []
        own = "c_" + eng
        for k, v in waits.items():
            if k == own and eng == "pe":
                continue
            if self.waited[eng].get(k, 0) >= v:
                continue
            self.waited[eng][k] = v
            wl.append((k, v))
        self.ops[eng].append((wl, fn, ev[0], inc))
        self.nops += 1
        for k in w:
            s = self._st(k)
            s["w"] = ev
            s["r"] = {}
        for k in r:
            if k in w:
                continue
            s = self._st(k)
            if s["r"].get(ev[0], 0) < ev[1]:
                s["r"][ev[0]] = ev[1]
        return ev

    def all_events(self):
        evs = {}
        for e in self.CE:
            if self.cnt[e] > 0:
                evs["c_" + e] = self.cnt[e]
        for k, v in self.dma_val.items():
            if v > 0:
                evs[k] = v
        return evs

    def barrier(self):
        evs = self.all_events()
        for e in self.ALLE:
            for k, v in evs.items():
                if self.pending[e].get(k, 0) < v:
                    self.pending[e][k] = v
        self.buf = {}

    def emit(self):
        nc = self.nc
        final = self.all_events()
        with nc.Block() as block:
            def run(eng_name, e, is_last=False):
                for wl, fn, sk, inc in self.ops[eng_name]:
                    for k, v in wl:
                        e.wait_ge(self.sem[k], v)
                    ins = fn(e)
                    ins.then_inc(self.sem[sk], inc)
                if is_last:
                    for k, v in final.items():
                        e.wait_ge(self.sem[k], v)

            @block.tensor
            def _(e):
                run("pe", e)

            @block.scalar
            def _(e):
                run("act", e)

            @block.vector
            def _(e):
                run("dve", e)

            @block.gpsimd
            def _(e):
                run("pool", e)

            @block.sync
            def _(e):
                run("sp", e, True)


def build(phases="ABCD", debug=False, ng=16, own=True, alltok=True, ostage=99, nb=16, bstage=99, mstart=0, nkcap=999, nobias=False, dummy=0):
    nc = bass.Bass("TRN2", target_bir_lowering=False)

    def din(name, shape, dt=F32):
        return nc.dram_tensor(name, shape, dt, kind="ExternalInput").ap()

    def dscr(name, shape, dt):
        kind = "ExternalOutput" if debug else "Internal"
        return nc.dram_tensor(name, shape, dt, kind=kind).ap()

    xT = din("xT", [1024, 8192])
    xoT = din("xoT", [1024, 2048])
    xo = din("xo", [2048, 1024])
    posT = din("posT", [128, 64], F32)
    posoT = din("posoT", [128, 16], F32)
    w_kv = din("w_kv", [1024, 2624])
    w_q = din("w_q", [1024, 772])
    w_ro = din("w_ro", [1024, 1536])
    w_g = din("w_g", [1024, 2048])
    w_a = din("w_a", [512, 1024])
    w_r = din("w_r", [1024, 1024])
    w_o = din("w_o", [1024, 1024])
    w_up = din("w_up", [1024, 4096])
    w_dn = din("w_dn", [4096, 1024])
    lnk_g = din("lnk_g", [64, 1])
    lnk_b = din("lnk_b", [64, 1])
    ln1_g = din("ln1_g", [128, 1024])
    ln1_b = din("ln1_b", [128, 1024])
    ln2_g = din("ln2_g", [128, 1024])
    ln2_b = din("ln2_b", [128, 1024])
    c_ident = din("c_ident", [128, 128])
    c_invf = din("c_invf", [128, 32])
    c_decayT = din("c_decayT", [128, 1024])
    c_zeta8 = din("c_zeta8", [128, 8])
    c_xiT = din("c_xiT", [64, 1024])
    c_gmat = din("c_gmat", [64, 1024])
    c_tb = din("c_tb", [128, 512])
    c_cm = din("c_cm", [128, 512])
    c_bt = din("c_bt", [128, 5 * 1024])
    c_c31 = din("c_c31", [128, 8])
    c_oh = din("c_oh", [128, 4])
    out = nc.dram_tensor("out", [2048, 1024], F32, kind="ExternalOutput").ap()

    KT = dscr("s_KT", [512, 8192], BF16)
    VV = dscr("s_V", [64, 128, 520], BF16)
    IKT = dscr("s_IKT", [64, 8192], BF16)
    YAT = dscr("s_YAT", [512, 2048], BF16)
    YRT = dscr("s_YRT", [1024, 2048], BF16)
    X1 = dscr("s_X1", [2048, 1024], F32)
    X1T = dscr("s_X1T", [1024, 2048], BF16)

    with contextlib.ExitStack() as st0:
        S = Sched(nc, st0)
        rr = {"cast": 0}

        def SB(stk, name, shape, dt):
            return stk.enter_context(nc.sbuf_tensor(name, shape, dt))

        def PS(stk, name, shape, dt):
            return stk.enter_context(nc.psum_tensor(name, shape, dt))

        def dma(out_ap, in_ap, r, w, eng="sp"):
            S.op(eng, lambda e, o=out_ap, i=in_ap: e.dma_start(out=o, in_=i), r=r, w=w, dma=True)

        def mm(out_ap, lhsT, rhs, start, stop, r, w):
            S.op("pe", lambda e, o=out_ap, l=lhsT, rh=rhs, s0=start, s1=stop: e.matmul(o, lhsT=l, rhs=rh, start=s0, stop=s1),
                 r=r, w=w)

        def tr(out_ap, in_ap, ident_ap, r, w):
            S.op("pe", lambda e, o=out_ap, i=in_ap, d=ident_ap: e.transpose(out=o, in_=i, identity=d), r=r, w=w)

        def act(out_ap, in_ap, func, r, w, bias=None, scale=None, accum=None):
            kw = {}
            if bias is not None:
                kw["bias"] = bias
            if scale is not None:
                kw["scale"] = scale
            if accum is not None:
                kw["accum_out"] = accum
            S.op("act", lambda e, o=out_ap, i=in_ap, f=func, kw=kw: e.activation(out=o, in_=i, func=f, **kw), r=r, w=w)

        def tt(eng, out_ap, in0, in1, op, r, w):
            S.op(eng, lambda e, o=out_ap, a=in0, b=in1, p=op: e.tensor_tensor(out=o, in0=a, in1=b, op=p), r=r, w=w)

        def ts(eng, out_ap, in0, s1, s2, op0, op1, r, w, accum=None):
            kw = {}
            if op1 is not None:
                kw["op1"] = op1
            if accum is not None:
                kw["accum_out"] = accum
            S.op(eng, lambda e, o=out_ap, a=in0, x1=s1, x2=s2, p0=op0, kw=kw:
                 e.tensor_scalar(out=o, in0=a, scalar1=x1, scalar2=x2, op0=p0, **kw), r=r, w=w)

        def stt(eng, out_ap, in0, scalar, in1, op0, op1, r, w):
            S.op(eng, lambda e, o=out_ap, a=in0, s=scalar, b=in1, p0=op0, p1=op1:
                 e.scalar_tensor_tensor(out=o, in0=a, scalar=s, in1=b, op0=p0, op1=p1), r=r, w=w)

        def cp(eng, out_ap, in_ap, r, w):
            if eng == "act":
                act(out_ap, in_ap, AF.Copy, r, w)
            else:
                S.op(eng, lambda e, o=out_ap, i=in_ap: e.tensor_copy(out=o, in_=i), r=r, w=w)

        def red(eng, out_ap, in_ap, op, r, w):
            S.op(eng, lambda e, o=out_ap, i=in_ap, p=op: e.tensor_reduce(out=o, in_=i, axis=AX.X, op=p), r=r, w=w)

        def memset(eng, ap, val, w):
            S.op(eng, lambda e, a=ap, v=val: e.memset(a, v), w=w)

        ident_f = SB(st0, "ident_f", [128, 128], F32)
        ident = SB(st0, "ident", [128, 128], BF16)
        dma(ident_f[:], c_ident[:, :], [], ["ident_f"])
        cp("dve", ident[:], ident_f[:], ["ident_f"], ["ident"])
        epsT = SB(st0, "epsT", [128, 1], F32)
        memset("dve", epsT[:], EPS, ["epsT"])

        def rstd(out_ap, var_ap, r, w):
            act(out_ap, var_ap, AF.Sqrt, list(r) + ["epsT"], list(w), bias=epsT[:, 0:1], scale=1.0)
            S.op("dve", lambda e, o=out_ap: e.reciprocal(out=o, in_=o), r=list(w), w=list(w))

        pb = [PS(st0, "pb%d" % i, [128, 512], F32) for i in range(7)]
        pT = PS(st0, "pT", [128, 1024], BF16)
        pk = ["pb%d" % i for i in range(7)]

        def cast_eng():
            rr["cast"] += 1
            return ("pool", "act")[rr["cast"] % 2]

        def load_w(stk_stage, dst, dst_key, src, nrow_chunks, cols, stg, col0=0):
            for rc in range(nrow_chunks):
                for c0 in range(0, cols, 2048):
                    cw = min(2048, cols - c0)
                    i = rr.setdefault("stg", 0)
                    rr["stg"] = i + 1
                    sk = "stg%d" % (i % len(stg))
                    stile = stg[i % len(stg)]
                    dma(stile[:, 0:cw], src[rc * 128:(rc + 1) * 128, c0:c0 + cw], [], [sk])
                    cp(cast_eng(), dst[:, rc, col0 + c0:col0 + c0 + cw], stile[:, 0:cw], [sk], [dst_key])

        def layer_norm_rows(stk, pre, pre_key, outf, out_key, g_t, b_t, tmp, nm):
            s1 = nm + "_s1"
            st_t = stk["stat"]
            red("dve", st_t[:, 0:1], pre[:], ALU.add, [pre_key], [s1])
            act(tmp[:], pre[:], AF.Square, [pre_key], [nm + "_tmp", s1 + "q"], accum=st_t[:, 1:2])
            ts("dve", st_t[:, 2:3], st_t[:, 0:1], 1.0 / 1024, None, ALU.mult, None, [s1], [s1 + "m"])
            ts("dve", st_t[:, 3:4], st_t[:, 1:2], 1.0 / 1024, None, ALU.mult, None, [s1 + "q"], [s1 + "e"])
            tt("dve", st_t[:, 4:5], st_t[:, 2:3], st_t[:, 2:3], ALU.mult, [s1 + "m"], [s1 + "mm"])
            tt("dve", st_t[:, 5:6], st_t[:, 3:4], st_t[:, 4:5], ALU.subtract, [s1 + "e", s1 + "mm"], [s1 + "v"])
            rstd(st_t[:, 6:7], st_t[:, 5:6], [s1 + "v"], [s1 + "r"])
            ts("dve", pre[:], pre[:], st_t[:, 2:3], st_t[:, 6:7], ALU.subtract, ALU.mult, [pre_key, s1 + "m", s1 + "r"], [pre_key])
            tt("pool", pre[:], pre[:], g_t[:], ALU.mult, [pre_key, "lng"], [pre_key])
            tt("pool", outf[:], pre[:], b_t[:], ALU.add, [pre_key, "lnb"], [out_key])

        if "A" in phases:
            with contextlib.ExitStack() as sa:
                Wkv = SB(sa, "Wkv", [128, 8, 2624], BF16)
                Wro = SB(sa, "Wro", [128, 8, 1536], BF16)
                stg = [SB(sa, "stgA%d" % i, [128, 2048], F32) for i in range(2)]
                xgf2 = [SB(sa, "xgf%d" % i, [128, 8, 512], F32) for i in range(2)]
                xgb = SB(sa, "xgb", [128, 8, 512], BF16)
                posf = SB(sa, "posf", [128, 64], F32)
                posof = SB(sa, "posof", [128, 16], F32)
                invf = SB(sa, "invf", [128, 32], F32)
                ang = SB(sa, "ang", [128, 2, 32], F32)
                CC = SB(sa, "CC", [128, 64], F32)
                SS = SB(sa, "SS", [128, 64], F32)
                kT_sb = SB(sa, "kT_sb", [128, 4, 512], BF16)
                v_sb = SB(sa, "v_sb", [128, 4, 8, 65], BF16)
                ikT_sb = SB(sa, "ikT_sb", [64, 512], BF16)
                ik_f = SB(sa, "ik_f", [128, 64], F32)
                ik_n = SB(sa, "ik_n", [128, 64], BF16)
                ik_junk = SB(sa, "ik_junk", [128, 64], F32)
                stat = SB(sa, "statA", [128, 8], F32)
                lng = SB(sa, "lnkg", [64, 1], F32)
                lnb = SB(sa, "lnkb", [64, 1], F32)
                kz2 = [SB(sa, "kz%d" % i, [128, 512], BF16) for i in range(2)]
                vr2 = [SB(sa, "vr%d" % i, [128, 1024], BF16) for i in range(2)]
                rA = SB(sa, "rA", [128, 512], F32)
                rB = SB(sa, "rB", [128, 512], F32)
                rO = SB(sa, "rO", [128, 512], F32)
                zeta8 = SB(sa, "zeta8", [128, 8], F32)
                R = SB(sa, "R", [64, 1024], F32)
                Rsel = SB(sa, "Rsel", [64, 1024], F32)
                Rselb = SB(sa, "Rselb", [64, 1024], BF16)
                gmat = SB(sa, "gmat", [64, 1024], F32)
                oh = SB(sa, "oh", [128, 4], F32)
                xiT = SB(sa, "xiT", [64, 1024], F32)
                decayT = SB(sa, "decayT", [128, 1024], F32)
                xof = SB(sa, "xof", [128, 8, 128], F32)
                xob = SB(sa, "xob", [128, 8, 128], BF16)
                qrb = SB(sa, "qrb", [128, 512], BF16)
                krb = SB(sa, "krb", [128, 512], BF16)
                qT = SB(sa, "qT", [64, 8, 128], BF16)
                qxiT = SB(sa, "qxiT", [64, 8, 128], BF16)
                kTo = SB(sa, "kTo", [64, 8, 128], BF16)
                vro = SB(sa, "vro", [128, 1024], BF16)
                sgr = SB(sa, "sgr", [128, 1024], F32)
                Dm = SB(sa, "Dm", [128, 1024], BF16)
                osb = SB(sa, "osb", [128, 1024], F32)
                osq = SB(sa, "osq", [128, 1024], F32)
                hst = SB(sa, "hst", [128, 8, 8], F32)
                yrb = SB(sa, "yrb", [128, 1024], BF16)
                yrT = SB(sa, "yrT", [128, 8, 128], BF16)

                load_w(sa, Wkv, "Wkv", w_kv, 8, 2624, stg)
                load_w(sa, Wro, "Wro", w_ro, 8, 1536, stg)
                dma(posf[:], posT[:, :], [], ["posf"])
                dma(posof[:], posoT[:, :], [], ["posof"])
                dma(invf[:], c_invf[:, :], [], ["invf"])
                dma(lng[:], lnk_g[:, :], [], ["lnkg"])
                dma(lnb[:], lnk_b[:, :], [], ["lnkb"])
                dma(zeta8[:], c_zeta8[:, :], [], ["zeta8"])
                dma(gmat[:], c_gmat[:, :], [], ["gmat"])
                dma(oh[:], c_oh[:, :], [], ["oh"])
                dma(xiT[:], c_xiT[:, :], [], ["xiT"])
                dma(decayT[:], c_decayT[:, :], [], ["decayT"])
                memset("pool", R[:], 0.0, ["R"])
                memset("pool", v_sb[:], 1.0, ["v_sb"])

                def cos_sin(pcol, pkey):
                    MG = 12582912.0
                    ts("dve", ang[:, 0, :], invf[:], pcol, None, ALU.mult, None, ["invf", pkey], ["ang"])
                    ts("dve", ang[:, 1, :], ang[:, 0, :], 0.5 * PI, None, ALU.add, None, ["ang"], ["ang"])
                    ts("dve", angk[:], ang[:], 1.0 / (2 * PI), MG, ALU.mult, ALU.add, ["ang"], ["angk"])
                    ts("dve", angk[:], angk[:], -MG, None, ALU.add, None, ["angk"], ["angk"])
                    stt("dve", ang[:], angk[:], -2 * PI, ang[:], ALU.mult, ALU.add, ["angk", "ang"], ["ang"])
                    ts("dve", ang[:], ang[:], 3.1415925, -3.1415925, ALU.min, ALU.max, ["ang"], ["ang"])
                    act(SS[:, 0:32], ang[:, 0, :], AF.Sin, ["ang"], ["SS"])
                    act(SS[:, 32:64], ang[:, 0, :], AF.Sin, ["ang"], ["SS"], scale=-1.0)
                    act(CC[:, 0:32], ang[:, 1, :], AF.Sin, ["ang"], ["CC"])
                    act(CC[:, 32:64], ang[:, 1, :], AF.Sin, ["ang"], ["CC"])

                angk = SB(sa, "angk", [128, 2, 32], F32)
                qf = SB(sa, "qf", [64, 1024], F32)

                def rope(src_ps, src_key, dst):
                    s3 = src_ps.rearrange("p (h d) -> p h d", h=8)
                    a3 = rA[:].rearrange("p (h d) -> p h d", h=8)
                    b3 = rB[:].rearrange("p (h d) -> p h d", h=8)
                    o3 = dst[:].rearrange("p (h d) -> p h d", h=8)
                    ccb = CC[:].unsqueeze(1).to_broadcast([128, 8, 64])
                    ssb = SS[:].unsqueeze(1).to_broadcast([128, 8, 64])
                    tt("dve", a3, s3, ccb, ALU.mult, [src_key, "CC"], ["rA"])
                    tt("dve", b3, s3, ssb, ALU.mult, [src_key, "SS"], ["rB"])
                    tt("pool", o3[:, :, 0:32], a3[:, :, 0:32], b3[:, :, 32:64], ALU.add, ["rA", "rB"], ["rO"])
                    tt("pool", o3[:, :, 32:64], a3[:, :, 32:64], b3[:, :, 0:32], ALU.add, ["rA", "rB"], ["rO"])

                xT3 = xT.rearrange("(c p) t -> p c t", p=128)
                xoT3 = xoT.rearrange("(c p) t -> p c t", p=128)
                KT3 = KT.rearrange("(c p) t -> p c t", p=128)
                YRT3 = YRT.rearrange("(c p) t -> p c t", p=128)
                VV3 = VV.rearrange("t p f -> p t f")

                for m in range(ng):
                    if not alltok:
                        break
                    xgf = xgf2[m % 2]
                    xgk = "xgf%d" % (m % 2)
                    if m == 0:
                        dma(xgf[:], xT3[:, :, 0:512], [], [xgk])
                    cp("pool", xgb[:, 0:4, :], xgf[:, 0:4, :], [xgk], ["xgb"])
                    cp("act", xgb[:, 4:8, :], xgf[:, 4:8, :], [xgk], ["xgb"])
                    if m + 1 < ng:
                        dma(xgf2[(m + 1) % 2][:], xT3[:, :, (m + 1) * 512:(m + 2) * 512], [], ["xgf%d" % ((m + 1) % 2)])
                    pend = [None]

                    def flushU():
                        if pend[0] is None:
                            return
                        ps = pend[0]
                        pend[0] = None
                        kzp, vrp = kz2[ps], vr2[ps]
                        for h in range(8):
                            mm(pb[5 + h // 4][0:64, (h % 4) * 128:(h % 4 + 1) * 128], kzp[:, h * 64:(h + 1) * 64],
                               vrp[:, h * 128:(h + 1) * 128], True, True, ["kz%d" % ps, "vr%d" % ps], [pk[5 + h // 4]])
                        tt("pool", R[:], R[:], gmat[:], ALU.mult, ["R", "gmat"], ["R"])
                        tt("dve", R[:, 0:512], R[:, 0:512], pb[5][0:64, :], ALU.add, ["R", pk[5]], ["R"])
                        tt("dve", R[:, 512:1024], R[:, 512:1024], pb[6][0:64, :], ALU.add, ["R", pk[6]], ["R"])
                    for fc in range(4):
                        for dc in range(8):
                            mm(pb[0][:], Wkv[:, dc, fc * 128:(fc + 1) * 128], xgb[:, dc, :], dc == 0, dc == 7,
                               ["Wkv", "xgb"], [pk[0]])
                        cp("act", kT_sb[:, fc, :], pb[0][:], [pk[0]], ["kT_sb"])
                    dma(KT3[:, :, m * 512:(m + 1) * 512], kT_sb[:], ["kT_sb"], ["KT"])
                    for i in range(4):
                        t = 4 * m + i
                        xs = slice(i * 128, (i + 1) * 128)
                        for dc in range(8):
                            mm(pb[0][:, 0:64], xgb[:, dc, xs], Wkv[:, dc, 1024:1088], dc == 0, dc == 7,
                               ["Wkv", "xgb"], [pk[0]])
                        cp("dve", ik_f[:], pb[0][:, 0:64], [pk[0]], ["ik_f"])
                        red("dve", stat[:, 0:1], ik_f[:], ALU.add, ["ik_f"], ["st0"])
                        act(ik_junk[:], ik_f[:], AF.Square, ["ik_f"], ["ik_junk", "st1"], accum=stat[:, 1:2])
                        ts("dve", stat[:, 2:3], stat[:, 0:1], 1.0 / 64, None, ALU.mult, None, ["st0"], ["st2"])
                        ts("dve", stat[:, 3:4], stat[:, 1:2], 1.0 / 64, None, ALU.mult, None, ["st1"], ["st3"])
                        tt("dve", stat[:, 4:5], stat[:, 2:3], stat[:, 2:3], ALU.mult, ["st2"], ["st4"])
                        tt("dve", stat[:, 5:6], stat[:, 3:4], stat[:, 4:5], ALU.subtract, ["st3", "st4"], ["st5"])
                        rstd(stat[:, 6:7], stat[:, 5:6], ["st5"], ["st6"])
                        ts("dve", ik_n[:], ik_f[:], stat[:, 2:3], stat[:, 6:7], ALU.subtract, ALU.mult,
                           ["ik_f", "st2", "st6"], ["ik_n"])
                        tr(pT[0:64, 0:128], ik_n[:], ident[:], ["ik_n", "ident"], ["pT"])
                        act(ikT_sb[:, xs], pT[0:64, 0:128], AF.Identity, ["pT", "lnkg", "lnkb"], ["ikT_sb"],
                            bias=lnb[:, 0:1], scale=lng[:, 0:1])
                        for dc in range(8):
                            mm(pb[1][:], xgb[:, dc, xs], Wkv[:, dc, 512:1024], dc == 0, dc == 7, ["Wkv", "xgb"], [pk[1]])
                        cp("act", v_sb[:, i, :, 0:64], pb[1][:].rearrange("p (h d) -> p h d", h=8), [pk[1]], ["v_sb"])
                        for dc in range(8):
                            mm(pb[2][:], xgb[:, dc, xs], Wkv[:, dc, 1088:1600], dc == 0, dc == 7, ["Wkv", "xgb"], [pk[2]])
                        for hf in range(2):
                            for dc in range(8):
                                mm(pb[3 + hf][:], xgb[:, dc, xs], Wkv[:, dc, 1600 + hf * 512:2112 + hf * 512],
                                   dc == 0, dc == 7, ["Wkv", "xgb"], [pk[3 + hf]])
                            cp("act", vr2[t % 2][:, hf * 512:(hf + 1) * 512], pb[3 + hf][:], [pk[3 + hf]], ["vr%d" % (t % 2)])
                        flushU()
                        cos_sin(posf[:, t:t + 1], "posf")
                        rope(pb[2][:], pk[2], rO)
                        tt("dve", kz2[t % 2][:].rearrange("p (h d) -> p h d", h=8), rO[:].rearrange("p (h d) -> p h d", h=8),
                           zeta8[:].unsqueeze(2).to_broadcast([128, 8, 64]), ALU.mult, ["rO", "zeta8"], ["kz%d" % (t % 2)])
                        if i == 0:
                            ts("dve", Rsel[:], R[:], oh[0:64, 0:1], None, ALU.mult, None, ["R", "oh"], ["Rsel"])
                        else:
                            stt("dve", Rsel[:], R[:], oh[0:64, i:i + 1], Rsel[:], ALU.mult, ALU.add, ["R", "oh", "Rsel"], ["Rsel"])
                        pend[0] = t % 2
                    flushU()
                    dma(VV3[:, 4 * m:4 * m + 4, :], v_sb[:].rearrange("p t h d -> p t (h d)"), ["v_sb"], ["VV"])
                    dma(IKT[:, m * 512:(m + 1) * 512], ikT_sb[:], ["ikT_sb"], ["IKT"])
                    if not own:
                        continue

                    os_ = slice(m * 128, (m + 1) * 128)
                    dma(xof[:], xoT3[:, :, os_], [], ["xof"])
                    cp("pool", xob[:], xof[:], ["xof"], ["xob"])
                    cp("act", Rselb[:], Rsel[:], ["Rsel"], ["Rselb"])
                    cos_sin(posof[:, m:m + 1], "posof")
                    for dc in range(8):
                        mm(pb[1][:], xob[:, dc, :], Wro[:, dc, 0:512], dc == 0, dc == 7, ["Wro", "xob"], [pk[1]])
                    rope(pb[1][:], pk[1], rO)
                    cp("act", qrb[:], rO[:], ["rO"], ["qrb"])
                    for dc in range(8):
                        mm(pb[2][:], xob[:, dc, :], Wkv[:, dc, 1088:1600], dc == 0, dc == 7, ["Wkv", "xob"], [pk[2]])
                    rope(pb[2][:], pk[2], rO)
                    S.op("act", lambda e, o=krb[:], i=rO[:]: e.mul(out=o, in_=i, mul=0.125), r=["rO"], w=["krb"])
                    for hf in range(2):
                        for dc in range(8):
                            mm(pb[3 + hf][:], xob[:, dc, :], Wkv[:, dc, 1600 + hf * 512:2112 + hf * 512],
                               dc == 0, dc == 7, ["Wkv", "xob"], [pk[3 + hf]])
                        cp("act", vro[:, hf * 512:(hf + 1) * 512], pb[3 + hf][:], [pk[3 + hf]], ["vro"])
                    for hf in range(2):
                        for dc in range(8):
                            mm(pb[5 + hf][:], xob[:, dc, :], Wro[:, dc, 512 + hf * 512:1024 + hf * 512],
                               dc == 0, dc == 7, ["Wro", "xob"], [pk[5 + hf]])
                        act(sgr[:, hf * 512:(hf + 1) * 512], pb[5 + hf][:], AF.Sigmoid, [pk[5 + hf]], ["sgr"])
                        tt("dve", sgr[:, hf * 512:(hf + 1) * 512], sgr[:, hf * 512:(hf + 1) * 512], pb[5 + hf][:], ALU.mult,
                           ["sgr", pk[5 + hf]], ["sgr"])
                    if ostage < 1:
                        continue
                    for h in range(8):
                        tr(pT[0:64, h * 128:(h + 1) * 128], qrb[:, h * 64:(h + 1) * 64], ident[:], ["qrb", "ident"], ["pT"])
                    cp("act", qf[:], pT[0:64, :], ["pT"], ["qf"])
                    cp("pool", qT[:].rearrange("p h n -> p (h n)"), qf[:], ["qf"], ["qT"])
                    tt("dve", qxiT[:].rearrange("p h n -> p (h n)"), qf[:], xiT[:], ALU.mult, ["qf", "xiT"], ["qxiT"])
                    for h in range(8):
                        tr(pT[0:64, h * 128:(h + 1) * 128], krb[:, h * 64:(h + 1) * 64], ident[:], ["krb", "ident"], ["pT"])
                    cp("act", kTo[:].rearrange("p h n -> p (h n)"), pT[0:64, :], ["pT"], ["kTo"])
                    if ostage < 2:
                        continue
                    for h in range(8):
                        mm(pb[3 + h // 4][:, (h % 4) * 128:(h % 4 + 1) * 128], kTo[:, h, :], qT[:, h, :], True, True,
                           ["kTo", "qT"], [pk[3 + h // 4]])
                    tt("dve", Dm[:, 0:512], pb[3][:], decayT[:, 0:512], ALU.mult, [pk[3], "decayT"], ["Dm"])
                    tt("dve", Dm[:, 512:1024], pb[4][:], decayT[:, 512:1024], ALU.mult, [pk[4], "decayT"], ["Dm"])
                    for h in range(8):
                        o_ap = pb[5 + h // 4][:, (h % 4) * 128:(h % 4 + 1) * 128]
                        mm(o_ap, Dm[:, h * 128:(h + 1) * 128], vro[:, h * 128:(h + 1) * 128], True, False,
                           ["Dm", "vro"], [pk[5 + h // 4]])
                        mm(o_ap, qxiT[:, h, :], Rselb[:, h * 128:(h + 1) * 128], False, True,
                           ["qxiT", "Rselb"], [pk[5 + h // 4]])
                    if ostage < 3:
                        continue
                    cp("act", osb[:, 0:512], pb[5][:], [pk[5]], ["osb"])
                    cp("act", osb[:, 512:1024], pb[6][:], [pk[6]], ["osb"])
                    o3 = osb[:].rearrange("p (h v) -> p h v", h=8)
                    q3 = osq[:].rearrange("p (h v) -> p h v", h=8)
                    red("dve", hst[:, 0, :], o3, ALU.add, ["osb"], ["h0"])
                    tt("pool", osq[:], osb[:], osb[:], ALU.mult, ["osb"], ["osq"])
                    red("dve", hst[:, 1, :], q3, ALU.add, ["osq"], ["h1"])
                    ts("dve", hst[:, 2, :], hst[:, 0, :], 1.0 / 128, None, ALU.mult, None, ["h0"], ["h2"])
                    ts("dve", hst[:, 3, :], hst[:, 1, :], 1.0 / 128, None, ALU.mult, None, ["h1"], ["h3"])
                    tt("dve", hst[:, 4, :], hst[:, 2, :], hst[:, 2, :], ALU.mult, ["h2"], ["h4"])
                    tt("dve", hst[:, 5, :], hst[:, 3, :], hst[:, 4, :], ALU.subtract, ["h3", "h4"], ["h5"])
                    rstd(hst[:, 6, :], hst[:, 5, :], ["h5"], ["h6"])
                    tt("dve", q3, o3, hst[:, 2, :].unsqueeze(2).to_broadcast([128, 8, 128]), ALU.subtract, ["osb", "h2", "osq"], ["osq"])
                    tt("dve", q3, q3, hst[:, 6, :].unsqueeze(2).to_broadcast([128, 8, 128]), ALU.mult, ["osq", "h6"], ["osq"])
                    if ostage < 4:
                        continue
                    tt("pool", yrb[:], osq[:], sgr[:], ALU.mult, ["osq", "sgr"], ["yrb"])
                    for fc in range(8):
                        tr(pT[:, fc * 128:(fc + 1) * 128], yrb[:, fc * 128:(fc + 1) * 128], ident[:], ["yrb", "ident"], ["pT"])
                    cp("act", yrT[:].rearrange("p c n -> p (c n)"), pT[:, :], ["pT"], ["yrT"])
                    dma(YRT3[:, :, os_], yrT[:], ["yrT"], ["YRT"])
            S.barrier()

        if "B" in phases:
            with contextlib.ExitStack() as sbk:
                KTs = SB(sbk, "KTs", [128, 4, 8192], BF16)
                Vc = [SB(sbk, "Vc%d" % i, [128, 4, 520], BF16) for i in range(2)]
                IKs = SB(sbk, "IKs", [64, 8192], BF16)
                score = SB(sbk, "score", [128, 8192], F32)
                Wq = SB(sbk, "Wq", [128, 8, 772], BF16)
                stg = [SB(sbk, "stgB%d" % i, [128, 1024], F32) for i in range(2)]
                xof = SB(sbk, "xofB", [128, 8, 128], F32)
                xob = SB(sbk, "xobB", [128, 8, 128], BF16)
                qaT = SB(sbk, "qaT", [128, 4, 2, 128], BF16)
                iqT = SB(sbk, "iqT", [64, 4, 128], BF16)
                iwf = SB(sbk, "iwf", [128, 4], F32)
                aw = SB(sbk, "aw", [128, 4], F32)
                sgn = SB(sbk, "sgn", [128, 4], F32)
                rl = [SB(sbk, "rl%d" % i, [128, 512], F32) for i in range(2)]
                tb = SB(sbk, "tb", [128, 512], F32)
                cm = SB(sbk, "cm", [128, 512], F32)
                btf = SB(sbk, "btf", [128, 1024], F32)
                bts = SB(sbk, "bts", [128, 5, 1024], BF16)
                c31 = SB(sbk, "c31", [128, 8], F32)
                bs = SB(sbk, "bs", [128, 64], F32)
                junk = SB(sbk, "junkB", [128, 3712], BF16)
                junkA = SB(sbk, "junkA", [128, 4608], BF16)
                mk = [SB(sbk, "mk%d" % i, [128, 128], BF16) for i in range(2)]
                mkT = [SB(sbk, "mkT%d" % i, [128, 128], BF16) for i in range(2)]
                Eb = [SB(sbk, "Eb%d" % i, [128, 1024], BF16) for i in range(2)]
                Pb = [SB(sbk, "Pb%d" % i, [128, 1024], BF16) for i in range(2)]
                osb = SB(sbk, "osbB", [128, 520], F32)
                rs = SB(sbk, "rsB", [128, 8], F32)
                yab = SB(sbk, "yab", [128, 512], BF16)
                yaT = SB(sbk, "yaT", [128, 4, 128], BF16)

                KT3 = KT.rearrange("(c p) t -> p c t", p=128)
                for c4 in range(4):
                    dma(KTs[:, c4, :], KT3[:, c4, :], ["KT"], ["KTs"])
                VV3 = VV.rearrange("t p f -> p t f")
                dma(IKs[:], IKT[:, :], ["IKT"], ["IKs"])
                load_w(sbk, Wq, "Wq", w_q, 8, 772, stg)
                memset("pool", qaT[:], 0.0, ["qaT"])
                zr = SB(sbk, "zr", [128, 512], BF16)
                memset("pool", zr[:], 0.0, ["zr"])
                dma(tb[:], c_tb[:, :], [], ["tb"])
                dma(cm[:], c_cm[:, :], [], ["cm"])
                dma(c31[:], c_c31[:, :], [], ["c31"])
                for kr in range(5):
                    dma(btf[:], c_bt[:, kr * 1024:(kr + 1) * 1024], [], ["btf"])
                    tt("dve", bts[:, kr, :].rearrange("p (h q) -> p h q", h=8), btf[:].rearrange("p (h q) -> p h q", h=8),
                       c31[:].unsqueeze(2).to_broadcast([128, 8, 128]), ALU.subtract, ["btf", "c31"], ["bts"])
                xoT3 = xoT.rearrange("(c p) t -> p c t", p=128)
                YAT3 = YAT.rearrange("(c p) t -> p c t", p=128)
                vcount = [0]
                LO, W0, MID, WK, VV_, PW = 0, 1, 2, 3, 4, 5

                for m in range(mstart, nb):
                    os_ = slice(m * 128, (m + 1) * 128)
                    n5 = m + 1
                    nk = min(4 * (m + 1), nkcap)
                    n = 512 * (m + 1)
                    dma(xof[:], xoT3[:, :, os_], [], ["xof"])
                    cp("pool", xob[:], xof[:], ["xof"], ["xob"])
                    for fc in range(4):
                        for dc in range(8):
                            mm(pb[0][:, fc * 128:(fc + 1) * 128], Wq[:, dc, fc * 128:(fc + 1) * 128], xob[:, dc, :],
                               dc == 0, dc == 7, ["Wq", "xob"], [pk[0]])
                    p03 = pb[0][:].rearrange("p (c n) -> p c n", c=4)
                    S.op("act", lambda e, o=qaT[0:64, :, 0, :], i=p03[0:64, :, :]: e.mul(out=o, in_=i, mul=0.125),
                         r=[pk[0]], w=["qaT"])
                    S.op("act", lambda e, o=qaT[64:128, :, 1, :], i=p03[64:128, :, :]: e.mul(out=o, in_=i, mul=0.125),
                         r=[pk[0]], w=["qaT"])
                    for h in range(4):
                        for dc in range(8):
                            mm(pb[2][0:64, h * 128:(h + 1) * 128], Wq[:, dc, 512 + h * 64:512 + (h + 1) * 64], xob[:, dc, :],
                               dc == 0, dc == 7, ["Wq", "xob"], [pk[2]])
                    cp("act", iqT[:].rearrange("p h n -> p (h n)"), pb[2][0:64, :], [pk[2]], ["iqT"])
                    for dc in range(8):
                        mm(pb[3][:, 0:4], xob[:, dc, :], Wq[:, dc, 768:772], dc == 0, dc == 7, ["Wq", "xob"], [pk[3]])
                    cp("dve", iwf[:], pb[3][:, 0:4], [pk[3]], ["iwf"])
                    act(aw[:], iwf[:], AF.Abs, ["iwf"], ["aw"], scale=1.0 / 16)
                    act(sgn[:], iwf[:], AF.Sign, ["iwf"], ["sgn"])
                    if bstage < 1:
                        continue
                    for c5 in range(n5):
                        ks = slice(c5 * 512, (c5 + 1) * 512)
                        for h in range(4):
                            mm(pb[h][:], iqT[:, h, :], IKs[:, ks], True, True, ["iqT", "IKs"], [pk[h]])
                        ts("pool", score[:, ks], tb[:], -1e-30 * 512 * c5, None, ALU.add, None, ["tb"], ["score"])
                        if c5 == n5 - 1:
                            tt("pool", score[:, ks], score[:, ks], cm[:], ALU.add, ["score", "cm"], ["score"])
                        for h in range(4):
                            rk = "rl%d" % (h % 2)
                            act(rl[h % 2][:], pb[h][:], AF.Relu, [pk[h], "aw"], [rk], scale=aw[:, h:h + 1])
                            stt("dve", score[:, ks], rl[h % 2][:], sgn[:, h:h + 1], score[:, ks], ALU.mult, ALU.add,
                                [rk, "sgn", "score"], ["score"])
                    if bstage < 2:
                        continue
                    nd = max(128, ((45 * n // 100) // 128) * 128)
                    na = n - nd
                    red("dve", bs[:, W0:W0 + 1], score[:, 0:n], ALU.max, ["score"], ["bs"])
                    ts("dve", bs[:, W0:W0 + 1], bs[:, W0:W0 + 1], 1e-3 - LO_INIT, None, ALU.add, None, ["bs"], ["bs"])
                    memset("dve", bs[:, LO:LO + 1], LO_INIT, ["bs"])
                    memset("dve", bs[:, 8:64], 0.0, ["bs", "bsa"])
                    for it in range(NBIS):
                        ts("dve", bs[:, WK:WK + 1], bs[:, W0:W0 + 1], 0.5 ** (it + 1), None, ALU.mult, None, ["bs"], ["bs"])
                        tt("dve", bs[:, MID:MID + 1], bs[:, LO:LO + 1], bs[:, WK:WK + 1], ALU.add, ["bs"], ["bs", "mid"])
                        ts("dve", junk[:, 0:nd], score[:, 0:nd], bs[:, MID:MID + 1], 0.0, ALU.is_ge, ALU.add,
                           ["score", "mid"], ["junk", "bs"], accum=bs[:, 8 + it:9 + it])
                        act(junkA[:, 0:na], score[:, nd:n], AF.Sign, ["score", "mid"], ["junkA", "bsa"],
                            bias=bs[:, MID:MID + 1], scale=-1.0, accum=bs[:, 36 + it:37 + it])
                        stt("dve", bs[:, VV_:VV_ + 1], bs[:, 8 + it:9 + it], 2.0, bs[:, 36 + it:37 + it], ALU.mult, ALU.subtract,
                            ["bs", "bsa"], ["bs"])
                        ts("dve", bs[:, PW:PW + 1], bs[:, VV_:VV_ + 1], 511.5 - na, bs[:, WK:WK + 1], ALU.is_ge, ALU.mult,
                           ["bs"], ["bs"])
                        tt("dve", bs[:, LO:LO + 1], bs[:, LO:LO + 1], bs[:, PW:PW + 1], ALU.add, ["bs", "mid"], ["bs"])
                    if bstage < 3:
                        continue
                    for hh in range(2):
                        mm(pb[4 + hh][:], zr[:, 0:128], zr[:], True, True, ["zr"], [pk[4 + hh]])
                    for kc in range(nk):
                        sl = kc % 2
                        kcs = slice(kc * 128, (kc + 1) * 128)
                        pS = (pb[2 * sl], pb[2 * sl + 1])
                        pSk = (pk[2 * sl], pk[2 * sl + 1])
                        kr = kc - (4 * m - 1)
                        near = 0 <= kr <= 4 and kc >= 0 and not nobias
                        if kc % 4 == 0:
                            vs_ = vcount[0] % 2
                            vcount[0] += 1
                            dma(Vc[vs_][:], VV3[:, kc:kc + 4, :], ["VV"], ["Vc%d" % vs_])
                        Vv = Vc[vs_][:].rearrange("p t (h d) -> p t h d", h=8)
                        for h in range(8):
                            mm(pS[h // 4][:, (h % 4) * 128:(h % 4 + 1) * 128], KTs[:, h // 2, kcs], qaT[:, h // 2, h % 2, :], True, not near,
                               ["KTs", "qaT"], [pSk[h // 4]])
                            if near:
                                mm(pS[h // 4][:, (h % 4) * 128:(h % 4 + 1) * 128], ident[:], bts[:, kr, h * 128:(h + 1) * 128], False, True,
                                   ["ident", "bts"], [pSk[h // 4]])
                            elif dummy == 1:
                                mm(pb[6][:, 0:128], ident[:], bts[:, 0, h * 128:(h + 1) * 128], True, True,
                                   ["ident", "bts"], [pk[6]])
                        ts("dve", mk[sl][:], score[:, kcs], bs[:, LO:LO + 1], None, ALU.is_ge, None, ["score", "bs"], ["mk%d" % sl])
                        tr(pT[:, sl * 128:(sl + 1) * 128], mk[sl][:], ident[:], ["mk%d" % sl, "ident"], ["pT%d" % sl])
                        for hh in range(2):
                            act(Eb[sl][:, hh * 512:(hh + 1) * 512], pS[hh][:], AF.Exp, [pSk[hh]], ["Eb%d" % sl])
                        cp("act", mkT[sl][:], pT[:, sl * 128:(sl + 1) * 128], ["pT%d" % sl], ["mkT%d" % sl])
                        tt("dve", Pb[sl][:].rearrange("p (h q) -> p h q", h=8), Eb[sl][:].rearrange("p (h q) -> p h q", h=8),
                           mkT[sl][:].unsqueeze(1).to_broadcast([128, 8, 128]), ALU.mult,
                           ["Eb%d" % sl, "mkT%d" % sl], ["Pb%d" % sl])
                        for h in range(8):
                            mm(pb[4 + h // 4][:, (h % 4) * 128:(h % 4) * 128 + 65], Pb[sl][:, h * 128:(h + 1) * 128],
                               Vv[:, kc % 4, h, :], False, kc == nk - 1, ["Pb%d" % sl, "Vc%d" % vs_], [pk[4 + h // 4]])
                    if bstage < 4:
                        continue
                    cp("act", osb[:, 0:260].rearrange("p (h d) -> p h d", h=4), pb[4][:].rearrange("p (h d) -> p h d", h=4)[:, :, 0:65],
                       [pk[4]], ["osbB"])
                    cp("act", osb[:, 260:520].rearrange("p (h d) -> p h d", h=4), pb[5][:].rearrange("p (h d) -> p h d", h=4)[:, :, 0:65],
                       [pk[5]], ["osbB"])
                    o4 = osb[:].rearrange("p (h d) -> p h d", h=8)
                    S.op("dve", lambda e, o=rs[:], i=o4[:, :, 64]: e.reciprocal(out=o, in_=i), r=["osbB"], w=["rsB"])
                    tt("dve", yab[:].rearrange("p (h d) -> p h d", h=8), o4[:, :, 0:64],
                       rs[:].unsqueeze(2).to_broadcast([128, 8, 64]), ALU.mult, ["osbB", "rsB"], ["yab"])
                    for fc in range(4):
                        tr(pT[:, 256 + fc * 128:256 + (fc + 1) * 128], yab[:, fc * 128:(fc + 1) * 128], ident[:],
                           ["yab", "ident"], ["pTy"])
                    cp("act", yaT[:].rearrange("p c n -> p (c n)"), pT[:, 256:768], ["pTy"], ["yaT"])
                    dma(YAT3[:, :, os_], yaT[:], ["yaT"], ["YAT"])
            S.barrier()

        if "C" in phases:
            with contextlib.ExitStack() as sc:
                Wg = SB(sc, "Wg", [128, 8, 2048], BF16)
                Wa = SB(sc, "Wa", [128, 4, 1024], BF16)
                Wr = SB(sc, "Wr", [128, 8, 1024], BF16)
                Wo = SB(sc, "Wo", [128, 8, 1024], BF16)
                stg = [SB(sc, "stgC%d" % i, [128, 2048], F32) for i in range(2)]
                g1 = SB(sc, "g1", [128, 1024], F32)
                b1 = SB(sc, "b1", [128, 1024], F32)
                xof = SB(sc, "xofC", [128, 8, 128], F32)
                xob = SB(sc, "xobC", [128, 8, 128], BF16)
                yaT = SB(sc, "yaTC", [128, 4, 128], BF16)
                yrT = SB(sc, "yrTC", [128, 8, 128], BF16)
                sg = SB(sc, "sg", [128, 2048], F32)
                hf_ = SB(sc, "hf", [128, 1024], F32)
                h2 = SB(sc, "h2", [128, 1024], F32)
                hb = SB(sc, "hb", [128, 1024], BF16)
                hT = SB(sc, "hT", [128, 8, 128], BF16)
                xres = SB(sc, "xres", [128, 1024], F32)
                pre = SB(sc, "pre", [128, 1024], F32)
                tmp = SB(sc, "tmpC", [128, 1024], F32)
                x1f = SB(sc, "x1f", [128, 1024], F32)
                x1b = SB(sc, "x1b", [128, 1024], BF16)
                x1T = SB(sc, "x1T", [128, 8, 128], BF16)
                stat = SB(sc, "statC", [128, 8], F32)
                load_w(sc, Wg, "Wg", w_g, 8, 2048, stg)
                load_w(sc, Wa, "Wa", w_a, 4, 1024, stg)
                load_w(sc, Wr, "Wr", w_r, 8, 1024, stg)
                load_w(sc, Wo, "Wo", w_o, 8, 1024, stg)
                dma(g1[:], ln1_g[:, :], [], ["lng"])
                dma(b1[:], ln1_b[:, :], [], ["lnb"])
                xoT3 = xoT.rearrange("(c p) t -> p c t", p=128)
                YAT3 = YAT.rearrange("(c p) t -> p c t", p=128)
                YRT3 = YRT.rearrange("(c p) t -> p c t", p=128)
                X1T3 = X1T.rearrange("(c p) t -> p c t", p=128)
                for m in range(16):
                    os_ = slice(m * 128, (m + 1) * 128)
                    dma(xof[:], xoT3[:, :, os_], [], ["xof"])
                    cp("pool", xob[:], xof[:], ["xof"], ["xob"])
                    dma(yaT[:], YAT3[:, :, os_], ["YAT"], ["yaT"])
                    dma(yrT[:], YRT3[:, :, os_], ["YRT"], ["yrT"])
                    dma(xres[:], xo[os_, :], [], ["xres"])
                    for q4 in range(4):
                        for dc in range(8):
                            mm(pb[q4][:], xob[:, dc, :], Wg[:, dc, q4 * 512:(q4 + 1) * 512], dc == 0, dc == 7,
                               ["Wg", "xob"], [pk[q4]])
                        act(sg[:, q4 * 512:(q4 + 1) * 512], pb[q4][:], AF.Sigmoid, [pk[q4]], ["sg"])
                    for hh in range(2):
                        for fc in range(4):
                            mm(pb[4 + hh][:], yaT[:, fc, :], Wa[:, fc, hh * 512:(hh + 1) * 512], fc == 0, fc == 3,
                               ["Wa", "yaT"], [pk[4 + hh]])
                        tt("dve", hf_[:, hh * 512:(hh + 1) * 512], pb[4 + hh][:], sg[:, hh * 512:(hh + 1) * 512], ALU.mult,
                           [pk[4 + hh], "sg"], ["hf"])
                    for hh in range(2):
                        for fc in range(8):
                            mm(pb[hh][:], yrT[:, fc, :], Wr[:, fc, hh * 512:(hh + 1) * 512], fc == 0, fc == 7,
                               ["Wr", "yrT"], [pk[hh]])
                        tt("dve", h2[:, hh * 512:(hh + 1) * 512], pb[hh][:], sg[:, 1024 + hh * 512:1024 + (hh + 1) * 512],
                           ALU.mult, [pk[hh], "sg"], ["h2"])
                    tt("pool", hb[:], hf_[:], h2[:], ALU.add, ["hf", "h2"], ["hb"])
                    for fc in range(8):
                        tr(pT[:, fc * 128:(fc + 1) * 128], hb[:, fc * 128:(fc + 1) * 128], ident[:], ["hb", "ident"], ["pT"])
                    cp("act", hT[:].rearrange("p c n -> p (c n)"), pT[:, :], ["pT"], ["hT"])
                    for hh in range(2):
                        for fc in range(8):
                            mm(pb[2 + hh][:], hT[:, fc, :], Wo[:, fc, hh * 512:(hh + 1) * 512], fc == 0, fc == 7,
                               ["Wo", "hT"], [pk[2 + hh]])
                        stt("dve", pre[:, hh * 512:(hh + 1) * 512], xres[:, hh * 512:(hh + 1) * 512], ALPHA, pb[2 + hh][:],
                            ALU.mult, ALU.add, ["xres", pk[2 + hh]], ["pre"])
                    layer_norm_rows({"stat": stat}, pre, "pre", x1f, "x1f", g1, b1, tmp, "lnC")
                    dma(X1[os_, :], x1f[:], ["x1f"], ["X1"])
                    cp("act", x1b[:], x1f[:], ["x1f"], ["x1b"])
                    for fc in range(8):
                        tr(pT[:, fc * 128:(fc + 1) * 128], x1b[:, fc * 128:(fc + 1) * 128], ident[:], ["x1b", "ident"], ["pT"])
                    cp("act", x1T[:].rearrange("p c n -> p (c n)"), pT[:, :], ["pT"], ["x1T"])
                    dma(X1T3[:, :, os_], x1T[:], ["x1T"], ["X1T"])
            S.barrier()

        if "D" in phases:
            with contextlib.ExitStack() as sd:
                Wdn = SB(sd, "Wdn", [128, 32, 1024], BF16)
                HT = SB(sd, "HT", [128, 32, 1024], BF16)
                x1T = SB(sd, "x1TD", [128, 8, 1024], BF16)
                stg = [SB(sd, "stgD%d" % i, [128, 1024], F32) for i in range(3)]
                wub = [SB(sd, "wub%d" % i, [128, 8, 128], BF16) for i in range(2)]
                rl = [SB(sd, "rlD%d" % i, [128, 512], F32) for i in range(2)]
                g2 = SB(sd, "g2", [128, 1024], F32)
                b2 = SB(sd, "b2", [128, 1024], F32)
                x1r = SB(sd, "x1r", [128, 1024], F32)
                pre = SB(sd, "preD", [128, 1024], F32)
                tmp = SB(sd, "tmpD", [128, 1024], F32)
                of_ = SB(sd, "of", [128, 1024], F32)
                stat = SB(sd, "statD", [128, 8], F32)
                load_w(sd, Wdn, "Wdn", w_dn, 32, 1024, stg)
                dma(g2[:], ln2_g[:, :], [], ["lng"])
                dma(b2[:], ln2_b[:, :], [], ["lnb"])
                X1T3 = X1T.rearrange("(c p) t -> p c t", p=128)
                w_up3 = w_up.rearrange("(c p) f -> p c f", p=128)
                for th in range(2):
                    dma(x1T[:], X1T3[:, :, th * 1024:(th + 1) * 1024], ["X1T"], ["x1TD"])
                    for f in range(32):
                        ws = f % 2
                        wk = "wub%d" % ws
                        i = rr["stg"]
                        rr["stg"] = i + 1
                        sk = "stg%d" % (i % 3)
                        stile = stg[i % 3]
                        dma(stile[:].rearrange("p (c f) -> p c f", c=8), w_up3[:, :, f * 128:(f + 1) * 128], [], [sk])
                        cp("pool", wub[ws][:].rearrange("p c f -> p (c f)"), stile[:], [sk], [wk])
                        for s2 in range(2):
                            bank = 2 * (f % 2) + s2
                            for dc in range(8):
                                mm(pb[bank][:], wub[ws][:, dc, :], x1T[:, dc, s2 * 512:(s2 + 1) * 512], dc == 0, dc == 7,
                                   [wk, "x1TD"], [pk[bank]])
                            rk = "rlD%d" % s2
                            act(rl[s2][:], pb[bank][:], AF.Relu, [pk[bank]], [rk])
                            tt("dve" if s2 == 0 else "pool", HT[:, f, s2 * 512:(s2 + 1) * 512], rl[s2][:], rl[s2][:], ALU.mult,
                               [rk], ["HT"])
                    for tl in range(8):
                        tg = th * 8 + tl
                        os_ = slice(tg * 128, (tg + 1) * 128)
                        dma(x1r[:], X1[os_, :], ["X1"], ["x1r"])
                        for hh in range(2):
                            for f in range(32):
                                mm(pb[4 + hh][:], HT[:, f, tl * 128:(tl + 1) * 128], Wdn[:, f, hh * 512:(hh + 1) * 512],
                                   f == 0, f == 31, ["HT", "Wdn"], [pk[4 + hh]])
                            stt("dve", pre[:, hh * 512:(hh + 1) * 512], x1r[:, hh * 512:(hh + 1) * 512], ALPHA, pb[4 + hh][:],
                                ALU.mult, ALU.add, ["x1r", pk[4 + hh]], ["preD"])
                        layer_norm_rows({"stat": stat}, pre, "preD", of_, "of", g2, b2, tmp, "lnD")
                        dma(out[os_, :], of_[:], ["of"], ["out"])
        S.emit()
    return nc


def _t5_bucket(n):
    n = np.maximum(n, 0)
    nf = np.maximum(n, 1).astype(np.float32)
    large = 16 + (np.log(nf / np.float32(16)) / np.float32(math.log(128 / 16)) * np.float32(16)).astype(np.int32)
    large = np.minimum(large, 31)
    return np.where(n < 16, n, large)


def _consts(j):
    c = {}
    c["c_ident"] = np.eye(128, dtype=np.float32)
    half = 32
    inv = (np.float32(10000.0) ** (-np.arange(half, dtype=np.float32) / np.float32(half))).astype(np.float32)
    c["c_invf"] = np.broadcast_to(inv[None, :], (128, 32)).copy()
    H = 8
    gamma = (1.0 - 2.0 ** (-5.0 - np.arange(H, dtype=np.float64)))
    lg = np.log(gamma)
    nn = np.arange(128, dtype=np.float64)
    diff = nn[None, :] - nn[:, None]
    dT = np.where(diff[:, None, :] >= 0, np.exp(lg[None, :, None] * np.maximum(diff, 0)[:, None, :]), 0.0)
    c["c_decayT"] = dT.reshape(128, 1024).astype(np.float32)
    zeta = np.exp(lg[None, :] * (127.0 - nn[:, None]))
    c["c_zeta8"] = (zeta / 8.0).astype(np.float32)
    xi = np.exp(lg[:, None] * (nn[None, :] + 1.0))
    c["c_xiT"] = np.broadcast_to(xi.reshape(1, 1024), (64, 1024)).astype(np.float32).copy()
    g = np.exp(lg * 128.0)
    c["c_gmat"] = np.broadcast_to(np.repeat(g, 128)[None, :], (64, 1024)).astype(np.float32).copy()
    c["c_tb"] = np.broadcast_to((-1e-30 * np.arange(512, dtype=np.float64))[None, :], (128, 512)).astype(np.float32).copy()
    q = np.arange(128)[:, None]
    kk = np.arange(512)[None, :]
    c["c_cm"] = np.where(kk <= 128 * j + q, 0.0, MASKV).astype(np.float32)
    oh = np.zeros((128, 4), np.float32)
    oh[:, j] = 1.0
    c["c_oh"] = oh
    return c


def _bias_tiles(rel_bias, j):
    key = np.arange(128)[:, None, None]
    kr = np.arange(5)[None, :, None]
    q = np.arange(128)[None, None, :]
    dist = (j + 1 - kr) * 128 + q - key
    bucket = np.where(dist >= 0, _t5_bucket(dist), 31)
    bt = rel_bias[bucket]
    bt = np.transpose(bt, (0, 1, 3, 2))
    return np.ascontiguousarray(bt.reshape(128, 5 * 1024)).astype(np.float32)


_NC_CACHE = {}


def make_in_maps(x, positions, w_in, rel_bias, idx_k_ln_g, idx_k_ln_b, w_attn_branch, w_ret_branch,
                 w_out, ln_mix_g, ln_mix_b, w_up, w_down, ln_ffn_g, ln_ffn_b):
    x = np.asarray(x, np.float32)
    positions = np.asarray(positions, np.int32)
    w = np.asarray(w_in, np.float32)[0]
    rel_bias = np.asarray(rel_bias, np.float32)
    cs = lambda a, b: w[:, a:b]
    w_kv = np.ascontiguousarray(np.concatenate([cs(512, 1024), cs(1024, 1536), cs(1792, 1856), cs(2372, 2884), cs(2884, 3908)], axis=1))
    w_q = np.ascontiguousarray(np.concatenate([cs(0, 512), cs(1536, 1792), cs(1856, 1860)], axis=1))
    w_ro = np.ascontiguousarray(np.concatenate([cs(1860, 2372), cs(3908, 4932)], axis=1))
    w_g = np.ascontiguousarray(cs(4932, 6980))
    rep = lambda v: np.ascontiguousarray(np.broadcast_to(np.asarray(v, np.float32).reshape(1, -1), (128, 1024)))
    shared = {
        "w_kv": w_kv, "w_q": w_q, "w_ro": w_ro, "w_g": w_g,
        "w_a": np.ascontiguousarray(np.asarray(w_attn_branch, np.float32)[0]),
        "w_r": np.ascontiguousarray(np.asarray(w_ret_branch, np.float32)[0]),
        "w_o": np.ascontiguousarray(np.asarray(w_out, np.float32)[0]),
        "w_up": np.ascontiguousarray(np.asarray(w_up, np.float32)[0]),
        "w_dn": np.ascontiguousarray(np.asarray(w_down, np.float32)[0]),
        "lnk_g": np.ascontiguousarray(np.asarray(idx_k_ln_g, np.float32).reshape(64, 1)),
        "lnk_b": np.ascontiguousarray(np.asarray(idx_k_ln_b, np.float32).reshape(64, 1)),
        "ln1_g": rep(ln_mix_g), "ln1_b": rep(ln_mix_b), "ln2_g": rep(ln_ffn_g), "ln2_b": rep(ln_ffn_b),
        "c_c31": np.ascontiguousarray(np.broadcast_to(rel_bias[31][None, :], (128, 8))),
    }
    xTs = [np.ascontiguousarray(x[b].T) for b in range(2)]
    in_maps = []
    own_idx = []
    for c in range(8):
        b, j = c // 4, c % 4
        tok = (np.arange(16)[:, None] * 4 + j) * 128 + np.arange(128)[None, :]
        tok = tok.reshape(-1)
        own_idx.append((b, tok))
        d = dict(shared)
        d["xT"] = xTs[b]
        d["xoT"] = np.ascontiguousarray(xTs[b][:, tok])
        d["xo"] = np.ascontiguousarray(x[b][tok])
        d["posT"] = np.ascontiguousarray(positions[b].reshape(64, 128).T).astype(np.float32)
        d["posoT"] = np.ascontiguousarray(positions[b][tok].reshape(16, 128).T).astype(np.float32)
        d.update(_consts(j))
        d["c_bt"] = _bias_tiles(rel_bias, j)
        in_maps.append(d)
    return in_maps, own_idx


def kernel(**inputs):
    in_maps, own_idx = make_in_maps(**inputs)
    if "nc" not in _NC_CACHE:
        _NC_CACHE["nc"] = build()
    nc = _NC_CACHE["nc"]
    res = run_bass_kernel_spmd(nc, in_maps, core_ids=list(range(8)))
    outp = np.zeros((2, 8192, 1024), np.float32)
    for c in range(8):
        b, tok = own_idx[c]
        outp[b, tok] = res.results[c]["out"]
    return outp
```

```python
import contextlib
import math
import numpy as np
import concourse.bass as bass
import concourse.mybir as mybir
from concourse.bass_utils import run_bass_kernel_spmd

F32 = mybir.dt.float32
BF16 = mybir.dt.bfloat16
I32 = mybir.dt.int32
ALU = mybir.AluOpType
AF = mybir.ActivationFunctionType
AX = mybir.AxisListType

NBIS = 19
LO_INIT = -60.0
MASKV = -30000.0
ALPHA = 2.0 ** 0.25
EPS = 1e-5
PI = math.pi


class Sched:
    CE = ("pe", "act", "dve", "pool")
    ALLE = ("pe", "act", "dve", "pool", "sp")

    def __init__(self, nc, stack, n_dma=(("sp", 24), ("act", 8))):
        self.nc = nc
        self.ops = {e: [] for e in self.ALLE}
        self.cnt = {e: 0 for e in self.CE}
        self.sem = {}
        for e in self.CE:
            self.sem["c_" + e] = stack.enter_context(nc.semaphore("c_" + e))
        self.dma_slots = {}
        self.dma_next = {}
        self.dma_val = {}
        for e, n in n_dma:
            self.dma_slots[e] = []
            for i in range(n):
                k = "d_%s_%d" % (e, i)
                self.sem[k] = stack.enter_context(nc.semaphore(k))
                self.dma_slots[e].append(k)
                self.dma_val[k] = 0
            self.dma_next[e] = 0
        self.buf = {}
        self.waited = {e: {} for e in self.ALLE}
        self.pending = {e: {} for e in self.ALLE}
        self.nops = 0

    def _st(self, k):
        s = self.buf.get(k)
        if s is None:
            s = {"w": None, "r": {}}
            self.buf[k] = s
        return s

    def op(self, eng, fn, r=(), w=(), dma=False):
        waits = dict(self.pending[eng])
        self.pending[eng] = {}

        def add(ev):
            if ev is None:
                return
            k, v = ev
            if waits.get(k, 0) < v:
                waits[k] = v

        for k in r:
            add(self._st(k)["w"])
        for k in w:
            s = self._st(k)
            add(s["w"])
            for ev in s["r"].items():
                add(ev)
        if dma:
            slots = self.dma_slots[eng]
            k = slots[self.dma_next[eng] % len(slots)]
            self.dma_next[eng] += 1
            if self.dma_val[k] > 0:
                add((k, self.dma_val[k]))
            self.dma_val[k] += 16
            ev = (k, self.dma_val[k])
            inc = 16
        else:
            self.cnt[eng] += 1
            ev = ("c_" + eng, self.cnt[eng])
            inc = 1
        wl = []
        own = "c_" + eng
        for k, v in waits.items():
            if k == own and eng == "pe":
                continue
            if self.waited[eng].get(k, 0) >= v:
                continue
            self.waited[eng][k] = v
            wl.append((k, v))
        self.ops[eng].append((wl, fn, ev[0], inc))
        self.nops += 1
        for k in w:
            s = self._st(k)
            s["w"] = ev
            s["r"] = {}
        for k in r:
            if k in w:
                continue
            s = self._st(k)
            if s["r"].get(ev[0], 0) < ev[1]:
                s["r"][ev[0]] = ev[1]
        return ev

    def all_events(self):
        evs = {}
        for e in self.CE:
            if self.cnt[e] > 0:
                evs["c_" + e] = self.cnt[e]
        for k, v in self.dma_val.items():
            if v > 0:
                evs[k] = v
        return evs

    def barrier(self):
        evs = self.all_events()
        for e in self.ALLE:
            for k, v in evs.items():
                if self.pending[e].get(k, 0) < v:
                    self.pending[e][k] = v
        self.buf = {}

    def emit(self):
        nc = self.nc
        final = self.all_events()
        with nc.Block() as block:
            def run(eng_name, e, is_last=False):
                for wl, fn, sk, inc in self.ops[eng_name]:
                    for k, v in wl:
                        e.wait_ge(self.sem[k], v)
                    ins = fn(e)
                    ins.then_inc(self.sem[sk], inc)
                if is_last:
                    for k, v in final.items():
                        e.wait_ge(self.sem[k], v)

            @block.tensor
            def _(e):
                run("pe", e)

            @block.scalar
            def _(e):
                run("act", e)

            @block.vector
            def _(e):
                run("dve", e)

            @block.gpsimd
            def _(e):
                run("pool", e)

            @block.sync
            def _(e):
                run("sp", e, True)


def build(phases="ABCD", debug=False, ng=16, own=True, alltok=True, ostage=99, nb=16, bstage=99, mstart=0, nkcap=999, nobias=False, dummy=0):
    nc = bass.Bass("TRN2", target_bir_lowering=False)

    def din(name, shape, dt=F32):
        return nc.dram_tensor(name, shape, dt, kind="ExternalInput").ap()

    def dscr(name, shape, dt):
        kind = "ExternalOutput" if debug else "Internal"
        return nc.dram_tensor(name, shape, dt, kind=kind).ap()

    xT = din("xT", [1024, 8192])
    xoT = din("xoT", [1024, 2048])
    xo = din("xo", [2048, 1024])
    posT = din("posT", [128, 64], F32)
    posoT = din("posoT", [128, 16], F32)
    w_kv = din("w_kv", [1024, 2624])
    w_q = din("w_q", [1024, 772])
    w_ro = din("w_ro", [1024, 1536])
    w_g = din("w_g", [1024, 2048])
    w_a = din("w_a", [512, 1024])
    w_r = din("w_r", [1024, 1024])
    w_o = din("w_o", [1024, 1024])
    w_up = din("w_up", [1024, 4096])
    w_dn = din("w_dn", [4096, 1024])
    lnk_g = din("lnk_g", [64, 1])
    lnk_b = din("lnk_b", [64, 1])
    ln1_g = din("ln1_g", [128, 1024])
    ln1_b = din("ln1_b", [128, 1024])
    ln2_g = din("ln2_g", [128, 1024])
    ln2_b = din("ln2_b", [128, 1024])
    c_ident = din("c_ident", [128, 128])
    c_invf = din("c_invf", [128, 32])
    c_decayT = din("c_decayT", [128, 1024])
    c_zeta8 = din("c_zeta8", [128, 8])
    c_xiT = din("c_xiT", [64, 1024])
    c_gmat = din("c_gmat", [64, 1024])
    c_tb = din("c_tb", [128, 512])
    c_cm = din("c_cm", [128, 512])
    c_bt = din("c_bt", [128, 5 * 1024])
    c_c31 = din("c_c31", [128, 8])
    c_oh = din("c_oh", [128, 4])
    out = nc.dram_tensor("out", [2048, 1024], F32, kind="ExternalOutput").ap()

    KT = dscr("s_KT", [512, 8192], BF16)
    VV = dscr("s_V", [64, 128, 520], BF16)
    IKT = dscr("s_IKT", [64, 8192], BF16)
    YAT = dscr("s_YAT", [512, 2048], BF16)
    YRT = dscr("s_YRT", [1024, 2048], BF16)
    X1 = dscr("s_X1", [2048, 1024], F32)
    X1T = dscr("s_X1T", [1024, 2048], BF16)

    with contextlib.ExitStack() as st0:
        S = Sched(nc, st0)
        rr = {"cast": 0}

        def SB(stk, name, shape, dt):
            return stk.enter_context(nc.sbuf_tensor(name, shape, dt))

        def PS(stk, name, shape, dt):
            return stk.enter_context(nc.psum_tensor(name, shape, dt))

        def dma(out_ap, in_ap, r, w, eng="sp"):
            S.op(eng, lambda e, o=out_ap, i=in_ap: e.dma_start(out=o, in_=i), r=r, w=w, dma=True)

        def mm(out_ap, lhsT, rhs, start, stop, r, w):
            S.op("pe", lambda e, o=out_ap, l=lhsT, rh=rhs, s0=start, s1=stop: e.matmul(o, lhsT=l, rhs=rh, start=s0, stop=s1),
                 r=r, w=w)

        def tr(out_ap, in_ap, ident_ap, r, w):
            S.op("pe", lambda e, o=out_ap, i=in_ap, d=ident_ap: e.transpose(out=o, in_=i, identity=d), r=r, w=w)

        def act(out_ap, in_ap, func, r, w, bias=None, scale=None, accum=None):
            kw = {}
            if bias is not None:
                kw["bias"] = bias
            if scale is not None:
                kw["scale"] = scale
            if accum is not None:
                kw["accum_out"] = accum
            S.op("act", lambda e, o=out_ap, i=in_ap, f=func, kw=kw: e.activation(out=o, in_=i, func=f, **kw), r=r, w=w)

        def tt(eng, out_ap, in0, in1, op, r, w):
            S.op(eng, lambda e, o=out_ap, a=in0, b=in1, p=op: e.tensor_tensor(out=o, in0=a, in1=b, op=p), r=r, w=w)

        def ts(eng, out_ap, in0, s1, s2, op0, op1, r, w, accum=None):
            kw = {}
            if op1 is not None:
                kw["op1"] = op1
            if accum is not None:
                kw["accum_out"] = accum
            S.op(eng, lambda e, o=out_ap, a=in0, x1=s1, x2=s2, p0=op0, kw=kw:
                 e.tensor_scalar(out=o, in0=a, scalar1=x1, scalar2=x2, op0=p0, **kw), r=r, w=w)

        def stt(eng, out_ap, in0, scalar, in1, op0, op1, r, w):
            S.op(eng, lambda e, o=out_ap, a=in0, s=scalar, b=in1, p0=op0, p1=op1:
                 e.scalar_tensor_tensor(out=o, in0=a, scalar=s, in1=b, op0=p0, op1=p1), r=r, w=w)

        def cp(eng, out_ap, in_ap, r, w):
            if eng == "act":
                act(out_ap, in_ap, AF.Copy, r, w)
            else:
                S.op(eng, lambda e, o=out_ap, i=in_ap: e.tensor_copy(out=o, in_=i), r=r, w=w)

        def red(eng, out_ap, in_ap, op, r, w):
            S.op(eng, lambda e, o=out_ap, i=in_ap, p=op: e.tensor_reduce(out=o, in_=i, axis=AX.X, op=p), r=r, w=w)

        def memset(eng, ap, val, w):
            S.op(eng, lambda e, a=ap, v=val: e.memset(a, v), w=w)

        ident_f = SB(st0, "ident_f", [128, 128], F32)
        ident = SB(st0, "ident", [128, 128], BF16)
        dma(ident_f[:], c_ident[:, :], [], ["ident_f"])
        cp("dve", ident[:], ident_f[:], ["ident_f"], ["ident"])
        epsT = SB(st0, "epsT", [128, 1], F32)
        memset("dve", epsT[:], EPS, ["epsT"])

        def rstd(out_ap, var_ap, r, w):
            act(out_ap, var_ap, AF.Sqrt, list(r) + ["epsT"], list(w), bias=epsT[:, 0:1], scale=1.0)
            S.op("dve", lambda e, o=out_ap: e.reciprocal(out=o, in_=o), r=list(w), w=list(w))

        pb = [PS(st0, "pb%d" % i, [128, 512], F32) for i in range(7)]
        pT = PS(st0, "pT", [128, 1024], BF16)
        pk = ["pb%d" % i for i in range(7)]

        def cast_eng():
            rr["cast"] += 1
            return ("act", "dve", "act", "dve", "pool")[rr["cast"] % 5]

        def load_w(stk_stage, dst, dst_key, src, nrow_chunks, cols, stg, col0=0):
            for rc in range(nrow_chunks):
                for c0 in range(0, cols, 2048):
                    cw = min(2048, cols - c0)
                    i = rr.setdefault("stg", 0)
                    rr["stg"] = i + 1
                    sk = "stg%d" % (i % len(stg))
                    stile = stg[i % len(stg)]
                    dma(stile[:, 0:cw], src[rc * 128:(rc + 1) * 128, c0:c0 + cw], [], [sk])
                    cp(cast_eng(), dst[:, rc, col0 + c0:col0 + c0 + cw], stile[:, 0:cw], [sk], [dst_key])

        def layer_norm_rows(stk, pre, pre_key, outf, out_key, g_t, b_t, tmp, nm):
            s1 = nm + "_s1"
            st_t = stk["stat"]
            red("dve", st_t[:, 0:1], pre[:], ALU.add, [pre_key], [s1])
            act(tmp[:], pre[:], AF.Square, [pre_key], [nm + "_tmp", s1 + "q"], accum=st_t[:, 1:2])
            ts("dve", st_t[:, 2:3], st_t[:, 0:1], 1.0 / 1024, None, ALU.mult, None, [s1], [s1 + "m"])
            ts("dve", st_t[:, 3:4], st_t[:, 1:2], 1.0 / 1024, None, ALU.mult, None, [s1 + "q"], [s1 + "e"])
            tt("dve", st_t[:, 4:5], st_t[:, 2:3], st_t[:, 2:3], ALU.mult, [s1 + "m"], [s1 + "mm"])
            tt("dve", st_t[:, 5:6], st_t[:, 3:4], st_t[:, 4:5], ALU.subtract, [s1 + "e", s1 + "mm"], [s1 + "v"])
            rstd(st_t[:, 6:7], st_t[:, 5:6], [s1 + "v"], [s1 + "r"])
            ts("dve", pre[:], pre[:], st_t[:, 2:3], st_t[:, 6:7], ALU.subtract, ALU.mult, [pre_key, s1 + "m", s1 + "r"], [pre_key])
            tt("dve", pre[:], pre[:], g_t[:], ALU.mult, [pre_key, "lng"], [pre_key])
            tt("dve", outf[:], pre[:], b_t[:], ALU.add, [pre_key, "lnb"], [out_key])

        if "A" in phases:
            with contextlib.ExitStack() as sa:
                Wkv = SB(sa, "Wkv", [128, 8, 2624], BF16)
                Wro = SB(sa, "Wro", [128, 8, 1536], BF16)
                stg = [SB(sa, "stgA%d" % i, [128, 2048], F32) for i in range(2)]
                xgf2 = [SB(sa, "xgf%d" % i, [128, 8, 512], F32) for i in range(2)]
                xgb = SB(sa, "xgb", [128, 8, 512], BF16)
                posf = SB(sa, "posf", [128, 64], F32)
                posof = SB(sa, "posof", [128, 16], F32)
                invf = SB(sa, "invf", [128, 32], F32)
                ang = SB(sa, "ang", [128, 2, 32], F32)
                CC = SB(sa, "CC", [128, 64], F32)
                SS = SB(sa, "SS", [128, 64], F32)
                kT_sb = SB(sa, "kT_sb", [128, 4, 512], BF16)
                v_sb = SB(sa, "v_sb", [128, 4, 8, 65], BF16)
                ikT_sb = SB(sa, "ikT_sb", [64, 512], BF16)
                ik_f = SB(sa, "ik_f", [128, 64], F32)
                ik_n = SB(sa, "ik_n", [128, 64], BF16)
                ik_junk = SB(sa, "ik_junk", [128, 64], F32)
                stat = SB(sa, "statA", [128, 8], F32)
                lng = SB(sa, "lnkg", [64, 1], F32)
                lnb = SB(sa, "lnkb", [64, 1], F32)
                kz2 = [SB(sa, "kz%d" % i, [128, 512], BF16) for i in range(2)]
                vr2 = [SB(sa, "vr%d" % i, [128, 1024], BF16) for i in range(2)]
                rA = SB(sa, "rA", [128, 512], F32)
                rB = SB(sa, "rB", [128, 512], F32)
                rO = SB(sa, "rO", [128, 512], F32)
                zeta8 = SB(sa, "zeta8", [128, 8], F32)
                R = SB(sa, "R", [64, 1024], F32)
                Rsel = SB(sa, "Rsel", [64, 1024], F32)
                Rselb = SB(sa, "Rselb", [64, 1024], BF16)
                gmat = SB(sa, "gmat", [64, 1024], F32)
                oh = SB(sa, "oh", [128, 4], F32)
                xiT = SB(sa, "xiT", [64, 1024], F32)
                decayT = SB(sa, "decayT", [128, 1024], F32)
                xof = SB(sa, "xof", [128, 8, 128], F32)
                xob = SB(sa, "xob", [128, 8, 128], BF16)
                qrb = SB(sa, "qrb", [128, 512], BF16)
                krb = SB(sa, "krb", [128, 512], BF16)
                qT = SB(sa, "qT", [64, 8, 128], BF16)
                qxiT = SB(sa, "qxiT", [64, 8, 128], BF16)
                kTo = SB(sa, "kTo", [64, 8, 128], BF16)
                vro = SB(sa, "vro", [128, 1024], BF16)
                sgr = SB(sa, "sgr", [128, 1024], F32)
                Dm = SB(sa, "Dm", [128, 1024], BF16)
                osb = SB(sa, "osb", [128, 1024], F32)
                osq = SB(sa, "osq", [128, 1024], F32)
                hst = SB(sa, "hst", [128, 8, 8], F32)
                yrb = SB(sa, "yrb", [128, 1024], BF16)
                yrT = SB(sa, "yrT", [128, 8, 128], BF16)

                load_w(sa, Wkv, "Wkv", w_kv, 8, 2624, stg)
                load_w(sa, Wro, "Wro", w_ro, 8, 1536, stg)
                dma(posf[:], posT[:, :], [], ["posf"])
                dma(posof[:], posoT[:, :], [], ["posof"])
                dma(invf[:], c_invf[:, :], [], ["invf"])
                dma(lng[:], lnk_g[:, :], [], ["lnkg"])
                dma(lnb[:], lnk_b[:, :], [], ["lnkb"])
                dma(zeta8[:], c_zeta8[:, :], [], ["zeta8"])
                dma(gmat[:], c_gmat[:, :], [], ["gmat"])
                dma(oh[:], c_oh[:, :], [], ["oh"])
                dma(xiT[:], c_xiT[:, :], [], ["xiT"])
                dma(decayT[:], c_decayT[:, :], [], ["decayT"])
                memset("pool", R[:], 0.0, ["R"])
                memset("pool", v_sb[:], 1.0, ["v_sb"])

                def cos_sin(pcol, pkey):
                    MG = 12582912.0
                    ts("dve", ang[:, 0, :], invf[:], pcol, None, ALU.mult, None, ["invf", pkey], ["ang"])
                    ts("dve", ang[:, 1, :], ang[:, 0, :], 0.5 * PI, None, ALU.add, None, ["ang"], ["ang"])
                    ts("dve", angk[:], ang[:], 1.0 / (2 * PI), MG, ALU.mult, ALU.add, ["ang"], ["angk"])
                    ts("dve", angk[:], angk[:], -MG, None, ALU.add, None, ["angk"], ["angk"])
                    stt("dve", ang[:], angk[:], -2 * PI, ang[:], ALU.mult, ALU.add, ["angk", "ang"], ["ang"])
                    ts("dve", ang[:], ang[:], 3.1415925, -3.1415925, ALU.min, ALU.max, ["ang"], ["ang"])
                    act(SS[:, 0:32], ang[:, 0, :], AF.Sin, ["ang"], ["SS"])
                    act(SS[:, 32:64], ang[:, 0, :], AF.Sin, ["ang"], ["SS"], scale=-1.0)
                    act(CC[:, 0:32], ang[:, 1, :], AF.Sin, ["ang"], ["CC"])
                    act(CC[:, 32:64], ang[:, 1, :], AF.Sin, ["ang"], ["CC"])

                angk = SB(sa, "angk", [128, 2, 32], F32)
                qf = SB(sa, "qf", [64, 1024], F32)

                def rope(src_ps, src_key, dst):
                    s3 = src_ps.rearrange("p (h d) -> p h d", h=8)
                    a3 = rA[:].rearrange("p (h d) -> p h d", h=8)
                    b3 = rB[:].rearrange("p (h d) -> p h d", h=8)
                    o3 = dst[:].rearrange("p (h d) -> p h d", h=8)
                    ccb = CC[:].unsqueeze(1).to_broadcast([128, 8, 64])
                    ssb = SS[:].unsqueeze(1).to_broadcast([128, 8, 64])
                    tt("dve", a3, s3, ccb, ALU.mult, [src_key, "CC"], ["rA"])
                    tt("dve", b3, s3, ssb, ALU.mult, [src_key, "SS"], ["rB"])
                    tt("pool", o3[:, :, 0:32], a3[:, :, 0:32], b3[:, :, 32:64], ALU.add, ["rA", "rB"], ["rO"])
                    tt("pool", o3[:, :, 32:64], a3[:, :, 32:64], b3[:, :, 0:32], ALU.add, ["rA", "rB"], ["rO"])

                xT3 = xT.rearrange("(c p) t -> p c t", p=128)
                xoT3 = xoT.rearrange("(c p) t -> p c t", p=128)
                KT3 = KT.rearrange("(c p) t -> p c t", p=128)
                YRT3 = YRT.rearrange("(c p) t -> p c t", p=128)
                VV3 = VV.rearrange("t p f -> p t f")

                for m in range(ng):
                    if not alltok:
                        break
                    xgf = xgf2[m % 2]
                    xgk = "xgf%d" % (m % 2)
                    if m == 0:
                        dma(xgf[:], xT3[:, :, 0:512], [], [xgk])
                    cp("dve", xgb[:, 0:4, :], xgf[:, 0:4, :], [xgk], ["xgb"])
                    cp("act", xgb[:, 4:8, :], xgf[:, 4:8, :], [xgk], ["xgb"])
                    if m + 1 < ng:
                        dma(xgf2[(m + 1) % 2][:], xT3[:, :, (m + 1) * 512:(m + 2) * 512], [], ["xgf%d" % ((m + 1) % 2)])
                    pend = [None]

                    def flushU():
                        if pend[0] is None:
                            return
                        ps = pend[0]
                        pend[0] = None
                        kzp, vrp = kz2[ps], vr2[ps]
                        for h in range(8):
                            mm(pb[5 + h // 4][0:64, (h % 4) * 128:(h % 4 + 1) * 128], kzp[:, h * 64:(h + 1) * 64],
                               vrp[:, h * 128:(h + 1) * 128], True, True, ["kz%d" % ps, "vr%d" % ps], [pk[5 + h // 4]])
                        tt("dve", R[:], R[:], gmat[:], ALU.mult, ["R", "gmat"], ["R"])
                        tt("dve", R[:, 0:512], R[:, 0:512], pb[5][0:64, :], ALU.add, ["R", pk[5]], ["R"])
                        tt("dve", R[:, 512:1024], R[:, 512:1024], pb[6][0:64, :], ALU.add, ["R", pk[6]], ["R"])
                    for fc in range(4):
                        for dc in range(8):
                            mm(pb[0][:], Wkv[:, dc, fc * 128:(fc + 1) * 128], xgb[:, dc, :], dc == 0, dc == 7,
                               ["Wkv", "xgb"], [pk[0]])
                        cp("act", kT_sb[:, fc, :], pb[0][:], [pk[0]], ["kT_sb"])
                    dma(KT3[:, :, m * 512:(m + 1) * 512], kT_sb[:], ["kT_sb"], ["KT"])
                    for i in range(4):
                        t = 4 * m + i
                        xs = slice(i * 128, (i + 1) * 128)
                        for dc in range(8):
                            mm(pb[0][:, 0:64], xgb[:, dc, xs], Wkv[:, dc, 1024:1088], dc == 0, dc == 7,
                               ["Wkv", "xgb"], [pk[0]])
                        cp("dve", ik_f[:], pb[0][:, 0:64], [pk[0]], ["ik_f"])
                        red("dve", stat[:, 0:1], ik_f[:], ALU.add, ["ik_f"], ["st0"])
                        act(ik_junk[:], ik_f[:], AF.Square, ["ik_f"], ["ik_junk", "st1"], accum=stat[:, 1:2])
                        ts("dve", stat[:, 2:3], stat[:, 0:1], 1.0 / 64, None, ALU.mult, None, ["st0"], ["st2"])
                        ts("dve", stat[:, 3:4], stat[:, 1:2], 1.0 / 64, None, ALU.mult, None, ["st1"], ["st3"])
                        tt("dve", stat[:, 4:5], stat[:, 2:3], stat[:, 2:3], ALU.mult, ["st2"], ["st4"])
                        tt("dve", stat[:, 5:6], stat[:, 3:4], stat[:, 4:5], ALU.subtract, ["st3", "st4"], ["st5"])
                        rstd(stat[:, 6:7], stat[:, 5:6], ["st5"], ["st6"])
                        ts("dve", ik_n[:], ik_f[:], stat[:, 2:3], stat[:, 6:7], ALU.subtract, ALU.mult,
                           ["ik_f", "st2", "st6"], ["ik_n"])
                        tr(pT[0:64, 0:128], ik_n[:], ident[:], ["ik_n", "ident"], ["pT"])
                        act(ikT_sb[:, xs], pT[0:64, 0:128], AF.Identity, ["pT", "lnkg", "lnkb"], ["ikT_sb"],
                            bias=lnb[:, 0:1], scale=lng[:, 0:1])
                        for dc in range(8):
                            mm(pb[1][:], xgb[:, dc, xs], Wkv[:, dc, 512:1024], dc == 0, dc == 7, ["Wkv", "xgb"], [pk[1]])
                        cp("act", v_sb[:, i, :, 0:64], pb[1][:].rearrange("p (h d) -> p h d", h=8), [pk[1]], ["v_sb"])
                        for dc in range(8):
                            mm(pb[2][:], xgb[:, dc, xs], Wkv[:, dc, 1088:1600], dc == 0, dc == 7, ["Wkv", "xgb"], [pk[2]])
                        for hf in range(2):
                            for dc in range(8):
                                mm(pb[3 + hf][:], xgb[:, dc, xs], Wkv[:, dc, 1600 + hf * 512:2112 + hf * 512],
                                   dc == 0, dc == 7, ["Wkv", "xgb"], [pk[3 + hf]])
                            cp("act", vr2[t % 2][:, hf * 512:(hf + 1) * 512], pb[3 + hf][:], [pk[3 + hf]], ["vr%d" % (t % 2)])
                        flushU()
                        cos_sin(posf[:, t:t + 1], "posf")
                        rope(pb[2][:], pk[2], rO)
                        tt("dve", kz2[t % 2][:].rearrange("p (h d) -> p h d", h=8), rO[:].rearrange("p (h d) -> p h d", h=8),
                           zeta8[:].unsqueeze(2).to_broadcast([128, 8, 64]), ALU.mult, ["rO", "zeta8"], ["kz%d" % (t % 2)])
                        if i == 0:
                            ts("dve", Rsel[:], R[:], oh[0:64, 0:1], None, ALU.mult, None, ["R", "oh"], ["Rsel"])
                        else:
                            stt("dve", Rsel[:], R[:], oh[0:64, i:i + 1], Rsel[:], ALU.mult, ALU.add, ["R", "oh", "Rsel"], ["Rsel"])
                        pend[0] = t % 2
                    flushU()
                    dma(VV3[:, 4 * m:4 * m + 4, :], v_sb[:].rearrange("p t h d -> p t (h d)"), ["v_sb"], ["VV"])
                    dma(IKT[:, m * 512:(m + 1) * 512], ikT_sb[:], ["ikT_sb"], ["IKT"])
                    if not own:
                        continue

                    os_ = slice(m * 128, (m + 1) * 128)
                    dma(xof[:], xoT3[:, :, os_], [], ["xof"])
                    cp("act", xob[:], xof[:], ["xof"], ["xob"])
                    cp("act", Rselb[:], Rsel[:], ["Rsel"], ["Rselb"])
                    cos_sin(posof[:, m:m + 1], "posof")
                    for dc in range(8):
                        mm(pb[1][:], xob[:, dc, :], Wro[:, dc, 0:512], dc == 0, dc == 7, ["Wro", "xob"], [pk[1]])
                    rope(pb[1][:], pk[1], rO)
                    cp("act", qrb[:], rO[:], ["rO"], ["qrb"])
                    for dc in range(8):
                        mm(pb[2][:], xob[:, dc, :], Wkv[:, dc, 1088:1600], dc == 0, dc == 7, ["Wkv", "xob"], [pk[2]])
                    rope(pb[2][:], pk[2], rO)
                    S.op("act", lambda e, o=krb[:], i=rO[:]: e.mul(out=o, in_=i, mul=0.125), r=["rO"], w=["krb"])
                    for hf in range(2):
                        for dc in range(8):
                            mm(pb[3 + hf][:], xob[:, dc, :], Wkv[:, dc, 1600 + hf * 512:2112 + hf * 512],
                               dc == 0, dc == 7, ["Wkv", "xob"], [pk[3 + hf]])
                        cp("act", vro[:, hf * 512:(hf + 1) * 512], pb[3 + hf][:], [pk[3 + hf]], ["vro"])
                    for hf in range(2):
                        for dc in range(8):
                            mm(pb[5 + hf][:], xob[:, dc, :], Wro[:, dc, 512 + hf * 512:1024 + hf * 512],
                               dc == 0, dc == 7, ["Wro", "xob"], [pk[5 + hf]])
                        act(sgr[:, hf * 512:(hf + 1) * 512], pb[5 + hf][:], AF.Sigmoid, [pk[5 + hf]], ["sgr"])
                        tt("dve", sgr[:, hf * 512:(hf + 1) * 512], sgr[:, hf * 512:(hf + 1) * 512], pb[5 + hf][:], ALU.mult,
                           ["sgr", pk[5 + hf]], ["sgr"])
                    if ostage < 1:
                        continue
                    for h in range(8):
                        tr(pT[0:64, h * 128:(h + 1) * 128], qrb[:, h * 64:(h + 1) * 64], ident[:], ["qrb", "ident"], ["pT"])
                    cp("act", qf[:], pT[0:64, :], ["pT"], ["qf"])
                    cp("pool", qT[:].rearrange("p h n -> p (h n)"), qf[:], ["qf"], ["qT"])
                    tt("dve", qxiT[:].rearrange("p h n -> p (h n)"), qf[:], xiT[:], ALU.mult, ["qf", "xiT"], ["qxiT"])
                    for h in range(8):
                        tr(pT[0:64, h * 128:(h + 1) * 128], krb[:, h * 64:(h + 1) * 64], ident[:], ["krb", "ident"], ["pT"])
                    cp("act", kTo[:].rearrange("p h n -> p (h n)"), pT[0:64, :], ["pT"], ["kTo"])
                    if ostage < 2:
                        continue
                    for h in range(8):
                        mm(pb[3 + h // 4][:, (h % 4) * 128:(h % 4 + 1) * 128], kTo[:, h, :], qT[:, h, :], True, True,
                           ["kTo", "qT"], [pk[3 + h // 4]])
                    tt("dve", Dm[:, 0:512], pb[3][:], decayT[:, 0:512], ALU.mult, [pk[3], "decayT"], ["Dm"])
                    tt("dve", Dm[:, 512:1024], pb[4][:], decayT[:, 512:1024], ALU.mult, [pk[4], "decayT"], ["Dm"])
                    for h in range(8):
                        o_ap = pb[5 + h // 4][:, (h % 4) * 128:(h % 4 + 1) * 128]
                        mm(o_ap, Dm[:, h * 128:(h + 1) * 128], vro[:, h * 128:(h + 1) * 128], True, False,
                           ["Dm", "vro"], [pk[5 + h // 4]])
                        mm(o_ap, qxiT[:, h, :], Rselb[:, h * 128:(h + 1) * 128], False, True,
                           ["qxiT", "Rselb"], [pk[5 + h // 4]])
                    if ostage < 3:
                        continue
                    cp("act", osb[:, 0:512], pb[5][:], [pk[5]], ["osb"])
                    cp("act", osb[:, 512:1024], pb[6][:], [pk[6]], ["osb"])
                    o3 = osb[:].rearrange("p (h v) -> p h v", h=8)
                    q3 = osq[:].rearrange("p (h v) -> p h v", h=8)
                    red("dve", hst[:, 0, :], o3, ALU.add, ["osb"], ["h0"])
                    tt("pool", osq[:], osb[:], osb[:], ALU.mult, ["osb"], ["osq"])
                    red("dve", hst[:, 1, :], q3, ALU.add, ["osq"], ["h1"])
                    ts("dve", hst[:, 2, :], hst[:, 0, :], 1.0 / 128, None, ALU.mult, None, ["h0"], ["h2"])
                    ts("dve", hst[:, 3, :], hst[:, 1, :], 1.0 / 128, None, ALU.mult, None, ["h1"], ["h3"])
                    tt("dve", hst[:, 4, :], hst[:, 2, :], hst[:, 2, :], ALU.mult, ["h2"], ["h4"])
                    tt("dve", hst[:, 5, :], hst[:, 3, :], hst[:, 4, :], ALU.subtract, ["h3", "h4"], ["h5"])
                    rstd(hst[:, 6, :], hst[:, 5, :], ["h5"], ["h6"])
                    tt("dve", q3, o3, hst[:, 2, :].unsqueeze(2).to_broadcast([128, 8, 128]), ALU.subtract, ["osb", "h2", "osq"], ["osq"])
                    tt("dve", q3, q3, hst[:, 6, :].unsqueeze(2).to_broadcast([128, 8, 128]), ALU.mult, ["osq", "h6"], ["osq"])
                    if ostage < 4:
                        continue
                    tt("pool", yrb[:], osq[:], sgr[:], ALU.mult, ["osq", "sgr"], ["yrb"])
                    for fc in range(8):
                        tr(pT[:, fc * 128:(fc + 1) * 128], yrb[:, fc * 128:(fc + 1) * 128], ident[:], ["yrb", "ident"], ["pT"])
                    cp("act", yrT[:].rearrange("p c n -> p (c n)"), pT[:, :], ["pT"], ["yrT"])
                    dma(YRT3[:, :, os_], yrT[:], ["yrT"], ["YRT"])
            S.barrier()

        if "B" in phases:
            with contextlib.ExitStack() as sbk:
                KTs = SB(sbk, "KTs", [128, 4, 8192], BF16)
                Vc = [SB(sbk, "Vc%d" % i, [128, 4, 520], BF16) for i in range(2)]
                IKs = SB(sbk, "IKs", [64, 8192], BF16)
                score = SB(sbk, "score", [128, 8192], F32)
                Wq = SB(sbk, "Wq", [128, 8, 772], BF16)
                stg = [SB(sbk, "stgB%d" % i, [128, 1024], F32) for i in range(2)]
                xof = SB(sbk, "xofB", [128, 8, 128], F32)
                xob = SB(sbk, "xobB", [128, 8, 128], BF16)
                qaT = SB(sbk, "qaT", [128, 4, 2, 128], BF16)
                iqT = SB(sbk, "iqT", [64, 4, 128], BF16)
                iwf = SB(sbk, "iwf", [128, 4], F32)
                aw = SB(sbk, "aw", [128, 4], F32)
                sgn = SB(sbk, "sgn", [128, 4], F32)
                rl = [SB(sbk, "rl%d" % i, [128, 512], F32) for i in range(4)]
                tb = SB(sbk, "tb", [128, 512], F32)
                cm = SB(sbk, "cm", [128, 512], F32)
                btf = SB(sbk, "btf", [128, 1024], F32)
                bts = SB(sbk, "bts", [128, 5, 1024], BF16)
                c31 = SB(sbk, "c31", [128, 8], F32)
                bs = SB(sbk, "bs", [128, 64], F32)
                junk = SB(sbk, "junkB", [128, 3712], BF16)
                junkA = SB(sbk, "junkA", [128, 4608], BF16)
                mk = [SB(sbk, "mk%d" % i, [128, 128], BF16) for i in range(2)]
                mkT = [SB(sbk, "mkT%d" % i, [128, 128], BF16) for i in range(2)]
                Eb = [SB(sbk, "Eb%d" % i, [128, 1024], BF16) for i in range(2)]
                Pb = [SB(sbk, "Pb%d" % i, [128, 1024], BF16) for i in range(2)]
                osb = SB(sbk, "osbB", [128, 520], F32)
                rs = SB(sbk, "rsB", [128, 8], F32)
                yab = SB(sbk, "yab", [128, 512], BF16)
                yaT = SB(sbk, "yaT", [128, 4, 128], BF16)

                KT3 = KT.rearrange("(c p) t -> p c t", p=128)
                for c4 in range(4):
                    dma(KTs[:, c4, :], KT3[:, c4, :], ["KT"], ["KTs"])
                VV3 = VV.rearrange("t p f -> p t f")
                dma(IKs[:], IKT[:, :], ["IKT"], ["IKs"])
                load_w(sbk, Wq, "Wq", w_q, 8, 772, stg)
                memset("pool", qaT[:], 0.0, ["qaT"])
                zr = SB(sbk, "zr", [128, 512], BF16)
                memset("pool", zr[:], 0.0, ["zr"])
                dma(tb[:], c_tb[:, :], [], ["tb"])
                dma(cm[:], c_cm[:, :], [], ["cm"])
                dma(c31[:], c_c31[:, :], [], ["c31"])
                for kr in range(5):
                    dma(btf[:], c_bt[:, kr * 1024:(kr + 1) * 1024], [], ["btf"])
                    tt("dve", bts[:, kr, :].rearrange("p (h q) -> p h q", h=8), btf[:].rearrange("p (h q) -> p h q", h=8),
                       c31[:].unsqueeze(2).to_broadcast([128, 8, 128]), ALU.subtract, ["btf", "c31"], ["bts"])
                xoT3 = xoT.rearrange("(c p) t -> p c t", p=128)
                YAT3 = YAT.rearrange("(c p) t -> p c t", p=128)
                vcount = [0]
                LO, W0, MID, WK, VV_, PW = 0, 1, 2, 3, 4, 5

                for m in range(mstart, nb):
                    os_ = slice(m * 128, (m + 1) * 128)
                    n5 = m + 1
                    nk = min(4 * (m + 1), nkcap)
                    n = 512 * (m + 1)
                    dma(xof[:], xoT3[:, :, os_], [], ["xof"])
                    cp("act", xob[:], xof[:], ["xof"], ["xob"])
                    for fc in range(4):
                        for dc in range(8):
                            mm(pb[0][:, fc * 128:(fc + 1) * 128], Wq[:, dc, fc * 128:(fc + 1) * 128], xob[:, dc, :],
                               dc == 0, dc == 7, ["Wq", "xob"], [pk[0]])
                    p03 = pb[0][:].rearrange("p (c n) -> p c n", c=4)
                    S.op("act", lambda e, o=qaT[0:64, :, 0, :], i=p03[0:64, :, :]: e.mul(out=o, in_=i, mul=0.125),
                         r=[pk[0]], w=["qaT"])
                    S.op("act", lambda e, o=qaT[64:128, :, 1, :], i=p03[64:128, :, :]: e.mul(out=o, in_=i, mul=0.125),
                         r=[pk[0]], w=["qaT"])
                    for h in range(4):
                        for dc in range(8):
                            mm(pb[2][0:64, h * 128:(h + 1) * 128], Wq[:, dc, 512 + h * 64:512 + (h + 1) * 64], xob[:, dc, :],
                               dc == 0, dc == 7, ["Wq", "xob"], [pk[2]])
                    cp("act", iqT[:].rearrange("p h n -> p (h n)"), pb[2][0:64, :], [pk[2]], ["iqT"])
                    for dc in range(8):
                        mm(pb[3][:, 0:4], xob[:, dc, :], Wq[:, dc, 768:772], dc == 0, dc == 7, ["Wq", "xob"], [pk[3]])
                    cp("dve", iwf[:], pb[3][:, 0:4], [pk[3]], ["iwf"])
                    act(aw[:], iwf[:], AF.Abs, ["iwf"], ["aw"], scale=1.0 / 16)
                    act(sgn[:], iwf[:], AF.Sign, ["iwf"], ["sgn"])
                    if bstage < 1:
                        continue
                    for c5 in range(n5):
                        ks = slice(c5 * 512, (c5 + 1) * 512)
                        for h in range(4):
                            mm(pb[h][:], iqT[:, h, :], IKs[:, ks], True, True, ["iqT", "IKs"], [pk[h]])
                        sck = "score%d" % c5
                        ts("dve", score[:, ks], tb[:], -1e-30 * 512 * c5, None, ALU.add, None, ["tb"], [sck])
                        if c5 == n5 - 1:
                            tt("dve", score[:, ks], score[:, ks], cm[:], ALU.add, [sck, "cm"], [sck])
                        for h in range(4):
                            rk = "rl%d" % h
                            act(rl[h][:], pb[h][:], AF.Relu, [pk[h], "aw"], [rk], scale=aw[:, h:h + 1])
                            stt("dve", score[:, ks], rl[h][:], sgn[:, h:h + 1], score[:, ks], ALU.mult, ALU.add,
                                [rk, "sgn", sck], [sck])
                    sckeys = ["score%d" % c5 for c5 in range(n5)]
                    if bstage < 2:
                        continue
                    nd = max(128, ((45 * n // 100) // 128) * 128)
                    na = n - nd
                    red("dve", bs[:, W0:W0 + 1], score[:, 0:n], ALU.max, sckeys, ["bs"])
                    ts("dve", bs[:, W0:W0 + 1], bs[:, W0:W0 + 1], 1e-3 - LO_INIT, None, ALU.add, None, ["bs"], ["bs"])
                    memset("dve", bs[:, LO:LO + 1], LO_INIT, ["bs"])
                    memset("dve", bs[:, 8:64], 0.0, ["bs", "bsa"])
                    for it in range(NBIS):
                        ts("dve", bs[:, WK:WK + 1], bs[:, W0:W0 + 1], 0.5 ** (it + 1), None, ALU.mult, None, ["bs"], ["bs"])
                        tt("dve", bs[:, MID:MID + 1], bs[:, LO:LO + 1], bs[:, WK:WK + 1], ALU.add, ["bs"], ["bs", "mid"])
                        ts("dve", junk[:, 0:nd], score[:, 0:nd], bs[:, MID:MID + 1], 0.0, ALU.is_ge, ALU.add,
                           sckeys + ["mid"], ["junk", "bs"], accum=bs[:, 8 + it:9 + it])
                        act(junkA[:, 0:na], score[:, nd:n], AF.Sign, sckeys + ["mid"], ["junkA", "bsa"],
                            bias=bs[:, MID:MID + 1], scale=-1.0, accum=bs[:, 36 + it:37 + it])
                        stt("dve", bs[:, VV_:VV_ + 1], bs[:, 8 + it:9 + it], 2.0, bs[:, 36 + it:37 + it], ALU.mult, ALU.subtract,
                            ["bs", "bsa"], ["bs"])
                        ts("dve", bs[:, PW:PW + 1], bs[:, VV_:VV_ + 1], 511.5 - na, bs[:, WK:WK + 1], ALU.is_ge, ALU.mult,
                           ["bs"], ["bs"])
                        tt("dve", bs[:, LO:LO + 1], bs[:, LO:LO + 1], bs[:, PW:PW + 1], ALU.add, ["bs", "mid"], ["bs"])
                    if bstage < 3:
                        continue
                    for hh in range(2):
                        mm(pb[4 + hh][:], zr[:, 0:128], zr[:], True, True, ["zr"], [pk[4 + hh]])
                    vslot = {}

                    def stage1(kc):
                        sl = kc % 2
                        kcs = slice(kc * 128, (kc + 1) * 128)
                        pS = (pb[2 * sl], pb[2 * sl + 1])
                        pSk = (pk[2 * sl], pk[2 * sl + 1])
                        kr = kc - (4 * m - 1)
                        near = 0 <= kr <= 4 and kc >= 0 and not nobias
                        if kc % 4 == 0:
                            vs_ = vcount[0] % 2
                            vcount[0] += 1
                            dma(Vc[vs_][:], VV3[:, kc:kc + 4, :], ["VV"], ["Vc%d" % vs_])
                            for k2 in range(kc, kc + 4):
                                vslot[k2] = vs_
                        ts("dve", mk[sl][:], score[:, kcs], bs[:, LO:LO + 1], None, ALU.is_ge, None, ["score%d" % (kc // 4), "bs"], ["mk%d" % sl])
                        tr(pT[:, sl * 128:(sl + 1) * 128], mk[sl][:], ident[:], ["mk%d" % sl, "ident"], ["pT%d" % sl])
                        for c4 in range(4):
                            reg = pS[c4 // 2][:, (c4 % 2) * 256:(c4 % 2 + 1) * 256]
                            mm(reg, KTs[:, c4, kcs], qaT[:, c4, :, :].rearrange("p a q -> p (a q)"), True, not near,
                               ["KTs", "qaT"], [pSk[c4 // 2]])
                            if near:
                                mm(reg, ident[:], bts[:, kr, c4 * 256:(c4 + 1) * 256], False, True,
                                   ["ident", "bts"], [pSk[c4 // 2]])
                        cp("act", mkT[sl][:], pT[:, sl * 128:(sl + 1) * 128], ["pT%d" % sl], ["mkT%d" % sl])
                        for hh in range(2):
                            act(Eb[sl][:, hh * 512:(hh + 1) * 512], pS[hh][:], AF.Exp, [pSk[hh]], ["Eb%d" % sl])
                        tt("dve", Pb[sl][:].rearrange("p (h q) -> p h q", h=8), Eb[sl][:].rearrange("p (h q) -> p h q", h=8),
                           mkT[sl][:].unsqueeze(1).to_broadcast([128, 8, 128]), ALU.mult,
                           ["Eb%d" % sl, "mkT%d" % sl], ["Pb%d" % sl])

                    def stage2(kc):
                        sl = kc % 2
                        vs_ = vslot[kc]
                        Vv = Vc[vs_][:].rearrange("p t (h d) -> p t h d", h=8)
                        for h in range(8):
                            mm(pb[4 + h // 4][:, (h % 4) * 128:(h % 4) * 128 + 65], Pb[sl][:, h * 128:(h + 1) * 128],
                               Vv[:, kc % 4, h, :], False, kc == nk - 1, ["Pb%d" % sl, "Vc%d" % vs_], [pk[4 + h // 4]])

                    stage1(0)
                    for kc in range(1, nk):
                        stage1(kc)
                        stage2(kc - 1)
                    stage2(nk - 1)
                    if bstage < 4:
                        continue
                    cp("act", osb[:, 0:260].rearrange("p (h d) -> p h d", h=4), pb[4][:].rearrange("p (h d) -> p h d", h=4)[:, :, 0:65],
                       [pk[4]], ["osbB"])
                    cp("act", osb[:, 260:520].rearrange("p (h d) -> p h d", h=4), pb[5][:].rearrange("p (h d) -> p h d", h=4)[:, :, 0:65],
                       [pk[5]], ["osbB"])
                    o4 = osb[:].rearrange("p (h d) -> p h d", h=8)
                    S.op("dve", lambda e, o=rs[:], i=o4[:, :, 64]: e.reciprocal(out=o, in_=i), r=["osbB"], w=["rsB"])
                    tt("dve", yab[:].rearrange("p (h d) -> p h d", h=8), o4[:, :, 0:64],
                       rs[:].unsqueeze(2).to_broadcast([128, 8, 64]), ALU.mult, ["osbB", "rsB"], ["yab"])
                    for fc in range(4):
                        tr(pT[:, 256 + fc * 128:256 + (fc + 1) * 128], yab[:, fc * 128:(fc + 1) * 128], ident[:],
                           ["yab", "ident"], ["pTy"])
                    cp("act", yaT[:].rearrange("p c n -> p (c n)"), pT[:, 256:768], ["pTy"], ["yaT"])
                    dma(YAT3[:, :, os_], yaT[:], ["yaT"], ["YAT"])
            S.barrier()

        if "C" in phases:
            with contextlib.ExitStack() as sc:
                Wg = SB(sc, "Wg", [128, 8, 2048], BF16)
                Wa = SB(sc, "Wa", [128, 4, 1024], BF16)
                Wr = SB(sc, "Wr", [128, 8, 1024], BF16)
                Wo = SB(sc, "Wo", [128, 8, 1024], BF16)
                stg = [SB(sc, "stgC%d" % i, [128, 2048], F32) for i in range(2)]
                g1 = SB(sc, "g1", [128, 1024], F32)
                b1 = SB(sc, "b1", [128, 1024], F32)
                xof = SB(sc, "xofC", [128, 8, 128], F32)
                xob = SB(sc, "xobC", [128, 8, 128], BF16)
                yaT = SB(sc, "yaTC", [128, 4, 128], BF16)
                yrT = SB(sc, "yrTC", [128, 8, 128], BF16)
                sg = SB(sc, "sg", [128, 2048], F32)
                hf_ = SB(sc, "hf", [128, 1024], F32)
                h2 = SB(sc, "h2", [128, 1024], F32)
                hb = SB(sc, "hb", [128, 1024], BF16)
                hT = SB(sc, "hT", [128, 8, 128], BF16)
                xres = SB(sc, "xres", [128, 1024], F32)
                pre = SB(sc, "pre", [128, 1024], F32)
                tmp = SB(sc, "tmpC", [128, 1024], F32)
                x1f = SB(sc, "x1f", [128, 1024], F32)
                x1b = SB(sc, "x1b", [128, 1024], BF16)
                x1T = SB(sc, "x1T", [128, 8, 128], BF16)
                stat = SB(sc, "statC", [128, 8], F32)
                load_w(sc, Wg, "Wg", w_g, 8, 2048, stg)
                load_w(sc, Wa, "Wa", w_a, 4, 1024, stg)
                load_w(sc, Wr, "Wr", w_r, 8, 1024, stg)
                load_w(sc, Wo, "Wo", w_o, 8, 1024, stg)
                dma(g1[:], ln1_g[:, :], [], ["lng"])
                dma(b1[:], ln1_b[:, :], [], ["lnb"])
                xoT3 = xoT.rearrange("(c p) t -> p c t", p=128)
                YAT3 = YAT.rearrange("(c p) t -> p c t", p=128)
                YRT3 = YRT.rearrange("(c p) t -> p c t", p=128)
                X1T3 = X1T.rearrange("(c p) t -> p c t", p=128)
                for m in range(16):
                    os_ = slice(m * 128, (m + 1) * 128)
                    dma(xof[:], xoT3[:, :, os_], [], ["xof"])
                    cp("act", xob[:], xof[:], ["xof"], ["xob"])
                    dma(yaT[:], YAT3[:, :, os_], ["YAT"], ["yaT"])
                    dma(yrT[:], YRT3[:, :, os_], ["YRT"], ["yrT"])
                    dma(xres[:], xo[os_, :], [], ["xres"])
                    for q4 in range(4):
                        for dc in range(8):
                            mm(pb[q4][:], xob[:, dc, :], Wg[:, dc, q4 * 512:(q4 + 1) * 512], dc == 0, dc == 7,
                               ["Wg", "xob"], [pk[q4]])
                        act(sg[:, q4 * 512:(q4 + 1) * 512], pb[q4][:], AF.Sigmoid, [pk[q4]], ["sg"])
                    for hh in range(2):
                        for fc in range(4):
                            mm(pb[4 + hh][:], yaT[:, fc, :], Wa[:, fc, hh * 512:(hh + 1) * 512], fc == 0, fc == 3,
                               ["Wa", "yaT"], [pk[4 + hh]])
                        tt("dve", hf_[:, hh * 512:(hh + 1) * 512], pb[4 + hh][:], sg[:, hh * 512:(hh + 1) * 512], ALU.mult,
                           [pk[4 + hh], "sg"], ["hf"])
                    for hh in range(2):
                        for fc in range(8):
                            mm(pb[hh][:], yrT[:, fc, :], Wr[:, fc, hh * 512:(hh + 1) * 512], fc == 0, fc == 7,
                               ["Wr", "yrT"], [pk[hh]])
                        tt("dve", h2[:, hh * 512:(hh + 1) * 512], pb[hh][:], sg[:, 1024 + hh * 512:1024 + (hh + 1) * 512],
                           ALU.mult, [pk[hh], "sg"], ["h2"])
                    tt("dve", hb[:], hf_[:], h2[:], ALU.add, ["hf", "h2"], ["hb"])
                    for fc in range(8):
                        tr(pT[:, fc * 128:(fc + 1) * 128], hb[:, fc * 128:(fc + 1) * 128], ident[:], ["hb", "ident"], ["pT"])
                    cp("act", hT[:].rearrange("p c n -> p (c n)"), pT[:, :], ["pT"], ["hT"])
                    for hh in range(2):
                        for fc in range(8):
                            mm(pb[2 + hh][:], hT[:, fc, :], Wo[:, fc, hh * 512:(hh + 1) * 512], fc == 0, fc == 7,
                               ["Wo", "hT"], [pk[2 + hh]])
                        stt("dve", pre[:, hh * 512:(hh + 1) * 512], xres[:, hh * 512:(hh + 1) * 512], ALPHA, pb[2 + hh][:],
                            ALU.mult, ALU.add, ["xres", pk[2 + hh]], ["pre"])
                    layer_norm_rows({"stat": stat}, pre, "pre", x1f, "x1f", g1, b1, tmp, "lnC")
                    dma(X1[os_, :], x1f[:], ["x1f"], ["X1"])
                    cp("act", x1b[:], x1f[:], ["x1f"], ["x1b"])
                    for fc in range(8):
                        tr(pT[:, fc * 128:(fc + 1) * 128], x1b[:, fc * 128:(fc + 1) * 128], ident[:], ["x1b", "ident"], ["pT"])
                    cp("act", x1T[:].rearrange("p c n -> p (c n)"), pT[:, :], ["pT"], ["x1T"])
                    dma(X1T3[:, :, os_], x1T[:], ["x1T"], ["X1T"])
            S.barrier()

        if "D" in phases:
            with contextlib.ExitStack() as sd:
                Wdn = SB(sd, "Wdn", [128, 32, 1024], BF16)
                HT = SB(sd, "HT", [128, 32, 1024], BF16)
                x1T = SB(sd, "x1TD", [128, 8, 1024], BF16)
                stg = [SB(sd, "stgD%d" % i, [128, 1024], F32) for i in range(3)]
                wub = [SB(sd, "wub%d" % i, [128, 8, 128], BF16) for i in range(2)]
                rl = [SB(sd, "rlD%d" % i, [128, 512], F32) for i in range(2)]
                g2 = SB(sd, "g2", [128, 1024], F32)
                b2 = SB(sd, "b2", [128, 1024], F32)
                x1r = SB(sd, "x1r", [128, 1024], F32)
                pre = SB(sd, "preD", [128, 1024], F32)
                tmp = SB(sd, "tmpD", [128, 1024], F32)
                of_ = SB(sd, "of", [128, 1024], F32)
                stat = SB(sd, "statD", [128, 8], F32)
                load_w(sd, Wdn, "Wdn", w_dn, 32, 1024, stg)
                dma(g2[:], ln2_g[:, :], [], ["lng"])
                dma(b2[:], ln2_b[:, :], [], ["lnb"])
                X1T3 = X1T.rearrange("(c p) t -> p c t", p=128)
                w_up3 = w_up.rearrange("(c p) f -> p c f", p=128)
                for th in range(2):
                    dma(x1T[:], X1T3[:, :, th * 1024:(th + 1) * 1024], ["X1T"], ["x1TD"])
                    for f in range(32):
                        ws = f % 2
                        wk = "wub%d" % ws
                        i = rr["stg"]
                        rr["stg"] = i + 1
                        sk = "stg%d" % (i % 3)
                        stile = stg[i % 3]
                        dma(stile[:].rearrange("p (c f) -> p c f", c=8), w_up3[:, :, f * 128:(f + 1) * 128], [], [sk])
                        cp("pool" if f % 2 == 0 else "dve", wub[ws][:].rearrange("p c f -> p (c f)"), stile[:], [sk], [wk])
                        for s2 in range(2):
                            bank = 2 * (f % 2) + s2
                            for dc in range(8):
                                mm(pb[bank][:], wub[ws][:, dc, :], x1T[:, dc, s2 * 512:(s2 + 1) * 512], dc == 0, dc == 7,
                                   [wk, "x1TD"], [pk[bank]])
                            rk = "rlD%d" % s2
                            act(rl[s2][:], pb[bank][:], AF.Relu, [pk[bank]], [rk])
                            tt("dve" if s2 == 0 else "pool", HT[:, f, s2 * 512:(s2 + 1) * 512], rl[s2][:], rl[s2][:], ALU.mult,
                               [rk], ["HT%d" % s2])
                    for tl in range(8):
                        tg = th * 8 + tl
                        os_ = slice(tg * 128, (tg + 1) * 128)
                        dma(x1r[:], X1[os_, :], ["X1"], ["x1r"])
                        for hh in range(2):
                            for f in range(32):
                                mm(pb[4 + hh][:], HT[:, f, tl * 128:(tl + 1) * 128], Wdn[:, f, hh * 512:(hh + 1) * 512],
                                   f == 0, f == 31, ["HT0", "HT1", "Wdn"], [pk[4 + hh]])
                            stt("dve", pre[:, hh * 512:(hh + 1) * 512], x1r[:, hh * 512:(hh + 1) * 512], ALPHA, pb[4 + hh][:],
                                ALU.mult, ALU.add, ["x1r", pk[4 + hh]], ["preD"])
                        layer_norm_rows({"stat": stat}, pre, "preD", of_, "of", g2, b2, tmp, "lnD")
                        dma(out[os_, :], of_[:], ["of"], ["out"])
        S.emit()
    return nc


def _t5_bucket(n):
    n = np.maximum(n, 0)
    nf = np.maximum(n, 1).astype(np.float32)
    large = 16 + (np.log(nf / np.float32(16)) / np.float32(math.log(128 / 16)) * np.float32(16)).astype(np.int32)
    large = np.minimum(large, 31)
    return np.where(n < 16, n, large)


def _consts(j):
    c = {}
    c["c_ident"] = np.eye(128, dtype=np.float32)
    half = 32
    inv = (np.float32(10000.0) ** (-np.arange(half, dtype=np.float32) / np.float32(half))).astype(np.float32)
    c["c_invf"] = np.broadcast_to(inv[None, :], (128, 32)).copy()
    H = 8
    gamma = (1.0 - 2.0 ** (-5.0 - np.arange(H, dtype=np.float64)))
    lg = np.log(gamma)
    nn = np.arange(128, dtype=np.float64)
    diff = nn[None, :] - nn[:, None]
    dT = np.where(diff[:, None, :] >= 0, np.exp(lg[None, :, None] * np.maximum(diff, 0)[:, None, :]), 0.0)
    c["c_decayT"] = dT.reshape(128, 1024).astype(np.float32)
    zeta = np.exp(lg[None, :] * (127.0 - nn[:, None]))
    c["c_zeta8"] = (zeta / 8.0).astype(np.float32)
    xi = np.exp(lg[:, None] * (nn[None, :] + 1.0))
    c["c_xiT"] = np.broadcast_to(xi.reshape(1, 1024), (64, 1024)).astype(np.float32).copy()
    g = np.exp(lg * 128.0)
    c["c_gmat"] = np.broadcast_to(np.repeat(g, 128)[None, :], (64, 1024)).astype(np.float32).copy()
    c["c_tb"] = np.broadcast_to((-1e-30 * np.arange(512, dtype=np.float64))[None, :], (128, 512)).astype(np.float32).copy()
    q = np.arange(128)[:, None]
    kk = np.arange(512)[None, :]
    c["c_cm"] = np.where(kk <= 128 * j + q, 0.0, MASKV).astype(np.float32)
    oh = np.zeros((128, 4), np.float32)
    oh[:, j] = 1.0
    c["c_oh"] = oh
    return c


def _bias_tiles(rel_bias, j):
    key = np.arange(128)[:, None, None]
    kr = np.arange(5)[None, :, None]
    q = np.arange(128)[None, None, :]
    dist = (j + 1 - kr) * 128 + q - key
    bucket = np.where(dist >= 0, _t5_bucket(dist), 31)
    bt = rel_bias[bucket]
    bt = np.transpose(bt, (0, 1, 3, 2))
    return np.ascontiguousarray(bt.reshape(128, 5 * 1024)).astype(np.float32)


_NC_CACHE = {}


def make_in_maps(x, positions, w_in, rel_bias, idx_k_ln_g, idx_k_ln_b, w_attn_branch, w_ret_branch,
                 w_out, ln_mix_g, ln_mix_b, w_up, w_down, ln_ffn_g, ln_ffn_b):
    x = np.asarray(x, np.float32)
    positions = np.asarray(positions, np.int32)
    w = np.asarray(w_in, np.float32)[0]
    rel_bias = np.asarray(rel_bias, np.float32)
    cs = lambda a, b: w[:, a:b]
    w_kv = np.ascontiguousarray(np.concatenate([cs(512, 1024), cs(1024, 1536), cs(1792, 1856), cs(2372, 2884), cs(2884, 3908)], axis=1))
    w_q = np.ascontiguousarray(np.concatenate([cs(0, 512), cs(1536, 1792), cs(1856, 1860)], axis=1))
    w_ro = np.ascontiguousarray(np.concatenate([cs(1860, 2372), cs(3908, 4932)], axis=1))
    w_g = np.ascontiguousarray(cs(4932, 6980))
    rep = lambda v: np.ascontiguousarray(np.broadcast_to(np.asarray(v, np.float32).reshape(1, -1), (128, 1024)))
    shared = {
        "w_kv": w_kv, "w_q": w_q, "w_ro": w_ro, "w_g": w_g,
        "w_a": np.ascontiguousarray(np.asarray(w_attn_branch, np.float32)[0]),
        "w_r": np.ascontiguousarray(np.asarray(w_ret_branch, np.float32)[0]),
        "w_o": np.ascontiguousarray(np.asarray(w_out, np.float32)[0]),
        "w_up": np.ascontiguousarray(np.asarray(w_up, np.float32)[0]),
        "w_dn": np.ascontiguousarray(np.asarray(w_down, np.float32)[0]),
        "lnk_g": np.ascontiguousarray(np.asarray(idx_k_ln_g, np.float32).reshape(64, 1)),
        "lnk_b": np.ascontiguousarray(np.asarray(idx_k_ln_b, np.float32).reshape(64, 1)),
        "ln1_g": rep(ln_mix_g), "ln1_b": rep(ln_mix_b), "ln2_g": rep(ln_ffn_g), "ln2_b": rep(ln_ffn_b),
        "c_c31": np.ascontiguousarray(np.broadcast_to(rel_bias[31][None, :], (128, 8))),
    }
    xTs = [np.ascontiguousarray(x[b].T) for b in range(2)]
    in_maps = []
    own_idx = []
    for c in range(8):
        b, j = c // 4, c % 4
        tok = (np.arange(16)[:, None] * 4 + j) * 128 + np.arange(128)[None, :]
        tok = tok.reshape(-1)
        own_idx.append((b, tok))
        d = dict(shared)
        d["xT"] = xTs[b]
        d["xoT"] = np.ascontiguousarray(xTs[b][:, tok])
        d["xo"] = np.ascontiguousarray(x[b][tok])
        d["posT"] = np.ascontiguousarray(positions[b].reshape(64, 128).T).astype(np.float32)
        d["posoT"] = np.ascontiguousarray(positions[b][tok].reshape(16, 128).T).astype(np.float32)
        d.update(_consts(j))
        d["c_bt"] = _bias_tiles(rel_bias, j)
        in_maps.append(d)
    return in_maps, own_idx


def kernel(**inputs):
    in_maps, own_idx = make_in_maps(**inputs)
    if "nc" not in _NC_CACHE:
        _NC_CACHE["nc"] = build()
    nc = _NC_CACHE["nc"]
    res = run_bass_kernel_spmd(nc, in_maps, core_ids=list(range(8)))
    outp = np.zeros((2, 8192, 1024), np.float32)
    for c in range(8):
        b, tok = own_idx[c]
        outp[b, tok] = res.results[c]["out"]
    return outp
```

```python
import contextlib
import math
import numpy as np
import concourse.bass as bass
import concourse.mybir as mybir
from concourse.bass_utils import run_bass_kernel_spmd

F32 = mybir.dt.float32
BF16 = mybir.dt.bfloat16
I32 = mybir.dt.int32
ALU = mybir.AluOpType
AF = mybir.ActivationFunctionType
AX = mybir.AxisListType

NBIS = 19
LO_INIT = -60.0
MASKV = -30000.0
ALPHA = 2.0 ** 0.25
EPS = 1e-5
PI = math.pi


class Sched:
    CE = ("pe", "act", "dve", "pool")
    ALLE = ("pe", "act", "dve", "pool", "sp")

    def __init__(self, nc, stack, n_dma=(("sp", 24), ("act", 8))):
        self.nc = nc
        self.ops = {e: [] for e in self.ALLE}
        self.cnt = {e: 0 for e in self.CE}
        self.sem = {}
        for e in self.CE:
            self.sem["c_" + e] = stack.enter_context(nc.semaphore("c_" + e))
        self.dma_slots = {}
        self.dma_next = {}
        self.dma_val = {}
        for e, n in n_dma:
            self.dma_slots[e] = []
            for i in range(n):
                k = "d_%s_%d" % (e, i)
                self.sem[k] = stack.enter_context(nc.semaphore(k))
                self.dma_slots[e].append(k)
                self.dma_val[k] = 0
            self.dma_next[e] = 0
        self.buf = {}
        self.waited = {e: {} for e in self.ALLE}
        self.pending = {e: {} for e in self.ALLE}
        self.nops = 0

    def _st(self, k):
        s = self.buf.get(k)
        if s is None:
            s = {"w": None, "r": {}}
            self.buf[k] = s
        return s

    def op(self, eng, fn, r=(), w=(), dma=False):
        waits = dict(self.pending[eng])
        self.pending[eng] = {}

        def add(ev):
            if ev is None:
                return
            k, v = ev
            if waits.get(k, 0) < v:
                waits[k] = v

        for k in r:
            add(self._st(k)["w"])
        for k in w:
            s = self._st(k)
            add(s["w"])
            for ev in s["r"].items():
                add(ev)
        if dma:
            slots = self.dma_slots[eng]
            k = slots[self.dma_next[eng] % len(slots)]
            self.dma_next[eng] += 1
            if self.dma_val[k] > 0:
                add((k, self.dma_val[k]))
            self.dma_val[k] += 16
            ev = (k, self.dma_val[k])
            inc = 16
        else:
            self.cnt[eng] += 1
            ev = ("c_" + eng, self.cnt[eng])
            inc = 1
        wl = []
        own = "c_" + eng
        for k, v in waits.items():
            if k == own and eng == "pe":
                continue
            if self.waited[eng].get(k, 0) >= v:
                continue
            self.waited[eng][k] = v
            wl.append((k, v))
        self.ops[eng].append((wl, fn, ev[0], inc))
        self.nops += 1
        for k in w:
            s = self._st(k)
            s["w"] = ev
            s["r"] = {}
        for k in r:
            if k in w:
                continue
            s = self._st(k)
            if s["r"].get(ev[0], 0) < ev[1]:
                s["r"][ev[0]] = ev[1]
        return ev

    def all_events(self):
        evs = {}
        for e in self.CE:
            if self.cnt[e] > 0:
                evs["c_" + e] = self.cnt[e]
        for k, v in self.dma_val.items():
            if v > 0:
                evs[k] = v
        return evs

    def barrier(self):
        evs = self.all_events()
        for e in self.ALLE:
            for k, v in evs.items():
                if self.pending[e].get(k, 0) < v:
                    self.pending[e][k] = v
        self.buf = {}

    def emit(self):
        nc = self.nc
        final = self.all_events()
        with nc.Block() as block:
            def run(eng_name, e, is_last=False):
                for wl, fn, sk, inc in self.ops[eng_name]:
                    for k, v in wl:
                        e.wait_ge(self.sem[k], v)
                    ins = fn(e)
                    ins.then_inc(self.sem[sk], inc)
                if is_last:
                    for k, v in final.items():
                        e.wait_ge(self.sem[k], v)

            @block.tensor
            def _(e):
                run("pe", e)

            @block.scalar
            def _(e):
                run("act", e)

            @block.vector
            def _(e):
                run("dve", e)

            @block.gpsimd
            def _(e):
                run("pool", e)

            @block.sync
            def _(e):
                run("sp", e, True)


def build(phases="ABCD", debug=False, ng=16, own=True, alltok=True, ostage=99, nb=16, bstage=99, mstart=0, nkcap=999, nobias=False, dummy=0):
    nc = bass.Bass("TRN2", target_bir_lowering=False)

    def din(name, shape, dt=F32):
        return nc.dram_tensor(name, shape, dt, kind="ExternalInput").ap()

    def dscr(name, shape, dt):
        kind = "ExternalOutput" if debug else "Internal"
        return nc.dram_tensor(name, shape, dt, kind=kind).ap()

    xT = din("xT", [1024, 8192])
    xoT = din("xoT", [1024, 2048])
    xo = din("xo", [2048, 1024])
    posT = din("posT", [128, 64], F32)
    posoT = din("posoT", [128, 16], F32)
    w_kv = din("w_kv", [1024, 2624])
    w_q = din("w_q", [1024, 772])
    w_ro = din("w_ro", [1024, 1536])
    w_g = din("w_g", [1024, 2048])
    w_a = din("w_a", [512, 1024])
    w_r = din("w_r", [1024, 1024])
    w_o = din("w_o", [1024, 1024])
    w_up = din("w_up", [1024, 4096])
    w_dn = din("w_dn", [4096, 1024])
    lnk_g = din("lnk_g", [64, 1])
    lnk_b = din("lnk_b", [64, 1])
    ln1_g = din("ln1_g", [128, 1024])
    ln1_b = din("ln1_b", [128, 1024])
    ln2_g = din("ln2_g", [128, 1024])
    ln2_b = din("ln2_b", [128, 1024])
    c_ident = din("c_ident", [128, 128])
    c_invf = din("c_invf", [128, 32])
    c_decayT = din("c_decayT", [128, 1024])
    c_zeta8 = din("c_zeta8", [128, 8])
    c_xiT = din("c_xiT", [64, 1024])
    c_gmat = din("c_gmat", [64, 1024])
    c_tb = din("c_tb", [128, 512])
    c_cm = din("c_cm", [128, 512])
    c_bt = din("c_bt", [128, 5 * 1024])
    c_c31 = din("c_c31", [128, 8])
    c_oh = din("c_oh", [128, 4])
    out = nc.dram_tensor("out", [2048, 1024], F32, kind="ExternalOutput").ap()

    KT = dscr("s_KT", [512, 8192], BF16)
    VV = dscr("s_V", [64, 128, 520], BF16)
    IKT = dscr("s_IKT", [64, 8192], BF16)
    YAT = dscr("s_YAT", [512, 2048], BF16)
    YRT = dscr("s_YRT", [1024, 2048], BF16)
    X1 = dscr("s_X1", [2048, 1024], F32)
    X1T = dscr("s_X1T", [1024, 2048], BF16)

    with contextlib.ExitStack() as st0:
        S = Sched(nc, st0)
        rr = {"cast": 0}

        def SB(stk, name, shape, dt):
            return stk.enter_context(nc.sbuf_tensor(name, shape, dt))

        def PS(stk, name, shape, dt):
            return stk.enter_context(nc.psum_tensor(name, shape, dt))

        def dma(out_ap, in_ap, r, w, eng="sp"):
            S.op(eng, lambda e, o=out_ap, i=in_ap: e.dma_start(out=o, in_=i), r=r, w=w, dma=True)

        def mm(out_ap, lhsT, rhs, start, stop, r, w):
            S.op("pe", lambda e, o=out_ap, l=lhsT, rh=rhs, s0=start, s1=stop: e.matmul(o, lhsT=l, rhs=rh, start=s0, stop=s1),
                 r=r, w=w)

        def tr(out_ap, in_ap, ident_ap, r, w):
            S.op("pe", lambda e, o=out_ap, i=in_ap, d=ident_ap: e.transpose(out=o, in_=i, identity=d), r=r, w=w)

        def act(out_ap, in_ap, func, r, w, bias=None, scale=None, accum=None):
            kw = {}
            if bias is not None:
                kw["bias"] = bias
            if scale is not None:
                kw["scale"] = scale
            if accum is not None:
                kw["accum_out"] = accum
            S.op("act", lambda e, o=out_ap, i=in_ap, f=func, kw=kw: e.activation(out=o, in_=i, func=f, **kw), r=r, w=w)

        def tt(eng, out_ap, in0, in1, op, r, w):
            S.op(eng, lambda e, o=out_ap, a=in0, b=in1, p=op: e.tensor_tensor(out=o, in0=a, in1=b, op=p), r=r, w=w)

        def ts(eng, out_ap, in0, s1, s2, op0, op1, r, w, accum=None):
            kw = {}
            if op1 is not None:
                kw["op1"] = op1
            if accum is not None:
                kw["accum_out"] = accum
            S.op(eng, lambda e, o=out_ap, a=in0, x1=s1, x2=s2, p0=op0, kw=kw:
                 e.tensor_scalar(out=o, in0=a, scalar1=x1, scalar2=x2, op0=p0, **kw), r=r, w=w)

        def stt(eng, out_ap, in0, scalar, in1, op0, op1, r, w):
            S.op(eng, lambda e, o=out_ap, a=in0, s=scalar, b=in1, p0=op0, p1=op1:
                 e.scalar_tensor_tensor(out=o, in0=a, scalar=s, in1=b, op0=p0, op1=p1), r=r, w=w)

        def cp(eng, out_ap, in_ap, r, w):
            if eng == "act":
                act(out_ap, in_ap, AF.Copy, r, w)
            else:
                S.op(eng, lambda e, o=out_ap, i=in_ap: e.tensor_copy(out=o, in_=i), r=r, w=w)

        def red(eng, out_ap, in_ap, op, r, w):
            S.op(eng, lambda e, o=out_ap, i=in_ap, p=op: e.tensor_reduce(out=o, in_=i, axis=AX.X, op=p), r=r, w=w)

        def memset(eng, ap, val, w):
            S.op(eng, lambda e, a=ap, v=val: e.memset(a, v), w=w)

        ident_f = SB(st0, "ident_f", [128, 128], F32)
        ident = SB(st0, "ident", [128, 128], BF16)
        dma(ident_f[:], c_ident[:, :], [], ["ident_f"])
        cp("dve", ident[:], ident_f[:], ["ident_f"], ["ident"])
        epsT = SB(st0, "epsT", [128, 1], F32)
        memset("dve", epsT[:], EPS, ["epsT"])

        def rstd(out_ap, var_ap, r, w):
            act(out_ap, var_ap, AF.Sqrt, list(r) + ["epsT"], list(w), bias=epsT[:, 0:1], scale=1.0)
            S.op("dve", lambda e, o=out_ap: e.reciprocal(out=o, in_=o), r=list(w), w=list(w))

        pb = [PS(st0, "pb%d" % i, [128, 512], F32) for i in range(7)]
        pT = PS(st0, "pT", [128, 1024], BF16)
        pk = ["pb%d" % i for i in range(7)]

        def cast_eng():
            rr["cast"] += 1
            return ("act", "dve", "act", "dve", "pool")[rr["cast"] % 5]

        def load_w(stk_stage, dst, dst_key, src, nrow_chunks, cols, stg, col0=0):
            for rc in range(nrow_chunks):
                pw_ = stg[0].shape[1]
                for c0 in range(0, cols, pw_):
                    cw = min(pw_, cols - c0)
                    i = rr.setdefault("stg", 0)
                    rr["stg"] = i + 1
                    sk = "stg%d" % (i % len(stg))
                    stile = stg[i % len(stg)]
                    dma(stile[:, 0:cw], src[rc * 128:(rc + 1) * 128, c0:c0 + cw], [], [sk])
                    cp(cast_eng(), dst[:, rc, col0 + c0:col0 + c0 + cw], stile[:, 0:cw], [sk], [dst_key])

        def layer_norm_rows(stk, pre, pre_key, outf, out_key, g_t, b_t, tmp, nm):
            s1 = nm + "_s1"
            st_t = stk["stat"]
            red("dve", st_t[:, 0:1], pre[:], ALU.add, [pre_key], [s1])
            act(tmp[:], pre[:], AF.Square, [pre_key], [nm + "_tmp", s1 + "q"], accum=st_t[:, 1:2])
            ts("dve", st_t[:, 2:3], st_t[:, 0:1], 1.0 / 1024, None, ALU.mult, None, [s1], [s1 + "m"])
            ts("dve", st_t[:, 3:4], st_t[:, 1:2], 1.0 / 1024, None, ALU.mult, None, [s1 + "q"], [s1 + "e"])
            tt("dve", st_t[:, 4:5], st_t[:, 2:3], st_t[:, 2:3], ALU.mult, [s1 + "m"], [s1 + "mm"])
            tt("dve", st_t[:, 5:6], st_t[:, 3:4], st_t[:, 4:5], ALU.subtract, [s1 + "e", s1 + "mm"], [s1 + "v"])
            rstd(st_t[:, 6:7], st_t[:, 5:6], [s1 + "v"], [s1 + "r"])
            ts("dve", pre[:], pre[:], st_t[:, 2:3], st_t[:, 6:7], ALU.subtract, ALU.mult, [pre_key, s1 + "m", s1 + "r"], [pre_key])
            tt("dve", pre[:], pre[:], g_t[:], ALU.mult, [pre_key, "lng"], [pre_key])
            tt("dve", outf[:], pre[:], b_t[:], ALU.add, [pre_key, "lnb"], [out_key])

        if "A" in phases:
            with contextlib.ExitStack() as sa:
                Wkv = SB(sa, "Wkv", [128, 8, 2624], BF16)
                Wro = SB(sa, "Wro", [128, 8, 1536], BF16)
                stg = [SB(sa, "stgA%d" % i, [128, 1024], F32) for i in range(2)]
                xgf2 = [SB(sa, "xgf%d" % i, [128, 8, 512], F32) for i in range(2)]
                xgb = SB(sa, "xgb", [128, 8, 512], BF16)
                posf = SB(sa, "posf", [128, 64], F32)
                posof = SB(sa, "posof", [128, 16], F32)
                invf = SB(sa, "invf", [128, 32], F32)
                kT_sb = SB(sa, "kT_sb", [128, 4, 512], BF16)
                v_sb = SB(sa, "v_sb", [128, 4, 8, 65], BF16)
                ikT_sb = SB(sa, "ikT_sb", [64, 512], BF16)
                ik_f4 = SB(sa, "ik_f4", [128, 256], F32)
                ik_q4 = SB(sa, "ik_q4", [128, 256], F32)
                ik_n4 = SB(sa, "ik_n4", [128, 256], BF16)
                stat4 = SB(sa, "stat4", [128, 8, 4], F32)
                stat = SB(sa, "statA", [128, 8], F32)
                lng = SB(sa, "lnkg", [64, 1], F32)
                lnb = SB(sa, "lnkb", [64, 1], F32)
                kz2 = [SB(sa, "kz%d" % i, [128, 512], BF16) for i in range(2)]
                vr2 = [SB(sa, "vr%d" % i, [128, 1024], BF16) for i in range(2)]
                rA = SB(sa, "rA", [128, 512], F32)
                rB = SB(sa, "rB", [128, 512], F32)
                rO = SB(sa, "rO", [128, 512], F32)
                zeta8 = SB(sa, "zeta8", [128, 8], F32)
                R = SB(sa, "R", [64, 1024], F32)
                Rsel = SB(sa, "Rsel", [64, 1024], F32)
                Rselb = SB(sa, "Rselb", [64, 1024], BF16)
                gmat = SB(sa, "gmat", [64, 1024], F32)
                oh = SB(sa, "oh", [128, 4], F32)
                xiT = SB(sa, "xiT", [64, 1024], F32)
                decayT = SB(sa, "decayT", [128, 1024], F32)
                xof = SB(sa, "xof", [128, 8, 128], F32)
                xob = SB(sa, "xob", [128, 8, 128], BF16)
                qrb = SB(sa, "qrb", [128, 512], BF16)
                krb = SB(sa, "krb", [128, 512], BF16)
                qT = SB(sa, "qT", [64, 8, 128], BF16)
                qxiT = SB(sa, "qxiT", [64, 8, 128], BF16)
                kTo = SB(sa, "kTo", [64, 8, 128], BF16)
                vro = SB(sa, "vro", [128, 1024], BF16)
                sgr = SB(sa, "sgr", [128, 1024], F32)
                Dm = SB(sa, "Dm", [128, 1024], BF16)
                osb = SB(sa, "osb", [128, 1024], F32)
                osq = SB(sa, "osq", [128, 1024], F32)
                hst = SB(sa, "hst", [128, 8, 8], F32)
                yrb = SB(sa, "yrb", [128, 1024], BF16)
                yrT = SB(sa, "yrT", [128, 8, 128], BF16)

                load_w(sa, Wkv, "Wkv", w_kv, 8, 2624, stg)
                load_w(sa, Wro, "Wro", w_ro, 8, 1536, stg)
                dma(posf[:], posT[:, :], [], ["posf"])
                dma(posof[:], posoT[:, :], [], ["posof"])
                dma(invf[:], c_invf[:, :], [], ["invf"])
                dma(lng[:], lnk_g[:, :], [], ["lnkg"])
                dma(lnb[:], lnk_b[:, :], [], ["lnkb"])
                dma(zeta8[:], c_zeta8[:, :], [], ["zeta8"])
                dma(gmat[:], c_gmat[:, :], [], ["gmat"])
                dma(oh[:], c_oh[:, :], [], ["oh"])
                dma(xiT[:], c_xiT[:, :], [], ["xiT"])
                dma(decayT[:], c_decayT[:, :], [], ["decayT"])
                memset("pool", R[:], 0.0, ["R"])
                memset("pool", v_sb[:], 1.0, ["v_sb"])

                posall = SB(sa, "posall", [128, 16, 5], F32)
                ang5 = SB(sa, "ang5", [128, 5, 2, 32], F32)
                angk5 = SB(sa, "angk5", [128, 5, 2, 32], F32)
                CC5 = SB(sa, "CC5", [128, 5, 64], F32)
                SS5 = SB(sa, "SS5", [128, 5, 64], F32)
                cp("dve", posall[:, :, 0:4], posf[:].rearrange("p (m i) -> p m i", i=4), ["posf"], ["posall"])
                cp("dve", posall[:, :, 4:5], posof[:].unsqueeze(2), ["posof"], ["posall"])

                def cos_sin_group(m):
                    MG = 12582912.0
                    a5 = ang5[:].rearrange("p t a i -> p (t a i)")
                    k5 = angk5[:].rearrange("p t a i -> p (t a i)")
                    tt("dve", ang5[:, :, 0, :], invf[:].unsqueeze(1).to_broadcast([128, 5, 32]),
                       posall[:, m, :].unsqueeze(2).to_broadcast([128, 5, 32]), ALU.mult, ["invf", "posall"], ["ang"])
                    ts("dve", ang5[:, :, 1, :], ang5[:, :, 0, :], 0.5 * PI, None, ALU.add, None, ["ang"], ["ang"])
                    ts("dve", k5, a5, 1.0 / (2 * PI), MG, ALU.mult, ALU.add, ["ang"], ["angk"])
                    ts("dve", k5, k5, -MG, None, ALU.add, None, ["angk"], ["angk"])
                    stt("dve", a5, k5, -2 * PI, a5, ALU.mult, ALU.add, ["angk", "ang"], ["ang"])
                    ts("dve", a5, a5, 3.1415925, -3.1415925, ALU.min, ALU.max, ["ang"], ["ang"])
                    act(SS5[:, :, 0:32], ang5[:, :, 0, :], AF.Sin, ["ang"], ["SS"])
                    act(SS5[:, :, 32:64], ang5[:, :, 0, :], AF.Sin, ["ang"], ["SS"], scale=-1.0)
                    act(CC5[:, :, 0:32], ang5[:, :, 1, :], AF.Sin, ["ang"], ["CC"])
                    act(CC5[:, :, 32:64], ang5[:, :, 1, :], AF.Sin, ["ang"], ["CC"])

                qf = SB(sa, "qf", [64, 1024], F32)

                def rope(src_ps, src_key, dst, ti):
                    s3 = src_ps.rearrange("p (h d) -> p h d", h=8)
                    a3 = rA[:].rearrange("p (h d) -> p h d", h=8)
                    b3 = rB[:].rearrange("p (h d) -> p h d", h=8)
                    o3 = dst[:].rearrange("p (h d) -> p h d", h=8)
                    ccb = CC5[:, ti, :].unsqueeze(1).to_broadcast([128, 8, 64])
                    ssb = SS5[:, ti, :].unsqueeze(1).to_broadcast([128, 8, 64])
                    tt("dve", a3, s3, ccb, ALU.mult, [src_key, "CC"], ["rA"])
                    tt("dve", b3, s3, ssb, ALU.mult, [src_key, "SS"], ["rB"])
                    tt("pool", o3[:, :, 0:32], a3[:, :, 0:32], b3[:, :, 32:64], ALU.add, ["rA", "rB"], ["rO"])
                    tt("pool", o3[:, :, 32:64], a3[:, :, 32:64], b3[:, :, 0:32], ALU.add, ["rA", "rB"], ["rO"])

                xT3 = xT.rearrange("(c p) t -> p c t", p=128)
                xoT3 = xoT.rearrange("(c p) t -> p c t", p=128)
                KT3 = KT.rearrange("(c p) t -> p c t", p=128)
                YRT3 = YRT.rearrange("(c p) t -> p c t", p=128)
                VV3 = VV.rearrange("t p f -> p t f")

                for m in range(ng):
                    if not alltok:
                        break
                    xgf = xgf2[m % 2]
                    xgk = "xgf%d" % (m % 2)
                    if m == 0:
                        dma(xgf[:], xT3[:, :, 0:512], [], [xgk])
                    cp("dve", xgb[:, 0:4, :], xgf[:, 0:4, :], [xgk], ["xgb"])
                    cp("act", xgb[:, 4:8, :], xgf[:, 4:8, :], [xgk], ["xgb"])
                    if m + 1 < ng:
                        dma(xgf2[(m + 1) % 2][:], xT3[:, :, (m + 1) * 512:(m + 2) * 512], [], ["xgf%d" % ((m + 1) % 2)])
                    pend = [None]

                    def flushU():
                        if pend[0] is None:
                            return
                        ps = pend[0]
                        pend[0] = None
                        kzp, vrp = kz2[ps], vr2[ps]
                        for h in range(8):
                            mm(pb[5 + h // 4][0:64, (h % 4) * 128:(h % 4 + 1) * 128], kzp[:, h * 64:(h + 1) * 64],
                               vrp[:, h * 128:(h + 1) * 128], True, True, ["kz%d" % ps, "vr%d" % ps], [pk[5 + h // 4]])
                        tt("dve", R[:], R[:], gmat[:], ALU.mult, ["R", "gmat"], ["R"])
                        tt("dve", R[:, 0:512], R[:, 0:512], pb[5][0:64, :], ALU.add, ["R", pk[5]], ["R"])
                        tt("dve", R[:, 512:1024], R[:, 512:1024], pb[6][0:64, :], ALU.add, ["R", pk[6]], ["R"])
                    for fc in range(4):
                        for dc in range(8):
                            mm(pb[0][:], Wkv[:, dc, fc * 128:(fc + 1) * 128], xgb[:, dc, :], dc == 0, dc == 7,
                               ["Wkv", "xgb"], [pk[0]])
                        cp("act", kT_sb[:, fc, :], pb[0][:], [pk[0]], ["kT_sb"])
                    dma(KT3[:, :, m * 512:(m + 1) * 512], kT_sb[:], ["kT_sb"], ["KT"])
                    cos_sin_group(m)
                    for i in range(4):
                        for dc in range(8):
                            mm(pb[0][:, i * 64:(i + 1) * 64], xgb[:, dc, i * 128:(i + 1) * 128], Wkv[:, dc, 1024:1088], dc == 0, dc == 7,
                               ["Wkv", "xgb"], [pk[0]])
                    ikf3 = ik_f4[:].rearrange("p (t d) -> p t d", t=4)
                    ikq3 = ik_q4[:].rearrange("p (t d) -> p t d", t=4)
                    cp("dve", ik_f4[:], pb[0][:, 0:256], [pk[0]], ["ik_f"])
                    red("dve", stat4[:, 0, :], ikf3, ALU.add, ["ik_f"], ["st0"])
                    act(ik_q4[:], ik_f4[:], AF.Square, ["ik_f"], ["ik_q"])
                    red("dve", stat4[:, 1, :], ikq3, ALU.add, ["ik_q"], ["st1"])
                    ts("dve", stat4[:, 2, :], stat4[:, 0, :], 1.0 / 64, None, ALU.mult, None, ["st0"], ["st2"])
                    ts("dve", stat4[:, 3, :], stat4[:, 1, :], 1.0 / 64, None, ALU.mult, None, ["st1"], ["st3"])
                    tt("dve", stat4[:, 4, :], stat4[:, 2, :], stat4[:, 2, :], ALU.mult, ["st2"], ["st4"])
                    tt("dve", stat4[:, 5, :], stat4[:, 3, :], stat4[:, 4, :], ALU.subtract, ["st3", "st4"], ["st5"])
                    rstd(stat4[:, 6, :], stat4[:, 5, :], ["st5"], ["st6"])
                    tt("dve", ikq3, ikf3, stat4[:, 2, :].unsqueeze(2).to_broadcast([128, 4, 64]), ALU.subtract,
                       ["ik_f", "st2", "ik_q"], ["ik_q"])
                    tt("dve", ik_n4[:].rearrange("p (t d) -> p t d", t=4), ikq3,
                       stat4[:, 6, :].unsqueeze(2).to_broadcast([128, 4, 64]), ALU.mult, ["ik_q", "st6"], ["ik_n"])
                    for i in range(4):
                        tr(pT[0:64, i * 128:(i + 1) * 128], ik_n4[:, i * 64:(i + 1) * 64], ident[:], ["ik_n", "ident"], ["pT"])
                    act(ikT_sb[:], pT[0:64, 0:512], AF.Identity, ["pT", "lnkg", "lnkb"], ["ikT_sb"],
                        bias=lnb[:, 0:1], scale=lng[:, 0:1])
                    for i in range(4):
                        t = 4 * m + i
                        xs = slice(i * 128, (i + 1) * 128)
                        for dc in range(8):
                            mm(pb[1][:], xgb[:, dc, xs], Wkv[:, dc, 512:1024], dc == 0, dc == 7, ["Wkv", "xgb"], [pk[1]])
                        cp("act", v_sb[:, i, :, 0:64], pb[1][:].rearrange("p (h d) -> p h d", h=8), [pk[1]], ["v_sb"])
                        for dc in range(8):
                            mm(pb[2][:], xgb[:, dc, xs], Wkv[:, dc, 1088:1600], dc == 0, dc == 7, ["Wkv", "xgb"], [pk[2]])
                        for hf in range(2):
                            for dc in range(8):
                                mm(pb[3 + hf][:], xgb[:, dc, xs], Wkv[:, dc, 1600 + hf * 512:2112 + hf * 512],
                                   dc == 0, dc == 7, ["Wkv", "xgb"], [pk[3 + hf]])
                            cp("act", vr2[t % 2][:, hf * 512:(hf + 1) * 512], pb[3 + hf][:], [pk[3 + hf]], ["vr%d" % (t % 2)])
                        flushU()
                        rope(pb[2][:], pk[2], rO, i)
                        tt("dve", kz2[t % 2][:].rearrange("p (h d) -> p h d", h=8), rO[:].rearrange("p (h d) -> p h d", h=8),
                           zeta8[:].unsqueeze(2).to_broadcast([128, 8, 64]), ALU.mult, ["rO", "zeta8"], ["kz%d" % (t % 2)])
                        if i == 0:
                            ts("dve", Rsel[:], R[:], oh[0:64, 0:1], None, ALU.mult, None, ["R", "oh"], ["Rsel"])
                        else:
                            stt("dve", Rsel[:], R[:], oh[0:64, i:i + 1], Rsel[:], ALU.mult, ALU.add, ["R", "oh", "Rsel"], ["Rsel"])
                        pend[0] = t % 2
                    flushU()
                    dma(VV3[:, 4 * m:4 * m + 4, :], v_sb[:].rearrange("p t h d -> p t (h d)"), ["v_sb"], ["VV"])
                    dma(IKT[:, m * 512:(m + 1) * 512], ikT_sb[:], ["ikT_sb"], ["IKT"])
                    if not own:
                        continue

                    os_ = slice(m * 128, (m + 1) * 128)
                    dma(xof[:], xoT3[:, :, os_], [], ["xof"])
                    cp("act", xob[:], xof[:], ["xof"], ["xob"])
                    cp("act", Rselb[:], Rsel[:], ["Rsel"], ["Rselb"])
                    for dc in range(8):
                        mm(pb[1][:], xob[:, dc, :], Wro[:, dc, 0:512], dc == 0, dc == 7, ["Wro", "xob"], [pk[1]])
                    rope(pb[1][:], pk[1], rO, 4)
                    cp("act", qrb[:], rO[:], ["rO"], ["qrb"])
                    for dc in range(8):
                        mm(pb[2][:], xob[:, dc, :], Wkv[:, dc, 1088:1600], dc == 0, dc == 7, ["Wkv", "xob"], [pk[2]])
                    rope(pb[2][:], pk[2], rO, 4)
                    S.op("act", lambda e, o=krb[:], i=rO[:]: e.mul(out=o, in_=i, mul=0.125), r=["rO"], w=["krb"])
                    for hf in range(2):
                        for dc in range(8):
                            mm(pb[3 + hf][:], xob[:, dc, :], Wkv[:, dc, 1600 + hf * 512:2112 + hf * 512],
                               dc == 0, dc == 7, ["Wkv", "xob"], [pk[3 + hf]])
                        cp("act", vro[:, hf * 512:(hf + 1) * 512], pb[3 + hf][:], [pk[3 + hf]], ["vro"])
                    for hf in range(2):
                        for dc in range(8):
                            mm(pb[5 + hf][:], xob[:, dc, :], Wro[:, dc, 512 + hf * 512:1024 + hf * 512],
                               dc == 0, dc == 7, ["Wro", "xob"], [pk[5 + hf]])
                        act(sgr[:, hf * 512:(hf + 1) * 512], pb[5 + hf][:], AF.Sigmoid, [pk[5 + hf]], ["sgr"])
                        tt("dve", sgr[:, hf * 512:(hf + 1) * 512], sgr[:, hf * 512:(hf + 1) * 512], pb[5 + hf][:], ALU.mult,
                           ["sgr", pk[5 + hf]], ["sgr"])
                    if ostage < 1:
                        continue
                    for h in range(8):
                        tr(pT[0:64, h * 128:(h + 1) * 128], qrb[:, h * 64:(h + 1) * 64], ident[:], ["qrb", "ident"], ["pT"])
                    cp("act", qf[:], pT[0:64, :], ["pT"], ["qf"])
                    cp("pool", qT[:].rearrange("p h n -> p (h n)"), qf[:], ["qf"], ["qT"])
                    tt("dve", qxiT[:].rearrange("p h n -> p (h n)"), qf[:], xiT[:], ALU.mult, ["qf", "xiT"], ["qxiT"])
                    for h in range(8):
                        tr(pT[0:64, h * 128:(h + 1) * 128], krb[:, h * 64:(h + 1) * 64], ident[:], ["krb", "ident"], ["pT"])
                    cp("act", kTo[:].rearrange("p h n -> p (h n)"), pT[0:64, :], ["pT"], ["kTo"])
                    if ostage < 2:
                        continue
                    for h in range(8):
                        mm(pb[3 + h // 4][:, (h % 4) * 128:(h % 4 + 1) * 128], kTo[:, h, :], qT[:, h, :], True, True,
                           ["kTo", "qT"], [pk[3 + h // 4]])
                    tt("dve", Dm[:, 0:512], pb[3][:], decayT[:, 0:512], ALU.mult, [pk[3], "decayT"], ["Dm"])
                    tt("dve", Dm[:, 512:1024], pb[4][:], decayT[:, 512:1024], ALU.mult, [pk[4], "decayT"], ["Dm"])
                    for h in range(8):
                        o_ap = pb[5 + h // 4][:, (h % 4) * 128:(h % 4 + 1) * 128]
                        mm(o_ap, Dm[:, h * 128:(h + 1) * 128], vro[:, h * 128:(h + 1) * 128], True, False,
                           ["Dm", "vro"], [pk[5 + h // 4]])
                        mm(o_ap, qxiT[:, h, :], Rselb[:, h * 128:(h + 1) * 128], False, True,
                           ["qxiT", "Rselb"], [pk[5 + h // 4]])
                    if ostage < 3:
                        continue
                    cp("act", osb[:, 0:512], pb[5][:], [pk[5]], ["osb"])
                    cp("act", osb[:, 512:1024], pb[6][:], [pk[6]], ["osb"])
                    o3 = osb[:].rearrange("p (h v) -> p h v", h=8)
                    q3 = osq[:].rearrange("p (h v) -> p h v", h=8)
                    red("dve", hst[:, 0, :], o3, ALU.add, ["osb"], ["h0"])
                    tt("pool", osq[:], osb[:], osb[:], ALU.mult, ["osb"], ["osq"])
                    red("dve", hst[:, 1, :], q3, ALU.add, ["osq"], ["h1"])
                    ts("dve", hst[:, 2, :], hst[:, 0, :], 1.0 / 128, None, ALU.mult, None, ["h0"], ["h2"])
                    ts("dve", hst[:, 3, :], hst[:, 1, :], 1.0 / 128, None, ALU.mult, None, ["h1"], ["h3"])
                    tt("dve", hst[:, 4, :], hst[:, 2, :], hst[:, 2, :], ALU.mult, ["h2"], ["h4"])
                    tt("dve", hst[:, 5, :], hst[:, 3, :], hst[:, 4, :], ALU.subtract, ["h3", "h4"], ["h5"])
                    rstd(hst[:, 6, :], hst[:, 5, :], ["h5"], ["h6"])
                    tt("dve", q3, o3, hst[:, 2, :].unsqueeze(2).to_broadcast([128, 8, 128]), ALU.subtract, ["osb", "h2", "osq"], ["osq"])
                    tt("dve", q3, q3, hst[:, 6, :].unsqueeze(2).to_broadcast([128, 8, 128]), ALU.mult, ["osq", "h6"], ["osq"])
                    if ostage < 4:
                        continue
                    tt("pool", yrb[:], osq[:], sgr[:], ALU.mult, ["osq", "sgr"], ["yrb"])
                    for fc in range(8):
                        tr(pT[:, fc * 128:(fc + 1) * 128], yrb[:, fc * 128:(fc + 1) * 128], ident[:], ["yrb", "ident"], ["pT"])
                    cp("act", yrT[:].rearrange("p c n -> p (c n)"), pT[:, :], ["pT"], ["yrT"])
                    dma(YRT3[:, :, os_], yrT[:], ["yrT"], ["YRT"])
            S.barrier()

        if "B" in phases:
            with contextlib.ExitStack() as sbk:
                KTs = SB(sbk, "KTs", [128, 4, 8192], BF16)
                Vc = [SB(sbk, "Vc%d" % i, [128, 4, 520], BF16) for i in range(2)]
                IKs = SB(sbk, "IKs", [64, 8192], BF16)
                score = SB(sbk, "score", [128, 8192], F32)
                Wq = SB(sbk, "Wq", [128, 8, 772], BF16)
                stg = [SB(sbk, "stgB%d" % i, [128, 1024], F32) for i in range(2)]
                xof = SB(sbk, "xofB", [128, 8, 128], F32)
                xob = SB(sbk, "xobB", [128, 8, 128], BF16)
                qaT = SB(sbk, "qaT", [128, 4, 2, 128], BF16)
                iqT = SB(sbk, "iqT", [64, 4, 128], BF16)
                iwf = SB(sbk, "iwf", [128, 4], F32)
                aw = SB(sbk, "aw", [128, 4], F32)
                sgn = SB(sbk, "sgn", [128, 4], F32)
                rl = [SB(sbk, "rl%d" % i, [128, 512], F32) for i in range(4)]
                tb = SB(sbk, "tb", [128, 512], F32)
                cm = SB(sbk, "cm", [128, 512], F32)
                btf = SB(sbk, "btf", [128, 1024], F32)
                bts = SB(sbk, "bts", [128, 5, 1024], BF16)
                c31 = SB(sbk, "c31", [128, 8], F32)
                bs = SB(sbk, "bs", [128, 64], F32)
                wk = SB(sbk, "wk", [128, NBIS], F32)
                pw2 = SB(sbk, "pw2", [128, NBIS], F32)
                for it in range(NBIS):
                    memset("dve", pw2[:, it:it + 1], 0.5 ** (it + 1), ["pw2"])
                junk = SB(sbk, "junkB", [128, 3712], BF16)
                junkA = SB(sbk, "junkA", [128, 4608], BF16)
                mk = [SB(sbk, "mk%d" % i, [128, 128], BF16) for i in range(2)]
                mkT = [SB(sbk, "mkT%d" % i, [128, 128], BF16) for i in range(2)]
                Eb = [SB(sbk, "Eb%d" % i, [128, 1024], BF16) for i in range(2)]
                Pb = [SB(sbk, "Pb%d" % i, [128, 1024], BF16) for i in range(2)]
                osb = SB(sbk, "osbB", [128, 520], F32)
                rs = SB(sbk, "rsB", [128, 8], F32)
                yab = SB(sbk, "yab", [128, 512], BF16)
                yaT = SB(sbk, "yaT", [128, 4, 128], BF16)

                KT3 = KT.rearrange("(c p) t -> p c t", p=128)
                for c4 in range(4):
                    dma(KTs[:, c4, :], KT3[:, c4, :], ["KT"], ["KTs"])
                VV3 = VV.rearrange("t p f -> p t f")
                dma(IKs[:], IKT[:, :], ["IKT"], ["IKs"])
                load_w(sbk, Wq, "Wq", w_q, 8, 772, stg)
                memset("pool", qaT[:], 0.0, ["qaT"])
                zr = SB(sbk, "zr", [128, 512], BF16)
                memset("pool", zr[:], 0.0, ["zr"])
                dma(tb[:], c_tb[:, :], [], ["tb"])
                dma(cm[:], c_cm[:, :], [], ["cm"])
                dma(c31[:], c_c31[:, :], [], ["c31"])
                for kr in range(5):
                    dma(btf[:], c_bt[:, kr * 1024:(kr + 1) * 1024], [], ["btf"])
                    tt("dve", bts[:, kr, :].rearrange("p (h q) -> p h q", h=8), btf[:].rearrange("p (h q) -> p h q", h=8),
                       c31[:].unsqueeze(2).to_broadcast([128, 8, 128]), ALU.subtract, ["btf", "c31"], ["bts"])
                xoT3 = xoT.rearrange("(c p) t -> p c t", p=128)
                YAT3 = YAT.rearrange("(c p) t -> p c t", p=128)
                vcount = [0]
                LO, W0, MID, WK, VV_, PW = 0, 1, 2, 3, 4, 5

                for m in range(mstart, nb):
                    os_ = slice(m * 128, (m + 1) * 128)
                    n5 = m + 1
                    nk = min(4 * (m + 1), nkcap)
                    n = 512 * (m + 1)
                    dma(xof[:], xoT3[:, :, os_], [], ["xof"])
                    cp("act", xob[:], xof[:], ["xof"], ["xob"])
                    for fc in range(4):
                        for dc in range(8):
                            mm(pb[0][:, fc * 128:(fc + 1) * 128], Wq[:, dc, fc * 128:(fc + 1) * 128], xob[:, dc, :],
                               dc == 0, dc == 7, ["Wq", "xob"], [pk[0]])
                    p03 = pb[0][:].rearrange("p (c n) -> p c n", c=4)
                    S.op("act", lambda e, o=qaT[0:64, :, 0, :], i=p03[0:64, :, :]: e.mul(out=o, in_=i, mul=0.125),
                         r=[pk[0]], w=["qaT"])
                    S.op("act", lambda e, o=qaT[64:128, :, 1, :], i=p03[64:128, :, :]: e.mul(out=o, in_=i, mul=0.125),
                         r=[pk[0]], w=["qaT"])
                    for h in range(4):
                        for dc in range(8):
                            mm(pb[2][0:64, h * 128:(h + 1) * 128], Wq[:, dc, 512 + h * 64:512 + (h + 1) * 64], xob[:, dc, :],
                               dc == 0, dc == 7, ["Wq", "xob"], [pk[2]])
                    cp("act", iqT[:].rearrange("p h n -> p (h n)"), pb[2][0:64, :], [pk[2]], ["iqT"])
                    for dc in range(8):
                        mm(pb[3][:, 0:4], xob[:, dc, :], Wq[:, dc, 768:772], dc == 0, dc == 7, ["Wq", "xob"], [pk[3]])
                    cp("dve", iwf[:], pb[3][:, 0:4], [pk[3]], ["iwf"])
                    act(aw[:], iwf[:], AF.Abs, ["iwf"], ["aw"], scale=1.0 / 16)
                    act(sgn[:], iwf[:], AF.Sign, ["iwf"], ["sgn"])
                    if bstage < 1:
                        continue
                    for c5 in range(n5):
                        ks = slice(c5 * 512, (c5 + 1) * 512)
                        for h in range(4):
                            mm(pb[h][:], iqT[:, h, :], IKs[:, ks], True, True, ["iqT", "IKs"], [pk[h]])
                        sck = "score%d" % c5
                        ts("dve", score[:, ks], tb[:], -1e-30 * 512 * c5, None, ALU.add, None, ["tb"], [sck])
                        if c5 == n5 - 1:
                            tt("dve", score[:, ks], score[:, ks], cm[:], ALU.add, [sck, "cm"], [sck])
                        for h in range(4):
                            rk = "rl%d" % h
                            act(rl[h][:], pb[h][:], AF.Relu, [pk[h], "aw"], [rk], scale=aw[:, h:h + 1])
                            stt("dve", score[:, ks], rl[h][:], sgn[:, h:h + 1], score[:, ks], ALU.mult, ALU.add,
                                [rk, "sgn", sck], [sck])
                    sckeys = ["score%d" % c5 for c5 in range(n5)]
                    if bstage < 2:
                        continue
                    nd = max(128, ((45 * n // 100) // 128) * 128)
                    na = n - nd
                    red("dve", bs[:, W0:W0 + 1], score[:, 0:n], ALU.max, sckeys, ["bs"])
                    ts("dve", bs[:, W0:W0 + 1], bs[:, W0:W0 + 1], 1e-3 - LO_INIT, None, ALU.add, None, ["bs"], ["bs"])
                    memset("dve", bs[:, 8:64], 0.0, ["bs", "bsa"])
                    ts("dve", wk[:], pw2[:], bs[:, W0:W0 + 1], None, ALU.mult, None, ["bs", "pw2"], ["wk"])
                    ts("dve", bs[:, MID:MID + 1], wk[:, 0:1], LO_INIT, None, ALU.add, None, ["wk"], ["bs", "mid"])
                    for it in range(NBIS):
                        ts("dve", junk[:, 0:nd], score[:, 0:nd], bs[:, MID:MID + 1], 0.0, ALU.is_ge, ALU.add,
                           sckeys + ["mid"], ["junk", "bs"], accum=bs[:, 8 + it:9 + it])
                        act(junkA[:, 0:na], score[:, nd:n], AF.Sign, sckeys + ["mid"], ["junkA", "bsa"],
                            bias=bs[:, MID:MID + 1], scale=-1.0, accum=bs[:, 36 + it:37 + it])
                        stt("dve", bs[:, VV_:VV_ + 1], bs[:, 8 + it:9 + it], 2.0, bs[:, 36 + it:37 + it], ALU.mult, ALU.subtract,
                            ["bs", "bsa"], ["bs"])
                        ts("dve", bs[:, PW:PW + 1], bs[:, VV_:VV_ + 1], 511.5 - na, wk[:, it:it + 1], ALU.is_ge, ALU.mult,
                           ["bs", "wk"], ["bs"])
                        if it + 1 < NBIS:
                            stt("dve", bs[:, MID:MID + 1], bs[:, PW:PW + 1], bs[:, MID:MID + 1], wk[:, it + 1:it + 2],
                                ALU.add, ALU.subtract, ["bs", "wk", "mid"], ["bs", "mid"])
                        else:
                            stt("dve", bs[:, LO:LO + 1], bs[:, PW:PW + 1], bs[:, MID:MID + 1], wk[:, it:it + 1],
                                ALU.add, ALU.subtract, ["bs", "wk", "mid"], ["bs"])
                    if bstage < 3:
                        continue
                    for hh in range(2):
                        mm(pb[4 + hh][:], zr[:, 0:128], zr[:], True, True, ["zr"], [pk[4 + hh]])
                    vslot = {}

                    def stage1(kc):
                        sl = kc % 2
                        kcs = slice(kc * 128, (kc + 1) * 128)
                        pS = (pb[2 * sl], pb[2 * sl + 1])
                        pSk = (pk[2 * sl], pk[2 * sl + 1])
                        kr = kc - (4 * m - 1)
                        near = 0 <= kr <= 4 and kc >= 0 and not nobias
                        if kc % 4 == 0:
                            vs_ = vcount[0] % 2
                            vcount[0] += 1
                            dma(Vc[vs_][:], VV3[:, kc:kc + 4, :], ["VV"], ["Vc%d" % vs_])
                            for k2 in range(kc, kc + 4):
                                vslot[k2] = vs_
                        ts("dve", mk[sl][:], score[:, kcs], bs[:, LO:LO + 1], None, ALU.is_ge, None, ["score%d" % (kc // 4), "bs"], ["mk%d" % sl])
                        tr(pT[:, sl * 128:(sl + 1) * 128], mk[sl][:], ident[:], ["mk%d" % sl, "ident"], ["pT%d" % sl])
                        for c4 in range(4):
                            reg = pS[c4 // 2][:, (c4 % 2) * 256:(c4 % 2 + 1) * 256]
                            mm(reg, KTs[:, c4, kcs], qaT[:, c4, :, :].rearrange("p a q -> p (a q)"), True, not near,
                               ["KTs", "qaT"], [pSk[c4 // 2]])
                            if near:
                                mm(reg, ident[:], bts[:, kr, c4 * 256:(c4 + 1) * 256], False, True,
                                   ["ident", "bts"], [pSk[c4 // 2]])
                        cp("act", mkT[sl][:], pT[:, sl * 128:(sl + 1) * 128], ["pT%d" % sl], ["mkT%d" % sl])
                        for hh in range(2):
                            act(Eb[sl][:, hh * 512:(hh + 1) * 512], pS[hh][:], AF.Exp, [pSk[hh]], ["Eb%d_%d" % (sl, hh)])
                        for hh in range(2):
                            tt("dve", Pb[sl][:, hh * 512:(hh + 1) * 512].rearrange("p (h q) -> p h q", h=4),
                               Eb[sl][:, hh * 512:(hh + 1) * 512].rearrange("p (h q) -> p h q", h=4),
                               mkT[sl][:].unsqueeze(1).to_broadcast([128, 4, 128]), ALU.mult,
                               ["Eb%d_%d" % (sl, hh), "mkT%d" % sl], ["Pb%d_%d" % (sl, hh)])

                    def stage2(kc):
                        sl = kc % 2
                        vs_ = vslot[kc]
                        Vv = Vc[vs_][:].rearrange("p t (h d) -> p t h d", h=8)
                        for h in range(8):
                            mm(pb[4 + h // 4][:, (h % 4) * 128:(h % 4) * 128 + 65], Pb[sl][:, h * 128:(h + 1) * 128],
                               Vv[:, kc % 4, h, :], False, kc == nk - 1, ["Pb%d_%d" % (sl, h // 4), "Vc%d" % vs_], [pk[4 + h // 4]])

                    stage1(0)
                    for kc in range(1, nk):
                        stage1(kc)
                        stage2(kc - 1)
                    stage2(nk - 1)
                    if bstage < 4:
                        continue
                    cp("act", osb[:, 0:260].rearrange("p (h d) -> p h d", h=4), pb[4][:].rearrange("p (h d) -> p h d", h=4)[:, :, 0:65],
                       [pk[4]], ["osbB"])
                    cp("act", osb[:, 260:520].rearrange("p (h d) -> p h d", h=4), pb[5][:].rearrange("p (h d) -> p h d", h=4)[:, :, 0:65],
                       [pk[5]], ["osbB"])
                    o4 = osb[:].rearrange("p (h d) -> p h d", h=8)
                    S.op("dve", lambda e, o=rs[:], i=o4[:, :, 64]: e.reciprocal(out=o, in_=i), r=["osbB"], w=["rsB"])
                    tt("dve", yab[:].rearrange("p (h d) -> p h d", h=8), o4[:, :, 0:64],
                       rs[:].unsqueeze(2).to_broadcast([128, 8, 64]), ALU.mult, ["osbB", "rsB"], ["yab"])
                    for fc in range(4):
                        tr(pT[:, 256 + fc * 128:256 + (fc + 1) * 128], yab[:, fc * 128:(fc + 1) * 128], ident[:],
                           ["yab", "ident"], ["pTy"])
                    cp("act", yaT[:].rearrange("p c n -> p (c n)"), pT[:, 256:768], ["pTy"], ["yaT"])
                    dma(YAT3[:, :, os_], yaT[:], ["yaT"], ["YAT"])
            S.barrier()

        if "C" in phases:
            with contextlib.ExitStack() as sc:
                Wg = SB(sc, "Wg", [128, 8, 2048], BF16)
                Wa = SB(sc, "Wa", [128, 4, 1024], BF16)
                Wr = SB(sc, "Wr", [128, 8, 1024], BF16)
                Wo = SB(sc, "Wo", [128, 8, 1024], BF16)
                stg = [SB(sc, "stgC%d" % i, [128, 2048], F32) for i in range(2)]
                g1 = SB(sc, "g1", [128, 1024], F32)
                b1 = SB(sc, "b1", [128, 1024], F32)
                xof = SB(sc, "xofC", [128, 8, 128], F32)
                xob = SB(sc, "xobC", [128, 8, 128], BF16)
                yaT = SB(sc, "yaTC", [128, 4, 128], BF16)
                yrT = SB(sc, "yrTC", [128, 8, 128], BF16)
                sg = SB(sc, "sg", [128, 2048], F32)
                hf_ = SB(sc, "hf", [128, 1024], F32)
                h2 = SB(sc, "h2", [128, 1024], F32)
                hb = SB(sc, "hb", [128, 1024], BF16)
                hT = SB(sc, "hT", [128, 8, 128], BF16)
                xres = SB(sc, "xres", [128, 1024], F32)
                pre = SB(sc, "pre", [128, 1024], F32)
                tmp = SB(sc, "tmpC", [128, 1024], F32)
                x1f = SB(sc, "x1f", [128, 1024], F32)
                x1b = SB(sc, "x1b", [128, 1024], BF16)
                x1T = SB(sc, "x1T", [128, 8, 128], BF16)
                stat = SB(sc, "statC", [128, 8], F32)
                load_w(sc, Wg, "Wg", w_g, 8, 2048, stg)
                load_w(sc, Wa, "Wa", w_a, 4, 1024, stg)
                load_w(sc, Wr, "Wr", w_r, 8, 1024, stg)
                load_w(sc, Wo, "Wo", w_o, 8, 1024, stg)
                dma(g1[:], ln1_g[:, :], [], ["lng"])
                dma(b1[:], ln1_b[:, :], [], ["lnb"])
                xoT3 = xoT.rearrange("(c p) t -> p c t", p=128)
                YAT3 = YAT.rearrange("(c p) t -> p c t", p=128)
                YRT3 = YRT.rearrange("(c p) t -> p c t", p=128)
                X1T3 = X1T.rearrange("(c p) t -> p c t", p=128)
                for m in range(16):
                    os_ = slice(m * 128, (m + 1) * 128)
                    dma(xof[:], xoT3[:, :, os_], [], ["xof"])
                    cp("act", xob[:], xof[:], ["xof"], ["xob"])
                    dma(yaT[:], YAT3[:, :, os_], ["YAT"], ["yaT"])
                    dma(yrT[:], YRT3[:, :, os_], ["YRT"], ["yrT"])
                    dma(xres[:], xo[os_, :], [], ["xres"])
                    for q4 in range(4):
                        for dc in range(8):
                            mm(pb[q4][:], xob[:, dc, :], Wg[:, dc, q4 * 512:(q4 + 1) * 512], dc == 0, dc == 7,
                               ["Wg", "xob"], [pk[q4]])
                        act(sg[:, q4 * 512:(q4 + 1) * 512], pb[q4][:], AF.Sigmoid, [pk[q4]], ["sg"])
                    for hh in range(2):
                        for fc in range(4):
                            mm(pb[4 + hh][:], yaT[:, fc, :], Wa[:, fc, hh * 512:(hh + 1) * 512], fc == 0, fc == 3,
                               ["Wa", "yaT"], [pk[4 + hh]])
                        tt("dve", hf_[:, hh * 512:(hh + 1) * 512], pb[4 + hh][:], sg[:, hh * 512:(hh + 1) * 512], ALU.mult,
                           [pk[4 + hh], "sg"], ["hf"])
                    for hh in range(2):
                        for fc in range(8):
                            mm(pb[hh][:], yrT[:, fc, :], Wr[:, fc, hh * 512:(hh + 1) * 512], fc == 0, fc == 7,
                               ["Wr", "yrT"], [pk[hh]])
                        tt("dve", h2[:, hh * 512:(hh + 1) * 512], pb[hh][:], sg[:, 1024 + hh * 512:1024 + (hh + 1) * 512],
                           ALU.mult, [pk[hh], "sg"], ["h2"])
                    tt("dve", hb[:], hf_[:], h2[:], ALU.add, ["hf", "h2"], ["hb"])
                    for fc in range(8):
                        tr(pT[:, fc * 128:(fc + 1) * 128], hb[:, fc * 128:(fc + 1) * 128], ident[:], ["hb", "ident"], ["pT"])
                    cp("act", hT[:].rearrange("p c n -> p (c n)"), pT[:, :], ["pT"], ["hT"])
                    for hh in range(2):
                        for fc in range(8):
                            mm(pb[2 + hh][:], hT[:, fc, :], Wo[:, fc, hh * 512:(hh + 1) * 512], fc == 0, fc == 7,
                               ["Wo", "hT"], [pk[2 + hh]])
                        stt("dve", pre[:, hh * 512:(hh + 1) * 512], xres[:, hh * 512:(hh + 1) * 512], ALPHA, pb[2 + hh][:],
                            ALU.mult, ALU.add, ["xres", pk[2 + hh]], ["pre"])
                    layer_norm_rows({"stat": stat}, pre, "pre", x1f, "x1f", g1, b1, tmp, "lnC")
                    dma(X1[os_, :], x1f[:], ["x1f"], ["X1"])
                    cp("act", x1b[:], x1f[:], ["x1f"], ["x1b"])
                    for fc in range(8):
                        tr(pT[:, fc * 128:(fc + 1) * 128], x1b[:, fc * 128:(fc + 1) * 128], ident[:], ["x1b", "ident"], ["pT"])
                    cp("act", x1T[:].rearrange("p c n -> p (c n)"), pT[:, :], ["pT"], ["x1T"])
                    dma(X1T3[:, :, os_], x1T[:], ["x1T"], ["X1T"])
            S.barrier()

        if "D" in phases:
            with contextlib.ExitStack() as sd:
                Wdn = SB(sd, "Wdn", [128, 32, 1024], BF16)
                HT = SB(sd, "HT", [128, 32, 1024], BF16)
                x1T = SB(sd, "x1TD", [128, 8, 1024], BF16)
                stg = [SB(sd, "stgD%d" % i, [128, 1024], F32) for i in range(3)]
                wub = [SB(sd, "wub%d" % i, [128, 8, 128], BF16) for i in range(2)]
                rl = [SB(sd, "rlD%d" % i, [128, 512], F32) for i in range(2)]
                g2 = SB(sd, "g2", [128, 1024], F32)
                b2 = SB(sd, "b2", [128, 1024], F32)
                x1r = SB(sd, "x1r", [128, 1024], F32)
                pre = SB(sd, "preD", [128, 1024], F32)
                tmp = SB(sd, "tmpD", [128, 1024], F32)
                of_ = SB(sd, "of", [128, 1024], F32)
                stat = SB(sd, "statD", [128, 8], F32)
                load_w(sd, Wdn, "Wdn", w_dn, 32, 1024, stg)
                dma(g2[:], ln2_g[:, :], [], ["lng"])
                dma(b2[:], ln2_b[:, :], [], ["lnb"])
                X1T3 = X1T.rearrange("(c p) t -> p c t", p=128)
                w_up3 = w_up.rearrange("(c p) f -> p c f", p=128)
                for th in range(2):
                    dma(x1T[:], X1T3[:, :, th * 1024:(th + 1) * 1024], ["X1T"], ["x1TD"])
                    for f in range(32):
                        ws = f % 2
                        wk = "wub%d" % ws
                        i = rr["stg"]
                        rr["stg"] = i + 1
                        sk = "stg%d" % (i % 3)
                        stile = stg[i % 3]
                        dma(stile[:].rearrange("p (c f) -> p c f", c=8), w_up3[:, :, f * 128:(f + 1) * 128], [], [sk])
                        cp("pool" if f % 2 == 0 else "dve", wub[ws][:].rearrange("p c f -> p (c f)"), stile[:], [sk], [wk])
                        for s2 in range(2):
                            bank = 2 * (f % 2) + s2
                            for dc in range(8):
                                mm(pb[bank][:], wub[ws][:, dc, :], x1T[:, dc, s2 * 512:(s2 + 1) * 512], dc == 0, dc == 7,
                                   [wk, "x1TD"], [pk[bank]])
                            rk = "rlD%d" % s2
                            act(rl[s2][:], pb[bank][:], AF.Relu, [pk[bank]], [rk])
                            tt("dve" if s2 == 0 else "pool", HT[:, f, s2 * 512:(s2 + 1) * 512], rl[s2][:], rl[s2][:], ALU.mult,
                               [rk], ["HT%d" % s2])
                    for tl in range(8):
                        tg = th * 8 + tl
                        os_ = slice(tg * 128, (tg + 1) * 128)
                        dma(x1r[:], X1[os_, :], ["X1"], ["x1r"])
                        for hh in range(2):
                            for f in range(32):
                                mm(pb[4 + hh][:], HT[:, f, tl * 128:(tl + 1) * 128], Wdn[:, f, hh * 512:(hh + 1) * 512],
                                   f == 0, f == 31, ["HT0", "HT1", "Wdn"], [pk[4 + hh]])
                            stt("dve", pre[:, hh * 512:(hh + 1) * 512], x1r[:, hh * 512:(hh + 1) * 512], ALPHA, pb[4 + hh][:],
                                ALU.mult, ALU.add, ["x1r", pk[4 + hh]], ["preD"])
                        layer_norm_rows({"stat": stat}, pre, "preD", of_, "of", g2, b2, tmp, "lnD")
                        dma(out[os_, :], of_[:], ["of"], ["out"])
        S.emit()
    return nc


def _t5_bucket(n):
    n = np.maximum(n, 0)
    nf = np.maximum(n, 1).astype(np.float32)
    large = 16 + (np.log(nf / np.float32(16)) / np.float32(math.log(128 / 16)) * np.float32(16)).astype(np.int32)
    large = np.minimum(large, 31)
    return np.where(n < 16, n, large)


def _consts(j):
    c = {}
    c["c_ident"] = np.eye(128, dtype=np.float32)
    half = 32
    inv = (np.float32(10000.0) ** (-np.arange(half, dtype=np.float32) / np.float32(half))).astype(np.float32)
    c["c_invf"] = np.broadcast_to(inv[None, :], (128, 32)).copy()
    H = 8
    gamma = (1.0 - 2.0 ** (-5.0 - np.arange(H, dtype=np.float64)))
    lg = np.log(gamma)
    nn = np.arange(128, dtype=np.float64)
    diff = nn[None, :] - nn[:, None]
    dT = np.where(diff[:, None, :] >= 0, np.exp(lg[None, :, None] * np.maximum(diff, 0)[:, None, :]), 0.0)
    c["c_decayT"] = dT.reshape(128, 1024).astype(np.float32)
    zeta = np.exp(lg[None, :] * (127.0 - nn[:, None]))
    c["c_zeta8"] = (zeta / 8.0).astype(np.float32)
    xi = np.exp(lg[:, None] * (nn[None, :] + 1.0))
    c["c_xiT"] = np.broadcast_to(xi.reshape(1, 1024), (64, 1024)).astype(np.float32).copy()
    g = np.exp(lg * 128.0)
    c["c_gmat"] = np.broadcast_to(np.repeat(g, 128)[None, :], (64, 1024)).astype(np.float32).copy()
    c["c_tb"] = np.broadcast_to((-1e-30 * np.arange(512, dtype=np.float64))[None, :], (128, 512)).astype(np.float32).copy()
    q = np.arange(128)[:, None]
    kk = np.arange(512)[None, :]
    c["c_cm"] = np.where(kk <= 128 * j + q, 0.0, MASKV).astype(np.float32)
    oh = np.zeros((128, 4), np.float32)
    oh[:, j] = 1.0
    c["c_oh"] = oh
    return c


def _bias_tiles(rel_bias, j):
    key = np.arange(128)[:, None, None]
    kr = np.arange(5)[None, :, None]
    q = np.arange(128)[None, None, :]
    dist = (j + 1 - kr) * 128 + q - key
    bucket = np.where(dist >= 0, _t5_bucket(dist), 31)
    bt = rel_bias[bucket]
    bt = np.transpose(bt, (0, 1, 3, 2))
    return np.ascontiguousarray(bt.reshape(128, 5 * 1024)).astype(np.float32)


_NC_CACHE = {}


def make_in_maps(x, positions, w_in, rel_bias, idx_k_ln_g, idx_k_ln_b, w_attn_branch, w_ret_branch,
                 w_out, ln_mix_g, ln_mix_b, w_up, w_down, ln_ffn_g, ln_ffn_b):
    x = np.asarray(x, np.float32)
    positions = np.asarray(positions, np.int32)
    w = np.asarray(w_in, np.float32)[0]
    rel_bias = np.asarray(rel_bias, np.float32)
    cs = lambda a, b: w[:, a:b]
    w_kv = np.ascontiguousarray(np.concatenate([cs(512, 1024), cs(1024, 1536), cs(1792, 1856), cs(2372, 2884), cs(2884, 3908)], axis=1))
    w_q = np.ascontiguousarray(np.concatenate([cs(0, 512), cs(1536, 1792), cs(1856, 1860)], axis=1))
    w_ro = np.ascontiguousarray(np.concatenate([cs(1860, 2372), cs(3908, 4932)], axis=1))
    w_g = np.ascontiguousarray(cs(4932, 6980))
    rep = lambda v: np.ascontiguousarray(np.broadcast_to(np.asarray(v, np.float32).reshape(1, -1), (128, 1024)))
    shared = {
        "w_kv": w_kv, "w_q": w_q, "w_ro": w_ro, "w_g": w_g,
        "w_a": np.ascontiguousarray(np.asarray(w_attn_branch, np.float32)[0]),
        "w_r": np.ascontiguousarray(np.asarray(w_ret_branch, np.float32)[0]),
        "w_o": np.ascontiguousarray(np.asarray(w_out, np.float32)[0]),
        "w_up": np.ascontiguousarray(np.asarray(w_up, np.float32)[0]),
        "w_dn": np.ascontiguousarray(np.asarray(w_down, np.float32)[0]),
        "lnk_g": np.ascontiguousarray(np.asarray(idx_k_ln_g, np.float32).reshape(64, 1)),
        "lnk_b": np.ascontiguousarray(np.asarray(idx_k_ln_b, np.float32).reshape(64, 1)),
        "ln1_g": rep(ln_mix_g), "ln1_b": rep(ln_mix_b), "ln2_g": rep(ln_ffn_g), "ln2_b": rep(ln_ffn_b),
        "c_c31": np.ascontiguousarray(np.broadcast_to(rel_bias[31][None, :], (128, 8))),
    }
    xTs = [np.ascontiguousarray(x[b].T) for b in range(2)]
    in_maps = []
    own_idx = []
    for c in range(8):
        b, j = c // 4, c % 4
        tok = (np.arange(16)[:, None] * 4 + j) * 128 + np.arange(128)[None, :]
        tok = tok.reshape(-1)
        own_idx.append((b, tok))
        d = dict(shared)
        d["xT"] = xTs[b]
        d["xoT"] = np.ascontiguousarray(xTs[b][:, tok])
        d["xo"] = np.ascontiguousarray(x[b][tok])
        d["posT"] = np.ascontiguousarray(positions[b].reshape(64, 128).T).astype(np.float32)
        d["posoT"] = np.ascontiguousarray(positions[b][tok].reshape(16, 128).T).astype(np.float32)
        d.update(_consts(j))
        d["c_bt"] = _bias_tiles(rel_bias, j)
        in_maps.append(d)
    return in_maps, own_idx


def kernel(**inputs):
    in_maps, own_idx = make_in_maps(**inputs)
    if "nc" not in _NC_CACHE:
        _NC_CACHE["nc"] = build()
    nc = _NC_CACHE["nc"]
    res = run_bass_kernel_spmd(nc, in_maps, core_ids=list(range(8)))
    outp = np.zeros((2, 8192, 1024), np.float32)
    for c in range(8):
        b, tok = own_idx[c]
        outp[b, tok] = res.results[c]["out"]
    return outp
```

```python
import contextlib
import math
import numpy as np
import concourse.bass as bass
import concourse.mybir as mybir
from concourse.bass_utils import run_bass_kernel_spmd

F32 = mybir.dt.float32
BF16 = mybir.dt.bfloat16
I32 = mybir.dt.int32
ALU = mybir.AluOpType
AF = mybir.ActivationFunctionType
AX = mybir.AxisListType

NBIS = 19
LO_INIT = -60.0
MASKV = -30000.0
ALPHA = 2.0 ** 0.25
EPS = 1e-5
PI = math.pi


class Sched:
    CE = ("pe", "act", "dve", "pool")
    ALLE = ("pe", "act", "dve", "pool", "sp")

    def __init__(self, nc, stack, n_dma=(("sp", 24), ("act", 8))):
        self.nc = nc
        self.ops = {e: [] for e in self.ALLE}
        self.cnt = {e: 0 for e in self.CE}
        self.sem = {}
        for e in self.CE:
            self.sem["c_" + e] = stack.enter_context(nc.semaphore("c_" + e))
        self.dma_slots = {}
        self.dma_next = {}
        self.dma_val = {}
        for e, n in n_dma:
            self.dma_slots[e] = []
            for i in range(n):
                k = "d_%s_%d" % (e, i)
                self.sem[k] = stack.enter_context(nc.semaphore(k))
                self.dma_slots[e].append(k)
                self.dma_val[k] = 0
            self.dma_next[e] = 0
        self.buf = {}
        self.waited = {e: {} for e in self.ALLE}
        self.pending = {e: {} for e in self.ALLE}
        self.nops = 0

    def _st(self, k):
        s = self.buf.get(k)
        if s is None:
            s = {"w": None, "r": {}}
            self.buf[k] = s
        return s

    def op(self, eng, fn, r=(), w=(), dma=False):
        waits = dict(self.pending[eng])
        self.pending[eng] = {}

        def add(ev):
            if ev is None:
                return
            k, v = ev
            if waits.get(k, 0) < v:
                waits[k] = v

        for k in r:
            add(self._st(k)["w"])
        for k in w:
            s = self._st(k)
            add(s["w"])
            for ev in s["r"].items():
                add(ev)
        if dma:
            slots = self.dma_slots[eng]
            k = slots[self.dma_next[eng] % len(slots)]
            self.dma_next[eng] += 1
            if self.dma_val[k] > 0:
                add((k, self.dma_val[k]))
            self.dma_val[k] += 16
            ev = (k, self.dma_val[k])
            inc = 16
        else:
            self.cnt[eng] += 1
            ev = ("c_" + eng, self.cnt[eng])
            inc = 1
        wl = []
        own = "c_" + eng
        for k, v in waits.items():
            if k == own and eng == "pe":
                continue
            if self.waited[eng].get(k, 0) >= v:
                continue
            self.waited[eng][k] = v
            wl.append((k, v))
        self.ops[eng].append((wl, fn, ev[0], inc))
        self.nops += 1
        for k in w:
            s = self._st(k)
            s["w"] = ev
            s["r"] = {}
        for k in r:
            if k in w:
                continue
            s = self._st(k)
            if s["r"].get(ev[0], 0) < ev[1]:
                s["r"][ev[0]] = ev[1]
        return ev

    def all_events(self):
        evs = {}
        for e in self.CE:
            if self.cnt[e] > 0:
                evs["c_" + e] = self.cnt[e]
        for k, v in self.dma_val.items():
            if v > 0:
                evs[k] = v
        return evs

    def barrier(self):
        evs = self.all_events()
        for e in self.ALLE:
            for k, v in evs.items():
                if self.pending[e].get(k, 0) < v:
                    self.pending[e][k] = v
        self.buf = {}

    def emit(self):
        nc = self.nc
        final = self.all_events()
        with nc.Block() as block:
            def run(eng_name, e, is_last=False):
                for wl, fn, sk, inc in self.ops[eng_name]:
                    for k, v in wl:
                        e.wait_ge(self.sem[k], v)
                    ins = fn(e)
                    ins.then_inc(self.sem[sk], inc)
                if is_last:
                    for k, v in final.items():
                        e.wait_ge(self.sem[k], v)

            @block.tensor
            def _(e):
                run("pe", e)

            @block.scalar
            def _(e):
                run("act", e)

            @block.vector
            def _(e):
                run("dve", e)

            @block.gpsimd
            def _(e):
                run("pool", e)

            @block.sync
            def _(e):
                run("sp", e, True)


def build(phases="ABCD", debug=False, ng=16, own=True, alltok=True, ostage=99, nb=16, bstage=99, mstart=0, nkcap=999, nobias=False, dummy=0):
    nc = bass.Bass("TRN2", target_bir_lowering=False)

    def din(name, shape, dt=F32):
        return nc.dram_tensor(name, shape, dt, kind="ExternalInput").ap()

    def dscr(name, shape, dt):
        kind = "ExternalOutput" if debug else "Internal"
        return nc.dram_tensor(name, shape, dt, kind=kind).ap()

    xT = din("xT", [1024, 8192])
    xoT = din("xoT", [1024, 2048])
    xo = din("xo", [2048, 1024])
    posT = din("posT", [128, 64], F32)
    posoT = din("posoT", [128, 16], F32)
    w_kv = din("w_kv", [1024, 2624])
    w_q = din("w_q", [1024, 772])
    w_ro = din("w_ro", [1024, 1536])
    w_g = din("w_g", [1024, 2048])
    w_a = din("w_a", [512, 1024])
    w_r = din("w_r", [1024, 1024])
    w_o = din("w_o", [1024, 1024])
    w_up = din("w_up", [1024, 4096])
    w_dn = din("w_dn", [4096, 1024])
    lnk_g = din("lnk_g", [64, 1])
    lnk_b = din("lnk_b", [64, 1])
    ln1_g = din("ln1_g", [128, 1024])
    ln1_b = din("ln1_b", [128, 1024])
    ln2_g = din("ln2_g", [128, 1024])
    ln2_b = din("ln2_b", [128, 1024])
    c_ident = din("c_ident", [128, 128])
    c_invf = din("c_invf", [128, 32])
    c_decayT = din("c_decayT", [128, 1024])
    c_zeta8 = din("c_zeta8", [128, 8])
    c_xiT = din("c_xiT", [64, 1024])
    c_gmat = din("c_gmat", [64, 1024])
    c_tb = din("c_tb", [128, 512])
    c_cm = din("c_cm", [128, 512])
    c_bt = din("c_bt", [128, 5 * 1024])
    c_c31 = din("c_c31", [128, 8])
    c_oh = din("c_oh", [128, 4])
    out = nc.dram_tensor("out", [2048, 1024], F32, kind="ExternalOutput").ap()

    KT = dscr("s_KT", [512, 8192], BF16)
    VV = dscr("s_V", [64, 128, 520], BF16)
    IKT = dscr("s_IKT", [64, 8192], BF16)
    YAT = dscr("s_YAT", [512, 2048], BF16)
    YRT = dscr("s_YRT", [1024, 2048], BF16)
    X1 = dscr("s_X1", [2048, 1024], F32)
    X1T = dscr("s_X1T", [1024, 2048], BF16)

    with contextlib.ExitStack() as st0:
        S = Sched(nc, st0)
        rr = {"cast": 0}

        def SB(stk, name, shape, dt):
            return stk.enter_context(nc.sbuf_tensor(name, shape, dt))

        def PS(stk, name, shape, dt):
            return stk.enter_context(nc.psum_tensor(name, shape, dt))

        def dma(out_ap, in_ap, r, w, eng="sp"):
            S.op(eng, lambda e, o=out_ap, i=in_ap: e.dma_start(out=o, in_=i), r=r, w=w, dma=True)

        def mm(out_ap, lhsT, rhs, start, stop, r, w):
            S.op("pe", lambda e, o=out_ap, l=lhsT, rh=rhs, s0=start, s1=stop: e.matmul(o, lhsT=l, rhs=rh, start=s0, stop=s1),
                 r=r, w=w)

        def tr(out_ap, in_ap, ident_ap, r, w):
            S.op("pe", lambda e, o=out_ap, i=in_ap, d=ident_ap: e.transpose(out=o, in_=i, identity=d), r=r, w=w)

        def act(out_ap, in_ap, func, r, w, bias=None, scale=None, accum=None):
            kw = {}
            if bias is not None:
                kw["bias"] = bias
            if scale is not None:
                kw["scale"] = scale
            if accum is not None:
                kw["accum_out"] = accum
            S.op("act", lambda e, o=out_ap, i=in_ap, f=func, kw=kw: e.activation(out=o, in_=i, func=f, **kw), r=r, w=w)

        def tt(eng, out_ap, in0, in1, op, r, w):
            S.op(eng, lambda e, o=out_ap, a=in0, b=in1, p=op: e.tensor_tensor(out=o, in0=a, in1=b, op=p), r=r, w=w)

        def ts(eng, out_ap, in0, s1, s2, op0, op1, r, w, accum=None):
            kw = {}
            if op1 is not None:
                kw["op1"] = op1
            if accum is not None:
                kw["accum_out"] = accum
            S.op(eng, lambda e, o=out_ap, a=in0, x1=s1, x2=s2, p0=op0, kw=kw:
                 e.tensor_scalar(out=o, in0=a, scalar1=x1, scalar2=x2, op0=p0, **kw), r=r, w=w)

        def stt(eng, out_ap, in0, scalar, in1, op0, op1, r, w):
            S.op(eng, lambda e, o=out_ap, a=in0, s=scalar, b=in1, p0=op0, p1=op1:
                 e.scalar_tensor_tensor(out=o, in0=a, scalar=s, in1=b, op0=p0, op1=p1), r=r, w=w)

        def cp(eng, out_ap, in_ap, r, w):
            if eng == "act":
                act(out_ap, in_ap, AF.Copy, r, w)
            else:
                S.op(eng, lambda e, o=out_ap, i=in_ap: e.tensor_copy(out=o, in_=i), r=r, w=w)

        def red(eng, out_ap, in_ap, op, r, w):
            S.op(eng, lambda e, o=out_ap, i=in_ap, p=op: e.tensor_reduce(out=o, in_=i, axis=AX.X, op=p), r=r, w=w)

        def memset(eng, ap, val, w):
            S.op(eng, lambda e, a=ap, v=val: e.memset(a, v), w=w)

        ident_f = SB(st0, "ident_f", [128, 128], F32)
        ident = SB(st0, "ident", [128, 128], BF16)
        dma(ident_f[:], c_ident[:, :], [], ["ident_f"])
        cp("dve", ident[:], ident_f[:], ["ident_f"], ["ident"])
        epsT = SB(st0, "epsT", [128, 1], F32)
        memset("dve", epsT[:], EPS, ["epsT"])

        def rstd(out_ap, var_ap, r, w):
            act(out_ap, var_ap, AF.Sqrt, list(r) + ["epsT"], list(w), bias=epsT[:, 0:1], scale=1.0)
            S.op("dve", lambda e, o=out_ap: e.reciprocal(out=o, in_=o), r=list(w), w=list(w))

        pb = [PS(st0, "pb%d" % i, [128, 512], F32) for i in range(7)]
        pT = PS(st0, "pT", [128, 1024], BF16)
        pk = ["pb%d" % i for i in range(7)]

        def cast_eng():
            rr["cast"] += 1
            return ("act", "dve", "act", "dve", "pool")[rr["cast"] % 5]

        def load_w(stk_stage, dst, dst_key, src, nrow_chunks, cols, stg, col0=0):
            for rc in range(nrow_chunks):
                pw_ = stg[0].shape[1]
                for c0 in range(0, cols, pw_):
                    cw = min(pw_, cols - c0)
                    i = rr.setdefault("stg", 0)
                    rr["stg"] = i + 1
                    sk = "stg%d" % (i % len(stg))
                    stile = stg[i % len(stg)]
                    dma(stile[:, 0:cw], src[rc * 128:(rc + 1) * 128, c0:c0 + cw], [], [sk])
                    cp(cast_eng(), dst[:, rc, col0 + c0:col0 + c0 + cw], stile[:, 0:cw], [sk], [dst_key])

        def layer_norm_rows(stk, pre, pre_key, outf, out_key, g_t, b_t, tmp, nm):
            s1 = nm + "_s1"
            st_t = stk["stat"]
            red("dve", st_t[:, 0:1], pre[:], ALU.add, [pre_key], [s1])
            act(tmp[:], pre[:], AF.Square, [pre_key], [nm + "_tmp", s1 + "q"], accum=st_t[:, 1:2])
            ts("dve", st_t[:, 2:3], st_t[:, 0:1], 1.0 / 1024, None, ALU.mult, None, [s1], [s1 + "m"])
            ts("dve", st_t[:, 3:4], st_t[:, 1:2], 1.0 / 1024, None, ALU.mult, None, [s1 + "q"], [s1 + "e"])
            tt("dve", st_t[:, 4:5], st_t[:, 2:3], st_t[:, 2:3], ALU.mult, [s1 + "m"], [s1 + "mm"])
            tt("dve", st_t[:, 5:6], st_t[:, 3:4], st_t[:, 4:5], ALU.subtract, [s1 + "e", s1 + "mm"], [s1 + "v"])
            rstd(st_t[:, 6:7], st_t[:, 5:6], [s1 + "v"], [s1 + "r"])
            ts("dve", pre[:], pre[:], st_t[:, 2:3], st_t[:, 6:7], ALU.subtract, ALU.mult, [pre_key, s1 + "m", s1 + "r"], [pre_key])
            tt("dve", pre[:], pre[:], g_t[:], ALU.mult, [pre_key, "lng"], [pre_key])
            tt("dve", outf[:], pre[:], b_t[:], ALU.add, [pre_key, "lnb"], [out_key])

        if "A" in phases:
            with contextlib.ExitStack() as sa:
                Wkv = SB(sa, "Wkv", [128, 8, 2624], BF16)
                Wro = SB(sa, "Wro", [128, 8, 1536], BF16)
                stg = [SB(sa, "stgA%d" % i, [128, 1024], F32) for i in range(2)]
                xgf2 = [SB(sa, "xgf%d" % i, [128, 8, 512], F32) for i in range(2)]
                xgb = SB(sa, "xgb", [128, 8, 512], BF16)
                posf = SB(sa, "posf", [128, 64], F32)
                posof = SB(sa, "posof", [128, 16], F32)
                invf = SB(sa, "invf", [128, 32], F32)
                kT_sb = SB(sa, "kT_sb", [128, 4, 512], BF16)
                v_sb = SB(sa, "v_sb", [128, 4, 8, 65], BF16)
                ikT_sb = SB(sa, "ikT_sb", [64, 512], BF16)
                ik_f4 = SB(sa, "ik_f4", [128, 256], F32)
                ik_q4 = SB(sa, "ik_q4", [128, 256], F32)
                ik_n4 = SB(sa, "ik_n4", [128, 256], BF16)
                stat4 = SB(sa, "stat4", [128, 8, 4], F32)
                stat = SB(sa, "statA", [128, 8], F32)
                lng = SB(sa, "lnkg", [64, 1], F32)
                lnb = SB(sa, "lnkb", [64, 1], F32)
                kz2 = [SB(sa, "kz%d" % i, [128, 512], BF16) for i in range(2)]
                vr2 = [SB(sa, "vr%d" % i, [128, 1024], BF16) for i in range(2)]
                rA = SB(sa, "rA", [128, 512], F32)
                rB = SB(sa, "rB", [128, 512], F32)
                rO = SB(sa, "rO", [128, 512], F32)
                zeta8 = SB(sa, "zeta8", [128, 8], F32)
                R = SB(sa, "R", [64, 1024], F32)
                Rsel = SB(sa, "Rsel", [64, 1024], F32)
                Rselb = SB(sa, "Rselb", [64, 1024], BF16)
                gmat = SB(sa, "gmat", [64, 1024], F32)
                oh = SB(sa, "oh", [128, 4], F32)
                xiT = SB(sa, "xiT", [64, 1024], F32)
                decayT = SB(sa, "decayT", [128, 1024], F32)
                xof = SB(sa, "xof", [128, 8, 128], F32)
                xob = SB(sa, "xob", [128, 8, 128], BF16)
                qrb = SB(sa, "qrb", [128, 512], BF16)
                krb = SB(sa, "krb", [128, 512], BF16)
                qT = SB(sa, "qT", [64, 8, 128], BF16)
                qxiT = SB(sa, "qxiT", [64, 8, 128], BF16)
                kTo = SB(sa, "kTo", [64, 8, 128], BF16)
                vro = SB(sa, "vro", [128, 1024], BF16)
                sgr = SB(sa, "sgr", [128, 1024], F32)
                Dm = SB(sa, "Dm", [128, 1024], BF16)
                osb = SB(sa, "osb", [128, 1024], F32)
                osq = SB(sa, "osq", [128, 1024], F32)
                hst = SB(sa, "hst", [128, 8, 8], F32)
                yrb = SB(sa, "yrb", [128, 1024], BF16)
                yrT = SB(sa, "yrT", [128, 8, 128], BF16)

                load_w(sa, Wkv, "Wkv", w_kv, 8, 2624, stg)
                load_w(sa, Wro, "Wro", w_ro, 8, 1536, stg)
                dma(posf[:], posT[:, :], [], ["posf"])
                dma(posof[:], posoT[:, :], [], ["posof"])
                dma(invf[:], c_invf[:, :], [], ["invf"])
                dma(lng[:], lnk_g[:, :], [], ["lnkg"])
                dma(lnb[:], lnk_b[:, :], [], ["lnkb"])
                dma(zeta8[:], c_zeta8[:, :], [], ["zeta8"])
                dma(gmat[:], c_gmat[:, :], [], ["gmat"])
                dma(oh[:], c_oh[:, :], [], ["oh"])
                dma(xiT[:], c_xiT[:, :], [], ["xiT"])
                dma(decayT[:], c_decayT[:, :], [], ["decayT"])
                memset("pool", R[:], 0.0, ["R"])
                memset("pool", v_sb[:], 1.0, ["v_sb"])

                posall = SB(sa, "posall", [128, 16, 5], F32)
                ang5 = SB(sa, "ang5", [128, 5, 2, 32], F32)
                angk5 = SB(sa, "angk5", [128, 5, 2, 32], F32)
                CC5 = SB(sa, "CC5", [128, 5, 64], F32)
                SS5 = SB(sa, "SS5", [128, 5, 64], F32)
                cp("dve", posall[:, :, 0:4], posf[:].rearrange("p (m i) -> p m i", i=4), ["posf"], ["posall"])
                cp("dve", posall[:, :, 4:5], posof[:].unsqueeze(2), ["posof"], ["posall"])

                def cos_sin_group(m):
                    MG = 12582912.0
                    a5 = ang5[:].rearrange("p t a i -> p (t a i)")
                    k5 = angk5[:].rearrange("p t a i -> p (t a i)")
                    tt("dve", ang5[:, :, 0, :], invf[:].unsqueeze(1).to_broadcast([128, 5, 32]),
                       posall[:, m, :].unsqueeze(2).to_broadcast([128, 5, 32]), ALU.mult, ["invf", "posall"], ["ang"])
                    ts("dve", ang5[:, :, 1, :], ang5[:, :, 0, :], 0.5 * PI, None, ALU.add, None, ["ang"], ["ang"])
                    ts("dve", k5, a5, 1.0 / (2 * PI), MG, ALU.mult, ALU.add, ["ang"], ["angk"])
                    ts("dve", k5, k5, -MG, None, ALU.add, None, ["angk"], ["angk"])
                    stt("dve", a5, k5, -2 * PI, a5, ALU.mult, ALU.add, ["angk", "ang"], ["ang"])
                    ts("dve", a5, a5, 3.1415925, -3.1415925, ALU.min, ALU.max, ["ang"], ["ang"])
                    act(SS5[:, :, 0:32], ang5[:, :, 0, :], AF.Sin, ["ang"], ["SS"])
                    act(SS5[:, :, 32:64], ang5[:, :, 0, :], AF.Sin, ["ang"], ["SS"], scale=-1.0)
                    act(CC5[:, :, 0:32], ang5[:, :, 1, :], AF.Sin, ["ang"], ["CC"])
                    act(CC5[:, :, 32:64], ang5[:, :, 1, :], AF.Sin, ["ang"], ["CC"])

                qf = SB(sa, "qf", [64, 1024], F32)

                def rope(src_ps, src_key, dst, ti):
                    s3 = src_ps.rearrange("p (h d) -> p h d", h=8)
                    a3 = rA[:].rearrange("p (h d) -> p h d", h=8)
                    b3 = rB[:].rearrange("p (h d) -> p h d", h=8)
                    o3 = dst[:].rearrange("p (h d) -> p h d", h=8)
                    ccb = CC5[:, ti, :].unsqueeze(1).to_broadcast([128, 8, 64])
                    ssb = SS5[:, ti, :].unsqueeze(1).to_broadcast([128, 8, 64])
                    tt("dve", a3, s3, ccb, ALU.mult, [src_key, "CC"], ["rA"])
                    tt("dve", b3, s3, ssb, ALU.mult, [src_key, "SS"], ["rB"])
                    tt("pool", o3[:, :, 0:32], a3[:, :, 0:32], b3[:, :, 32:64], ALU.add, ["rA", "rB"], ["rO"])
                    tt("pool", o3[:, :, 32:64], a3[:, :, 32:64], b3[:, :, 0:32], ALU.add, ["rA", "rB"], ["rO"])

                xT3 = xT.rearrange("(c p) t -> p c t", p=128)
                xoT3 = xoT.rearrange("(c p) t -> p c t", p=128)
                KT3 = KT.rearrange("(c p) t -> p c t", p=128)
                YRT3 = YRT.rearrange("(c p) t -> p c t", p=128)
                VV3 = VV.rearrange("t p f -> p t f")

                for m in range(ng):
                    if not alltok:
                        break
                    xgf = xgf2[m % 2]
                    xgk = "xgf%d" % (m % 2)
                    if m == 0:
                        dma(xgf[:], xT3[:, :, 0:512], [], [xgk])
                    cp("dve", xgb[:, 0:4, :], xgf[:, 0:4, :], [xgk], ["xgb"])
                    cp("act", xgb[:, 4:8, :], xgf[:, 4:8, :], [xgk], ["xgb"])
                    if m + 1 < ng:
                        dma(xgf2[(m + 1) % 2][:], xT3[:, :, (m + 1) * 512:(m + 2) * 512], [], ["xgf%d" % ((m + 1) % 2)])
                    pend = [None]

                    def flushU():
                        if pend[0] is None:
                            return
                        ps = pend[0]
                        pend[0] = None
                        kzp, vrp = kz2[ps], vr2[ps]
                        for h in range(8):
                            mm(pb[5 + h // 4][0:64, (h % 4) * 128:(h % 4 + 1) * 128], kzp[:, h * 64:(h + 1) * 64],
                               vrp[:, h * 128:(h + 1) * 128], True, True, ["kz%d" % ps, "vr%d" % ps], [pk[5 + h // 4]])
                        tt("dve", R[:], R[:], gmat[:], ALU.mult, ["R", "gmat"], ["R"])
                        tt("dve", R[:, 0:512], R[:, 0:512], pb[5][0:64, :], ALU.add, ["R", pk[5]], ["R"])
                        tt("dve", R[:, 512:1024], R[:, 512:1024], pb[6][0:64, :], ALU.add, ["R", pk[6]], ["R"])
                    for fc in range(4):
                        for dc in range(8):
                            mm(pb[0][:], Wkv[:, dc, fc * 128:(fc + 1) * 128], xgb[:, dc, :], dc == 0, dc == 7,
                               ["Wkv", "xgb"], [pk[0]])
                        cp("act", kT_sb[:, fc, :], pb[0][:], [pk[0]], ["kT_sb"])
                    dma(KT3[:, :, m * 512:(m + 1) * 512], kT_sb[:], ["kT_sb"], ["KT"])
                    cos_sin_group(m)
                    for i in range(4):
                        for dc in range(8):
                            mm(pb[0][:, i * 64:(i + 1) * 64], xgb[:, dc, i * 128:(i + 1) * 128], Wkv[:, dc, 1024:1088], dc == 0, dc == 7,
                               ["Wkv", "xgb"], [pk[0]])
                    ikf3 = ik_f4[:].rearrange("p (t d) -> p t d", t=4)
                    ikq3 = ik_q4[:].rearrange("p (t d) -> p t d", t=4)
                    cp("dve", ik_f4[:], pb[0][:, 0:256], [pk[0]], ["ik_f"])
                    red("dve", stat4[:, 0, :], ikf3, ALU.add, ["ik_f"], ["st0"])
                    act(ik_q4[:], ik_f4[:], AF.Square, ["ik_f"], ["ik_q"])
                    red("dve", stat4[:, 1, :], ikq3, ALU.add, ["ik_q"], ["st1"])
                    ts("dve", stat4[:, 2, :], stat4[:, 0, :], 1.0 / 64, None, ALU.mult, None, ["st0"], ["st2"])
                    ts("dve", stat4[:, 3, :], stat4[:, 1, :], 1.0 / 64, None, ALU.mult, None, ["st1"], ["st3"])
                    tt("dve", stat4[:, 4, :], stat4[:, 2, :], stat4[:, 2, :], ALU.mult, ["st2"], ["st4"])
                    tt("dve", stat4[:, 5, :], stat4[:, 3, :], stat4[:, 4, :], ALU.subtract, ["st3", "st4"], ["st5"])
                    rstd(stat4[:, 6, :], stat4[:, 5, :], ["st5"], ["st6"])
                    tt("dve", ikq3, ikf3, stat4[:, 2, :].unsqueeze(2).to_broadcast([128, 4, 64]), ALU.subtract,
                       ["ik_f", "st2", "ik_q"], ["ik_q"])
                    tt("dve", ik_n4[:].rearrange("p (t d) -> p t d", t=4), ikq3,
                       stat4[:, 6, :].unsqueeze(2).to_broadcast([128, 4, 64]), ALU.mult, ["ik_q", "st6"], ["ik_n"])
                    for i in range(4):
                        tr(pT[0:64, i * 128:(i + 1) * 128], ik_n4[:, i * 64:(i + 1) * 64], ident[:], ["ik_n", "ident"], ["pT"])
                    act(ikT_sb[:], pT[0:64, 0:512], AF.Identity, ["pT", "lnkg", "lnkb"], ["ikT_sb"],
                        bias=lnb[:, 0:1], scale=lng[:, 0:1])
                    for i in range(4):
                        t = 4 * m + i
                        xs = slice(i * 128, (i + 1) * 128)
                        for dc in range(8):
                            mm(pb[1][:], xgb[:, dc, xs], Wkv[:, dc, 512:1024], dc == 0, dc == 7, ["Wkv", "xgb"], [pk[1]])
                        cp("act", v_sb[:, i, :, 0:64], pb[1][:].rearrange("p (h d) -> p h d", h=8), [pk[1]], ["v_sb"])
                        for dc in range(8):
                            mm(pb[2][:], xgb[:, dc, xs], Wkv[:, dc, 1088:1600], dc == 0, dc == 7, ["Wkv", "xgb"], [pk[2]])
                        for hf in range(2):
                            for dc in range(8):
                                mm(pb[3 + hf][:], xgb[:, dc, xs], Wkv[:, dc, 1600 + hf * 512:2112 + hf * 512],
                                   dc == 0, dc == 7, ["Wkv", "xgb"], [pk[3 + hf]])
                            cp("act", vr2[t % 2][:, hf * 512:(hf + 1) * 512], pb[3 + hf][:], [pk[3 + hf]], ["vr%d" % (t % 2)])
                        flushU()
                        rope(pb[2][:], pk[2], rO, i)
                        tt("dve", kz2[t % 2][:].rearrange("p (h d) -> p h d", h=8), rO[:].rearrange("p (h d) -> p h d", h=8),
                           zeta8[:].unsqueeze(2).to_broadcast([128, 8, 64]), ALU.mult, ["rO", "zeta8"], ["kz%d" % (t % 2)])
                        if i == 0:
                            ts("dve", Rsel[:], R[:], oh[0:64, 0:1], None, ALU.mult, None, ["R", "oh"], ["Rsel"])
                        else:
                            stt("dve", Rsel[:], R[:], oh[0:64, i:i + 1], Rsel[:], ALU.mult, ALU.add, ["R", "oh", "Rsel"], ["Rsel"])
                        pend[0] = t % 2
                    flushU()
                    dma(VV3[:, 4 * m:4 * m + 4, :], v_sb[:].rearrange("p t h d -> p t (h d)"), ["v_sb"], ["VV"])
                    dma(IKT[:, m * 512:(m + 1) * 512], ikT_sb[:], ["ikT_sb"], ["IKT"])
                    if not own:
                        continue

                    os_ = slice(m * 128, (m + 1) * 128)
                    dma(xof[:], xoT3[:, :, os_], [], ["xof"])
                    cp("act", xob[:], xof[:], ["xof"], ["xob"])
                    cp("act", Rselb[:], Rsel[:], ["Rsel"], ["Rselb"])
                    for dc in range(8):
                        mm(pb[1][:], xob[:, dc, :], Wro[:, dc, 0:512], dc == 0, dc == 7, ["Wro", "xob"], [pk[1]])
                    rope(pb[1][:], pk[1], rO, 4)
                    cp("act", qrb[:], rO[:], ["rO"], ["qrb"])
                    for dc in range(8):
                        mm(pb[2][:], xob[:, dc, :], Wkv[:, dc, 1088:1600], dc == 0, dc == 7, ["Wkv", "xob"], [pk[2]])
                    rope(pb[2][:], pk[2], rO, 4)
                    S.op("act", lambda e, o=krb[:], i=rO[:]: e.mul(out=o, in_=i, mul=0.125), r=["rO"], w=["krb"])
                    for hf in range(2):
                        for dc in range(8):
                            mm(pb[3 + hf][:], xob[:, dc, :], Wkv[:, dc, 1600 + hf * 512:2112 + hf * 512],
                               dc == 0, dc == 7, ["Wkv", "xob"], [pk[3 + hf]])
                        cp("act", vro[:, hf * 512:(hf + 1) * 512], pb[3 + hf][:], [pk[3 + hf]], ["vro"])
                    for hf in range(2):
                        for dc in range(8):
                            mm(pb[5 + hf][:], xob[:, dc, :], Wro[:, dc, 512 + hf * 512:1024 + hf * 512],
                               dc == 0, dc == 7, ["Wro", "xob"], [pk[5 + hf]])
                        act(sgr[:, hf * 512:(hf + 1) * 512], pb[5 + hf][:], AF.Sigmoid, [pk[5 + hf]], ["sgr"])
                        tt("dve", sgr[:, hf * 512:(hf + 1) * 512], sgr[:, hf * 512:(hf + 1) * 512], pb[5 + hf][:], ALU.mult,
                           ["sgr", pk[5 + hf]], ["sgr"])
                    if ostage < 1:
                        continue
                    for h in range(8):
                        tr(pT[0:64, h * 128:(h + 1) * 128], qrb[:, h * 64:(h + 1) * 64], ident[:], ["qrb", "ident"], ["pT"])
                    cp("act", qf[:], pT[0:64, :], ["pT"], ["qf"])
                    cp("pool", qT[:].rearrange("p h n -> p (h n)"), qf[:], ["qf"], ["qT"])
                    tt("dve", qxiT[:].rearrange("p h n -> p (h n)"), qf[:], xiT[:], ALU.mult, ["qf", "xiT"], ["qxiT"])
                    for h in range(8):
                        tr(pT[0:64, h * 128:(h + 1) * 128], krb[:, h * 64:(h + 1) * 64], ident[:], ["krb", "ident"], ["pT"])
                    cp("act", kTo[:].rearrange("p h n -> p (h n)"), pT[0:64, :], ["pT"], ["kTo"])
                    if ostage < 2:
                        continue
                    for h in range(8):
                        mm(pb[3 + h // 4][:, (h % 4) * 128:(h % 4 + 1) * 128], kTo[:, h, :], qT[:, h, :], True, True,
                           ["kTo", "qT"], [pk[3 + h // 4]])
                    tt("dve", Dm[:, 0:512], pb[3][:], decayT[:, 0:512], ALU.mult, [pk[3], "decayT"], ["Dm"])
                    tt("dve", Dm[:, 512:1024], pb[4][:], decayT[:, 512:1024], ALU.mult, [pk[4], "decayT"], ["Dm"])
                    for h in range(8):
                        o_ap = pb[5 + h // 4][:, (h % 4) * 128:(h % 4 + 1) * 128]
                        mm(o_ap, Dm[:, h * 128:(h + 1) * 128], vro[:, h * 128:(h + 1) * 128], True, False,
                           ["Dm", "vro"], [pk[5 + h // 4]])
                        mm(o_ap, qxiT[:, h, :], Rselb[:, h * 128:(h + 1) * 128], False, True,
                           ["qxiT", "Rselb"], [pk[5 + h // 4]])
                    if ostage < 3:
                        continue
                    cp("act", osb[:, 0:512], pb[5][:], [pk[5]], ["osb"])
                    cp("act", osb[:, 512:1024], pb[6][:], [pk[6]], ["osb"])
                    o3 = osb[:].rearrange("p (h v) -> p h v", h=8)
                    q3 = osq[:].rearrange("p (h v) -> p h v", h=8)
                    red("dve", hst[:, 0, :], o3, ALU.add, ["osb"], ["h0"])
                    tt("pool", osq[:], osb[:], osb[:], ALU.mult, ["osb"], ["osq"])
                    red("dve", hst[:, 1, :], q3, ALU.add, ["osq"], ["h1"])
                    ts("dve", hst[:, 2, :], hst[:, 0, :], 1.0 / 128, None, ALU.mult, None, ["h0"], ["h2"])
                    ts("dve", hst[:, 3, :], hst[:, 1, :], 1.0 / 128, None, ALU.mult, None, ["h1"], ["h3"])
                    tt("dve", hst[:, 4, :], hst[:, 2, :], hst[:, 2, :], ALU.mult, ["h2"], ["h4"])
                    tt("dve", hst[:, 5, :], hst[:, 3, :], hst[:, 4, :], ALU.subtract, ["h3", "h4"], ["h5"])
                    rstd(hst[:, 6, :], hst[:, 5, :], ["h5"], ["h6"])
                    tt("dve", q3, o3, hst[:, 2, :].unsqueeze(2).to_broadcast([128, 8, 128]), ALU.subtract, ["osb", "h2", "osq"], ["osq"])
                    tt("dve", q3, q3, hst[:, 6, :].unsqueeze(2).to_broadcast([128, 8, 128]), ALU.mult, ["osq", "h6"], ["osq"])
                    if ostage < 4:
                        continue
                    tt("pool", yrb[:], osq[:], sgr[:], ALU.mult, ["osq", "sgr"], ["yrb"])
                    for fc in range(8):
                        tr(pT[:, fc * 128:(fc + 1) * 128], yrb[:, fc * 128:(fc + 1) * 128], ident[:], ["yrb", "ident"], ["pT"])
                    cp("act", yrT[:].rearrange("p c n -> p (c n)"), pT[:, :], ["pT"], ["yrT"])
                    dma(YRT3[:, :, os_], yrT[:], ["yrT"], ["YRT"])
            S.barrier()

        if "B" in phases:
            with contextlib.ExitStack() as sbk:
                KTs = SB(sbk, "KTs", [128, 4, 8192], BF16)
                Vc = [SB(sbk, "Vc%d" % i, [128, 4, 520], BF16) for i in range(2)]
                IKs = SB(sbk, "IKs", [64, 8192], BF16)
                score = SB(sbk, "score", [128, 8192], F32)
                Wq = SB(sbk, "Wq", [128, 8, 772], BF16)
                stg = [SB(sbk, "stgB%d" % i, [128, 1024], F32) for i in range(2)]
                xof = SB(sbk, "xofB", [128, 8, 128], F32)
                xob = SB(sbk, "xobB", [128, 8, 128], BF16)
                qaT = SB(sbk, "qaT", [128, 4, 2, 128], BF16)
                iqT = SB(sbk, "iqT", [64, 4, 128], BF16)
                iwf = SB(sbk, "iwf", [128, 4], F32)
                aw = SB(sbk, "aw", [128, 4], F32)
                sgn = SB(sbk, "sgn", [128, 4], F32)
                rl = [SB(sbk, "rl%d" % i, [128, 512], F32) for i in range(4)]
                tb = SB(sbk, "tb", [128, 512], F32)
                cm = SB(sbk, "cm", [128, 512], F32)
                btf = SB(sbk, "btf", [128, 1024], F32)
                bts = SB(sbk, "bts", [128, 5, 1024], BF16)
                c31 = SB(sbk, "c31", [128, 8], F32)
                bs = SB(sbk, "bs", [128, 64], F32)
                wk = SB(sbk, "wk", [128, NBIS], F32)
                pw2 = SB(sbk, "pw2", [128, NBIS], F32)
                for it in range(NBIS):
                    memset("dve", pw2[:, it:it + 1], 0.5 ** (it + 1), ["pw2"])
                junk = SB(sbk, "junkB", [128, 3712], BF16)
                junkA = SB(sbk, "junkA", [128, 4608], BF16)
                mk = [SB(sbk, "mk%d" % i, [128, 128], BF16) for i in range(2)]
                mkT = [SB(sbk, "mkT%d" % i, [128, 128], BF16) for i in range(2)]
                Eb = [SB(sbk, "Eb%d" % i, [128, 1024], BF16) for i in range(2)]
                Pb = [SB(sbk, "Pb%d" % i, [128, 1024], BF16) for i in range(2)]
                osb = SB(sbk, "osbB", [128, 520], F32)
                rs = SB(sbk, "rsB", [128, 8], F32)
                yab = SB(sbk, "yab", [128, 512], BF16)
                yaT = SB(sbk, "yaT", [128, 4, 128], BF16)

                KT3 = KT.rearrange("(c p) t -> p c t", p=128)
                for c4 in range(4):
                    dma(KTs[:, c4, :], KT3[:, c4, :], ["KT"], ["KTs"])
                VV3 = VV.rearrange("t p f -> p t f")
                dma(IKs[:], IKT[:, :], ["IKT"], ["IKs"])
                load_w(sbk, Wq, "Wq", w_q, 8, 772, stg)
                memset("pool", qaT[:], 0.0, ["qaT"])
                zr = SB(sbk, "zr", [128, 512], BF16)
                memset("pool", zr[:], 0.0, ["zr"])
                dma(tb[:], c_tb[:, :], [], ["tb"])
                dma(cm[:], c_cm[:, :], [], ["cm"])
                dma(c31[:], c_c31[:, :], [], ["c31"])
                for kr in range(5):
                    dma(btf[:], c_bt[:, kr * 1024:(kr + 1) * 1024], [], ["btf"])
                    tt("dve", bts[:, kr, :].rearrange("p (h q) -> p h q", h=8), btf[:].rearrange("p (h q) -> p h q", h=8),
                       c31[:].unsqueeze(2).to_broadcast([128, 8, 128]), ALU.subtract, ["btf", "c31"], ["bts"])
                xoT3 = xoT.rearrange("(c p) t -> p c t", p=128)
                YAT3 = YAT.rearrange("(c p) t -> p c t", p=128)
                vcount = [0]
                LO, W0, MID, WK, VV_, PW = 0, 1, 2, 3, 4, 5

                for m in range(mstart, nb):
                    os_ = slice(m * 128, (m + 1) * 128)
                    n5 = m + 1
                    nk = min(4 * (m + 1), nkcap)
                    n = 512 * (m + 1)
                    dma(xof[:], xoT3[:, :, os_], [], ["xof"])
                    cp("act", xob[:], xof[:], ["xof"], ["xob"])
                    for fc in range(4):
                        for dc in range(8):
                            mm(pb[0][:, fc * 128:(fc + 1) * 128], Wq[:, dc, fc * 128:(fc + 1) * 128], xob[:, dc, :],
                               dc == 0, dc == 7, ["Wq", "xob"], [pk[0]])
                    p03 = pb[0][:].rearrange("p (c n) -> p c n", c=4)
                    S.op("act", lambda e, o=qaT[0:64, :, 0, :], i=p03[0:64, :, :]: e.mul(out=o, in_=i, mul=0.125),
                         r=[pk[0]], w=["qaT"])
                    S.op("act", lambda e, o=qaT[64:128, :, 1, :], i=p03[64:128, :, :]: e.mul(out=o, in_=i, mul=0.125),
                         r=[pk[0]], w=["qaT"])
                    for h in range(4):
                        for dc in range(8):
                            mm(pb[2][0:64, h * 128:(h + 1) * 128], Wq[:, dc, 512 + h * 64:512 + (h + 1) * 64], xob[:, dc, :],
                               dc == 0, dc == 7, ["Wq", "xob"], [pk[2]])
                    cp("act", iqT[:].rearrange("p h n -> p (h n)"), pb[2][0:64, :], [pk[2]], ["iqT"])
                    for dc in range(8):
                        mm(pb[3][:, 0:4], xob[:, dc, :], Wq[:, dc, 768:772], dc == 0, dc == 7, ["Wq", "xob"], [pk[3]])
                    cp("dve", iwf[:], pb[3][:, 0:4], [pk[3]], ["iwf"])
                    act(aw[:], iwf[:], AF.Abs, ["iwf"], ["aw"], scale=1.0 / 16)
                    act(sgn[:], iwf[:], AF.Sign, ["iwf"], ["sgn"])
                    if bstage < 1:
                        continue
                    for c5 in range(n5):
                        ks = slice(c5 * 512, (c5 + 1) * 512)
                        for h in range(4):
                            mm(pb[h][:], iqT[:, h, :], IKs[:, ks], True, True, ["iqT", "IKs"], [pk[h]])
                        sck = "score%d" % c5
                        ts("dve", score[:, ks], tb[:], -1e-30 * 512 * c5, None, ALU.add, None, ["tb"], [sck])
                        if c5 == n5 - 1:
                            tt("dve", score[:, ks], score[:, ks], cm[:], ALU.add, [sck, "cm"], [sck])
                        for h in range(4):
                            rk = "rl%d" % h
                            act(rl[h][:], pb[h][:], AF.Relu, [pk[h], "aw"], [rk], scale=aw[:, h:h + 1])
                            stt("dve", score[:, ks], rl[h][:], sgn[:, h:h + 1], score[:, ks], ALU.mult, ALU.add,
                                [rk, "sgn", sck], [sck])
                    sckeys = ["score%d" % c5 for c5 in range(n5)]
                    if bstage < 2:
                        continue
                    nd = max(128, ((45 * n // 100) // 128) * 128)
                    na = n - nd
                    red("dve", bs[:, W0:W0 + 1], score[:, 0:n], ALU.max, sckeys, ["bs"])
                    ts("dve", bs[:, W0:W0 + 1], bs[:, W0:W0 + 1], 1e-3 - LO_INIT, None, ALU.add, None, ["bs"], ["bs"])
                    memset("dve", bs[:, 8:64], 0.0, ["bs", "bsa"])
                    ts("dve", wk[:], pw2[:], bs[:, W0:W0 + 1], None, ALU.mult, None, ["bs", "pw2"], ["wk"])
                    ts("dve", bs[:, MID:MID + 1], wk[:, 0:1], LO_INIT, None, ALU.add, None, ["wk"], ["bs", "mid"])
                    for it in range(NBIS):
                        ts("dve", junk[:, 0:nd], score[:, 0:nd], bs[:, MID:MID + 1], 0.0, ALU.is_ge, ALU.add,
                           sckeys + ["mid"], ["junk", "bs"], accum=bs[:, 8 + it:9 + it])
                        act(junkA[:, 0:na], score[:, nd:n], AF.Sign, sckeys + ["mid"], ["junkA", "bsa"],
                            bias=bs[:, MID:MID + 1], scale=-1.0, accum=bs[:, 36 + it:37 + it])
                        stt("dve", bs[:, VV_:VV_ + 1], bs[:, 8 + it:9 + it], 2.0, bs[:, 36 + it:37 + it], ALU.mult, ALU.subtract,
                            ["bs", "bsa"], ["bs"])
                        ts("dve", bs[:, PW:PW + 1], bs[:, VV_:VV_ + 1], 511.5 - na, wk[:, it:it + 1], ALU.is_ge, ALU.mult,
                           ["bs", "wk"], ["bs"])
                        if it + 1 < NBIS:
                            stt("dve", bs[:, MID:MID + 1], bs[:, PW:PW + 1], bs[:, MID:MID + 1], wk[:, it + 1:it + 2],
                                ALU.add, ALU.subtract, ["bs", "wk", "mid"], ["bs", "mid"])
                        else:
                            stt("dve", bs[:, LO:LO + 1], bs[:, PW:PW + 1], bs[:, MID:MID + 1], wk[:, it:it + 1],
                                ALU.add, ALU.subtract, ["bs", "wk", "mid"], ["bs"])
                    if bstage < 3:
                        continue
                    for hh in range(2):
                        mm(pb[4 + hh][:], zr[:, 0:128], zr[:], True, True, ["zr"], [pk[4 + hh]])
                    vslot = {}

                    def stage1(kc):
                        sl = kc % 2
                        kcs = slice(kc * 128, (kc + 1) * 128)
                        pS = (pb[2 * sl], pb[2 * sl + 1])
                        pSk = (pk[2 * sl], pk[2 * sl + 1])
                        kr = kc - (4 * m - 1)
                        near = 0 <= kr <= 4 and kc >= 0 and not nobias
                        if kc % 4 == 0:
                            vs_ = vcount[0] % 2
                            vcount[0] += 1
                            dma(Vc[vs_][:], VV3[:, kc:kc + 4, :], ["VV"], ["Vc%d" % vs_])
                            for k2 in range(kc, kc + 4):
                                vslot[k2] = vs_
                        ts("dve", mk[sl][:], score[:, kcs], bs[:, LO:LO + 1], None, ALU.is_ge, None, ["score%d" % (kc // 4), "bs"], ["mk%d" % sl])
                        tr(pT[:, sl * 128:(sl + 1) * 128], mk[sl][:], ident[:], ["mk%d" % sl, "ident"], ["pT%d" % sl])
                        for c4 in range(4):
                            reg = pS[c4 // 2][:, (c4 % 2) * 256:(c4 % 2 + 1) * 256]
                            mm(reg, KTs[:, c4, kcs], qaT[:, c4, :, :].rearrange("p a q -> p (a q)"), True, not near,
                               ["KTs", "qaT"], [pSk[c4 // 2]])
                            if near:
                                mm(reg, ident[:], bts[:, kr, c4 * 256:(c4 + 1) * 256], False, True,
                                   ["ident", "bts"], [pSk[c4 // 2]])
                        cp("act", mkT[sl][:], pT[:, sl * 128:(sl + 1) * 128], ["pT%d" % sl], ["mkT%d" % sl])
                        for hh in range(2):
                            act(Eb[sl][:, hh * 512:(hh + 1) * 512], pS[hh][:], AF.Exp, [pSk[hh]], ["Eb%d_%d" % (sl, hh)])
                        for hh in range(2):
                            tt("dve", Pb[sl][:, hh * 512:(hh + 1) * 512].rearrange("p (h q) -> p h q", h=4),
                               Eb[sl][:, hh * 512:(hh + 1) * 512].rearrange("p (h q) -> p h q", h=4),
                               mkT[sl][:].unsqueeze(1).to_broadcast([128, 4, 128]), ALU.mult,
                               ["Eb%d_%d" % (sl, hh), "mkT%d" % sl], ["Pb%d_%d" % (sl, hh)])

                    def stage2(kc):
                        sl = kc % 2
                        vs_ = vslot[kc]
                        Vv = Vc[vs_][:].rearrange("p t (h d) -> p t h d", h=8)
                        for h in range(8):
                            mm(pb[4 + h // 4][:, (h % 4) * 128:(h % 4) * 128 + 65], Pb[sl][:, h * 128:(h + 1) * 128],
                               Vv[:, kc % 4, h, :], False, kc == nk - 1, ["Pb%d_%d" % (sl, h // 4), "Vc%d" % vs_], [pk[4 + h // 4]])

                    stage1(0)
                    for kc in range(1, nk):
                        stage1(kc)
                        stage2(kc - 1)
                    stage2(nk - 1)
                    if bstage < 4:
                        continue
                    cp("act", osb[:, 0:260].rearrange("p (h d) -> p h d", h=4), pb[4][:].rearrange("p (h d) -> p h d", h=4)[:, :, 0:65],
                       [pk[4]], ["osbB"])
                    cp("act", osb[:, 260:520].rearrange("p (h d) -> p h d", h=4), pb[5][:].rearrange("p (h d) -> p h d", h=4)[:, :, 0:65],
                       [pk[5]], ["osbB"])
                    o4 = osb[:].rearrange("p (h d) -> p h d", h=8)
                    S.op("dve", lambda e, o=rs[:], i=o4[:, :, 64]: e.reciprocal(out=o, in_=i), r=["osbB"], w=["rsB"])
                    tt("dve", yab[:].rearrange("p (h d) -> p h d", h=8), o4[:, :, 0:64],
                       rs[:].unsqueeze(2).to_broadcast([128, 8, 64]), ALU.mult, ["osbB", "rsB"], ["yab"])
                    for fc in range(4):
                        tr(pT[:, 256 + fc * 128:256 + (fc + 1) * 128], yab[:, fc * 128:(fc + 1) * 128], ident[:],
                           ["yab", "ident"], ["pTy"])
                    cp("act", yaT[:].rearrange("p c n -> p (c n)"), pT[:, 256:768], ["pTy"], ["yaT"])
                    dma(YAT3[:, :, os_], yaT[:], ["yaT"], ["YAT"])
            S.barrier()

        if "C" in phases:
            with contextlib.ExitStack() as sc:
                Wg = SB(sc, "Wg", [128, 8, 2048], BF16)
                Wa = SB(sc, "Wa", [128, 4, 1024], BF16)
                Wr = SB(sc, "Wr", [128, 8, 1024], BF16)
                Wo = SB(sc, "Wo", [128, 8, 1024], BF16)
                stg = [SB(sc, "stgC%d" % i, [128, 2048], F32) for i in range(4)]
                g1 = SB(sc, "g1", [128, 1024], F32)
                b1 = SB(sc, "b1", [128, 1024], F32)
                xof = SB(sc, "xofC", [128, 8, 128], F32)
                xob = SB(sc, "xobC", [128, 8, 128], BF16)
                yaT = SB(sc, "yaTC", [128, 4, 128], BF16)
                yrT = SB(sc, "yrTC", [128, 8, 128], BF16)
                sg = SB(sc, "sg", [128, 2048], F32)
                hf_ = SB(sc, "hf", [128, 1024], F32)
                h2 = SB(sc, "h2", [128, 1024], F32)
                hb = SB(sc, "hb", [128, 1024], BF16)
                hT = SB(sc, "hT", [128, 8, 128], BF16)
                xres = SB(sc, "xres", [128, 1024], F32)
                pre = SB(sc, "pre", [128, 1024], F32)
                tmp = SB(sc, "tmpC", [128, 1024], F32)
                x1f = SB(sc, "x1f", [128, 1024], F32)
                x1b = SB(sc, "x1b", [128, 1024], BF16)
                x1T = SB(sc, "x1T", [128, 8, 128], BF16)
                stat = SB(sc, "statC", [128, 8], F32)
                load_w(sc, Wg, "Wg", w_g, 8, 2048, stg)
                load_w(sc, Wa, "Wa", w_a, 4, 1024, stg)
                load_w(sc, Wr, "Wr", w_r, 8, 1024, stg)
                load_w(sc, Wo, "Wo", w_o, 8, 1024, stg)
                dma(g1[:], ln1_g[:, :], [], ["lng"])
                dma(b1[:], ln1_b[:, :], [], ["lnb"])
                xoT3 = xoT.rearrange("(c p) t -> p c t", p=128)
                YAT3 = YAT.rearrange("(c p) t -> p c t", p=128)
                YRT3 = YRT.rearrange("(c p) t -> p c t", p=128)
                X1T3 = X1T.rearrange("(c p) t -> p c t", p=128)
                for m in range(16):
                    os_ = slice(m * 128, (m + 1) * 128)
                    dma(xof[:], xoT3[:, :, os_], [], ["xof"])
                    cp("act", xob[:], xof[:], ["xof"], ["xob"])
                    dma(yaT[:], YAT3[:, :, os_], ["YAT"], ["yaT"])
                    dma(yrT[:], YRT3[:, :, os_], ["YRT"], ["yrT"])
                    dma(xres[:], xo[os_, :], [], ["xres"])
                    for q4 in range(4):
                        for dc in range(8):
                            mm(pb[q4][:], xob[:, dc, :], Wg[:, dc, q4 * 512:(q4 + 1) * 512], dc == 0, dc == 7,
                               ["Wg", "xob"], [pk[q4]])
                        act(sg[:, q4 * 512:(q4 + 1) * 512], pb[q4][:], AF.Sigmoid, [pk[q4]], ["sg"])
                    for hh in range(2):
                        for fc in range(4):
                            mm(pb[4 + hh][:], yaT[:, fc, :], Wa[:, fc, hh * 512:(hh + 1) * 512], fc == 0, fc == 3,
                               ["Wa", "yaT"], [pk[4 + hh]])
                        tt("dve", hf_[:, hh * 512:(hh + 1) * 512], pb[4 + hh][:], sg[:, hh * 512:(hh + 1) * 512], ALU.mult,
                           [pk[4 + hh], "sg"], ["hf"])
                    for hh in range(2):
                        for fc in range(8):
                            mm(pb[hh][:], yrT[:, fc, :], Wr[:, fc, hh * 512:(hh + 1) * 512], fc == 0, fc == 7,
                               ["Wr", "yrT"], [pk[hh]])
                        tt("dve", h2[:, hh * 512:(hh + 1) * 512], pb[hh][:], sg[:, 1024 + hh * 512:1024 + (hh + 1) * 512],
                           ALU.mult, [pk[hh], "sg"], ["h2"])
                    tt("dve", hb[:], hf_[:], h2[:], ALU.add, ["hf", "h2"], ["hb"])
                    for fc in range(8):
                        tr(pT[:, fc * 128:(fc + 1) * 128], hb[:, fc * 128:(fc + 1) * 128], ident[:], ["hb", "ident"], ["pT"])
                    cp("act", hT[:].rearrange("p c n -> p (c n)"), pT[:, :], ["pT"], ["hT"])
                    for hh in range(2):
                        for fc in range(8):
                            mm(pb[2 + hh][:], hT[:, fc, :], Wo[:, fc, hh * 512:(hh + 1) * 512], fc == 0, fc == 7,
                               ["Wo", "hT"], [pk[2 + hh]])
                        stt("dve", pre[:, hh * 512:(hh + 1) * 512], xres[:, hh * 512:(hh + 1) * 512], ALPHA, pb[2 + hh][:],
                            ALU.mult, ALU.add, ["xres", pk[2 + hh]], ["pre"])
                    layer_norm_rows({"stat": stat}, pre, "pre", x1f, "x1f", g1, b1, tmp, "lnC")
                    dma(X1[os_, :], x1f[:], ["x1f"], ["X1"])
                    cp("act", x1b[:], x1f[:], ["x1f"], ["x1b"])
                    for fc in range(8):
                        tr(pT[:, fc * 128:(fc + 1) * 128], x1b[:, fc * 128:(fc + 1) * 128], ident[:], ["x1b", "ident"], ["pT"])
                    cp("act", x1T[:].rearrange("p c n -> p (c n)"), pT[:, :], ["pT"], ["x1T"])
                    dma(X1T3[:, :, os_], x1T[:], ["x1T"], ["X1T"])
            S.barrier()

        if "D" in phases:
            with contextlib.ExitStack() as sd:
                Wdn = SB(sd, "Wdn", [128, 32, 1024], BF16)
                HT = SB(sd, "HT", [128, 32, 1024], BF16)
                x1T = SB(sd, "x1TD", [128, 8, 1024], BF16)
                stg = [SB(sd, "stgD%d" % i, [128, 1024], F32) for i in range(6)]
                wub = [SB(sd, "wub%d" % i, [128, 8, 128], BF16) for i in range(2)]
                rl = [SB(sd, "rlD%d" % i, [128, 512], F32) for i in range(2)]
                g2 = SB(sd, "g2", [128, 1024], F32)
                b2 = SB(sd, "b2", [128, 1024], F32)
                x1r = SB(sd, "x1r", [128, 1024], F32)
                pre = SB(sd, "preD", [128, 1024], F32)
                tmp = SB(sd, "tmpD", [128, 1024], F32)
                of_ = SB(sd, "of", [128, 1024], F32)
                stat = SB(sd, "statD", [128, 8], F32)
                load_w(sd, Wdn, "Wdn", w_dn, 32, 1024, stg)
                dma(g2[:], ln2_g[:, :], [], ["lng"])
                dma(b2[:], ln2_b[:, :], [], ["lnb"])
                X1T3 = X1T.rearrange("(c p) t -> p c t", p=128)
                w_up3 = w_up.rearrange("(c p) f -> p c f", p=128)
                for th in range(2):
                    dma(x1T[:], X1T3[:, :, th * 1024:(th + 1) * 1024], ["X1T"], ["x1TD"])
                    for f in range(32):
                        ws = f % 2
                        wk = "wub%d" % ws
                        i = rr["stg"]
                        rr["stg"] = i + 1
                        sk = "stg%d" % (i % 6)
                        stile = stg[i % 6]
                        dma(stile[:].rearrange("p (c f) -> p c f", c=8), w_up3[:, :, f * 128:(f + 1) * 128], [], [sk])
                        cp("pool" if f % 2 == 0 else "dve", wub[ws][:].rearrange("p c f -> p (c f)"), stile[:], [sk], [wk])
                        for s2 in range(2):
                            bank = 2 * (f % 2) + s2
                            for dc in range(8):
                                mm(pb[bank][:], wub[ws][:, dc, :], x1T[:, dc, s2 * 512:(s2 + 1) * 512], dc == 0, dc == 7,
                                   [wk, "x1TD"], [pk[bank]])
                            rk = "rlD%d" % s2
                            act(rl[s2][:], pb[bank][:], AF.Relu, [pk[bank]], [rk])
                            tt("dve" if s2 == 0 else "pool", HT[:, f, s2 * 512:(s2 + 1) * 512], rl[s2][:], rl[s2][:], ALU.mult,
                               [rk], ["HT%d" % s2])
                    for tl in range(8):
                        tg = th * 8 + tl
                        os_ = slice(tg * 128, (tg + 1) * 128)
                        dma(x1r[:], X1[os_, :], ["X1"], ["x1r"])
                        for hh in range(2):
                            for f in range(32):
                                mm(pb[4 + hh][:], HT[:, f, tl * 128:(tl + 1) * 128], Wdn[:, f, hh * 512:(hh + 1) * 512],
                                   f == 0, f == 31, ["HT0", "HT1", "Wdn"], [pk[4 + hh]])
                            stt("dve", pre[:, hh * 512:(hh + 1) * 512], x1r[:, hh * 512:(hh + 1) * 512], ALPHA, pb[4 + hh][:],
                                ALU.mult, ALU.add, ["x1r", pk[4 + hh]], ["preD"])
                        layer_norm_rows({"stat": stat}, pre, "preD", of_, "of", g2, b2, tmp, "lnD")
                        dma(out[os_, :], of_[:], ["of"], ["out"])
        S.emit()
    return nc


def _t5_bucket(n):
    n = np.maximum(n, 0)
    nf = np.maximum(n, 1).astype(np.float32)
    large = 16 + (np.log(nf / np.float32(16)) / np.float32(math.log(128 / 16)) * np.float32(16)).astype(np.int32)
    large = np.minimum(large, 31)
    return np.where(n < 16, n, large)


def _consts(j):
    c = {}
    c["c_ident"] = np.eye(128, dtype=np.float32)
    half = 32
    inv = (np.float32(10000.0) ** (-np.arange(half, dtype=np.float32) / np.float32(half))).astype(np.float32)
    c["c_invf"] = np.broadcast_to(inv[None, :], (128, 32)).copy()
    H = 8
    gamma = (1.0 - 2.0 ** (-5.0 - np.arange(H, dtype=np.float64)))
    lg = np.log(gamma)
    nn = np.arange(128, dtype=np.float64)
    diff = nn[None, :] - nn[:, None]
    dT = np.where(diff[:, None, :] >= 0, np.exp(lg[None, :, None] * np.maximum(diff, 0)[:, None, :]), 0.0)
    c["c_decayT"] = dT.reshape(128, 1024).astype(np.float32)
    zeta = np.exp(lg[None, :] * (127.0 - nn[:, None]))
    c["c_zeta8"] = (zeta / 8.0).astype(np.float32)
    xi = np.exp(lg[:, None] * (nn[None, :] + 1.0))
    c["c_xiT"] = np.broadcast_to(xi.reshape(1, 1024), (64, 1024)).astype(np.float32).copy()
    g = np.exp(lg * 128.0)
    c["c_gmat"] = np.broadcast_to(np.repeat(g, 128)[None, :], (64, 1024)).astype(np.float32).copy()
    c["c_tb"] = np.broadcast_to((-1e-30 * np.arange(512, dtype=np.float64))[None, :], (128, 512)).astype(np.float32).copy()
    q = np.arange(128)[:, None]
    kk = np.arange(512)[None, :]
    c["c_cm"] = np.where(kk <= 128 * j + q, 0.0, MASKV).astype(np.float32)
    oh = np.zeros((128, 4), np.float32)
    oh[:, j] = 1.0
    c["c_oh"] = oh
    return c


def _bias_tiles(rel_bias, j):
    key = np.arange(128)[:, None, None]
    kr = np.arange(5)[None, :, None]
    q = np.arange(128)[None, None, :]
    dist = (j + 1 - kr) * 128 + q - key
    bucket = np.where(dist >= 0, _t5_bucket(dist), 31)
    bt = rel_bias[bucket]
    bt = np.transpose(bt, (0, 1, 3, 2))
    return np.ascontiguousarray(bt.reshape(128, 5 * 1024)).astype(np.float32)


_NC_CACHE = {}


def make_in_maps(x, positions, w_in, rel_bias, idx_k_ln_g, idx_k_ln_b, w_attn_branch, w_ret_branch,
                 w_out, ln_mix_g, ln_mix_b, w_up, w_down, ln_ffn_g, ln_ffn_b):
    x = np.asarray(x, np.float32)
    positions = np.asarray(positions, np.int32)
    w = np.asarray(w_in, np.float32)[0]
    rel_bias = np.asarray(rel_bias, np.float32)
    cs = lambda a, b: w[:, a:b]
    w_kv = np.ascontiguousarray(np.concatenate([cs(512, 1024), cs(1024, 1536), cs(1792, 1856), cs(2372, 2884), cs(2884, 3908)], axis=1))
    w_q = np.ascontiguousarray(np.concatenate([cs(0, 512), cs(1536, 1792), cs(1856, 1860)], axis=1))
    w_ro = np.ascontiguousarray(np.concatenate([cs(1860, 2372), cs(3908, 4932)], axis=1))
    w_g = np.ascontiguousarray(cs(4932, 6980))
    rep = lambda v: np.ascontiguousarray(np.broadcast_to(np.asarray(v, np.float32).reshape(1, -1), (128, 1024)))
    shared = {
        "w_kv": w_kv, "w_q": w_q, "w_ro": w_ro, "w_g": w_g,
        "w_a": np.ascontiguousarray(np.asarray(w_attn_branch, np.float32)[0]),
        "w_r": np.ascontiguousarray(np.asarray(w_ret_branch, np.float32)[0]),
        "w_o": np.ascontiguousarray(np.asarray(w_out, np.float32)[0]),
        "w_up": np.ascontiguousarray(np.asarray(w_up, np.float32)[0]),
        "w_dn": np.ascontiguousarray(np.asarray(w_down, np.float32)[0]),
        "lnk_g": np.ascontiguousarray(np.asarray(idx_k_ln_g, np.float32).reshape(64, 1)),
        "lnk_b": np.ascontiguousarray(np.asarray(idx_k_ln_b, np.float32).reshape(64, 1)),
        "ln1_g": rep(ln_mix_g), "ln1_b": rep(ln_mix_b), "ln2_g": rep(ln_ffn_g), "ln2_b": rep(ln_ffn_b),
        "c_c31": np.ascontiguousarray(np.broadcast_to(rel_bias[31][None, :], (128, 8))),
    }
    xTs = [np.ascontiguousarray(x[b].T) for b in range(2)]
    in_maps = []
    own_idx = []
    for c in range(8):
        b, j = c // 4, c % 4
        tok = (np.arange(16)[:, None] * 4 + j) * 128 + np.arange(128)[None, :]
        tok = tok.reshape(-1)
        own_idx.append((b, tok))
        d = dict(shared)
        d["xT"] = xTs[b]
        d["xoT"] = np.ascontiguousarray(xTs[b][:, tok])
        d["xo"] = np.ascontiguousarray(x[b][tok])
        d["posT"] = np.ascontiguousarray(positions[b].reshape(64, 128).T).astype(np.float32)
        d["posoT"] = np.ascontiguousarray(positions[b][tok].reshape(16, 128).T).astype(np.float32)
        d.update(_consts(j))
        d["c_bt"] = _bias_tiles(rel_bias, j)
        in_maps.append(d)
    return in_maps, own_idx


def kernel(**inputs):
    in_maps, own_idx = make_in_maps(**inputs)
    if "nc" not in _NC_CACHE:
        _NC_CACHE["nc"] = build()
    nc = _NC_CACHE["nc"]
    res = run_bass_kernel_spmd(nc, in_maps, core_ids=list(range(8)))
    outp = np.zeros((2, 8192, 1024), np.float32)
    for c in range(8):
        b, tok = own_idx[c]
        outp[b, tok] = res.results[c]["out"]
    return outp
```
